# Optimizing a Trainium2 kernel written in Bass

```python
import jax, jax.numpy as jnp
from jax import lax
import numpy as np

D_MODEL = 1024
BATCH = 8
SEQ = 2048
DEPTH = 1
DEC_BATCH = 128
DEC_SEQ = 1
PAST_LEN = 16384
PAGE_SIZE = 128

N_META = 16
CHUNK = 64
HA = 4
DKA = 128
DVA = 128
CONV_A = 4
C_A = 2 * HA * DKA + HA * DVA
HB = 4
DKB = 128
DVB = 256
GATE_RANK = 16
GLA_GATE_NORM = 16.0
D_FF = 2816
CONV_F = 3
EPS = 1e-6

PROJ_SIZES = [HA * DKA, HA * DKA, HA * DVA, HA * DVA, HA, HA,
              HB * DKB, HB * DKB, HB * DVB, HB * DVB, GATE_RANK, D_MODEL, D_MODEL]
D_IN = int(sum(PROJ_SIZES))

kernel_name = "hybrid_gdn_gla_convffn_meta_step"


def rmsnorm(x, w):
    xf = x.astype(jnp.float32)
    y = xf * lax.rsqrt(jnp.mean(xf * xf, axis=-1, keepdims=True) + EPS)
    return (y * w.astype(jnp.float32)).astype(x.dtype)


def l2norm(a):
    return a * lax.rsqrt(jnp.sum(a * a, axis=-1, keepdims=True) + EPS)


def causal_dwconv(x, w, prev, bias=None):
    width = w.shape[0]
    t = x.shape[1]
    xp = jnp.concatenate([prev.astype(x.dtype), x], axis=1)
    y = xp[:, 0:t] * w[0]
    for i in range(1, width):
        y = y + xp[:, i:i + t] * w[i]
    if bias is not None:
        y = y + bias
    return y, xp[:, t:]


def _to_chunks(a):
    b, t, h = a.shape[:3]
    a = a.reshape((b, t // CHUNK, CHUNK, h) + a.shape[3:])
    return jnp.moveaxis(a, (1, 3), (0, 2))


def _from_chunks(o):
    n, b, h, c, d = o.shape
    return jnp.transpose(o, (1, 0, 3, 2, 4)).reshape(b, n * c, h, d)


def _delta_chunk_step(S, inp):
    q, k, v, g, beta = inp
    causal = jnp.tril(jnp.ones((CHUNK, CHUNK), bool))
    strict = jnp.tril(jnp.ones((CHUNK, CHUNK), bool), -1)
    G = jnp.cumsum(g, axis=-1)
    decay = jnp.exp(jnp.where(causal, G[..., :, None] - G[..., None, :], -jnp.inf))
    kb = k * beta[..., None]
    M = jnp.where(strict, jnp.einsum('bhid,bhjd->bhij', kb, k) * decay, 0.0)
    eye = jnp.broadcast_to(jnp.eye(CHUNK, dtype=M.dtype), M.shape)
    T = lax.linalg.triangular_solve(eye + M, eye, left_side=True, lower=True)
    u = jnp.einsum('bhij,bhje->bhie', T, v * beta[..., None])
    w = jnp.einsum('bhij,bhjd->bhid', T, kb * jnp.exp(G)[..., None])
    v_new = u - jnp.einsum('bhcd,bhde->bhce', w, S)
    attn = jnp.einsum('bhid,bhjd->bhij', q, k) * decay
    o = (jnp.einsum('bhcd,bhde->bhce', q * jnp.exp(G)[..., None], S)
         + jnp.einsum('bhij,bhje->bhie', attn, v_new))
    gl = G[..., -1]
    S = (S * jnp.exp(gl)[..., None, None]
         + jnp.einsum('bhcd,bhce->bhde', k * jnp.exp(gl[..., None] - G)[..., None], v_new))
    return S, o


def _delta_chunked(q, k, v, g, beta, s0):
    inp = tuple(_to_chunks(a) for a in (q, k, v, g, beta))
    S, o = lax.scan(_delta_chunk_step, s0.astype(jnp.float32), inp)
    return _from_chunks(o), S


def _delta_recurrent(q, k, v, g, beta, s0):
    def step(S, inp):
        qt, kt, vt, gt, bt = inp
        S = S * jnp.exp(gt)[..., None, None]
        err = vt - jnp.einsum('bhk,bhkv->bhv', kt, S)
        S = S + jnp.einsum('bhk,bhv->bhkv', kt, err * bt[..., None])
        return S, jnp.einsum('bhk,bhkv->bhv', qt, S)
    inp = tuple(jnp.moveaxis(a, 1, 0) for a in (q, k, v, g, beta))
    S, o = lax.scan(step, s0.astype(jnp.float32), inp)
    return jnp.moveaxis(o, 0, 1), S


def _gla_chunk_step(S, inp):
    q, k, v, lg = inp
    causal = jnp.tril(jnp.ones((CHUNK, CHUNK), bool))
    Bc = jnp.cumsum(lg, axis=-2)
    diff = Bc[:, :, :, None, :] - Bc[:, :, None, :, :]
    dec = jnp.exp(jnp.where(causal[..., None], diff, -jnp.inf))
    attn = jnp.einsum('bhid,bhjd,bhijd->bhij', q, k, dec)
    o = (jnp.einsum('bhcd,bhde->bhce', q * jnp.exp(Bc), S)
         + jnp.einsum('bhij,bhje->bhie', attn, v))
    bl = Bc[:, :, -1]
    S = (S * jnp.exp(bl)[..., None]
         + jnp.einsum('bhcd,bhce->bhde', k * jnp.exp(bl[:, :, None] - Bc), v))
    return S, o


def _gla_chunked(q, k, v, lg, s0):
    inp = tuple(_to_chunks(a) for a in (q, k, v, lg))
    S, o = lax.scan(_gla_chunk_step, s0.astype(jnp.float32), inp)
    return _from_chunks(o), S


def _gla_recurrent(q, k, v, lg, s0):
    def step(S, inp):
        qt, kt, vt, lt = inp
        S = S * jnp.exp(lt)[..., None] + jnp.einsum('bhk,bhv->bhkv', kt, vt)
        return S, jnp.einsum('bhk,bhkv->bhv', qt, S)
    inp = tuple(jnp.moveaxis(a, 1, 0) for a in (q, k, v, lg))
    S, o = lax.scan(step, s0.astype(jnp.float32), inp)
    return jnp.moveaxis(o, 0, 1), S


def _block(x, s_a0, conv_a_prev, s_b0, conv_f_prev, chunked,
           norm_mix, w_in, w_conv_a, a_log, dt_bias, w_gk2, b_gk, onorm_a, onorm_b,
           w_a_out, w_b_out, w_o, norm_ffn, w_ffn_in, w_conv_f, b_conv_f, w_ffn_out):
    f32 = jnp.float32
    bsz, t, _ = x.shape
    h = rmsnorm(x, norm_mix)
    idx = np.cumsum(PROJ_SIZES)[:-1].tolist()
    (qa, ka, va, za, ba, aa, qb, kb, vb, rb, lrb, gate_a, gate_b) = jnp.split(h @ w_in, idx, axis=-1)
    qkv_a, conv_a_new = causal_dwconv(jnp.concatenate([qa, ka, va], -1), w_conv_a, conv_a_prev)
    qkv_a = jax.nn.silu(qkv_a.astype(f32))
    qa, ka, va = jnp.split(qkv_a, [HA * DKA, 2 * HA * DKA], axis=-1)
    qa = l2norm(qa.reshape(bsz, t, HA, DKA)) * (DKA ** -0.5)
    ka = l2norm(ka.reshape(bsz, t, HA, DKA))
    va = va.reshape(bsz, t, HA, DVA)
    beta_a = jax.nn.sigmoid(ba.astype(f32))
    g_a = -jnp.exp(a_log.astype(f32)) * jax.nn.softplus(aa.astype(f32) + dt_bias.astype(f32))
    qb = qb.astype(f32).reshape(bsz, t, HB, DKB) * (DKB ** -0.5)
    kb = kb.astype(f32).reshape(bsz, t, HB, DKB)
    vb = vb.astype(f32).reshape(bsz, t, HB, DVB)
    lg_b = (jax.nn.log_sigmoid((lrb @ w_gk2 + b_gk).astype(f32)) / GLA_GATE_NORM).reshape(bsz, t, HB, DKB)
    if chunked:
        lead = (-N_META) % CHUNK
        pad = lambda a: jnp.pad(a, [(0, 0), (lead, 0)] + [(0, 0)] * (a.ndim - 2))
        o_a, s_a = _delta_chunked(pad(qa), pad(ka), pad(va), pad(g_a), pad(beta_a), s_a0)
        o_b, s_b = _gla_chunked(pad(qb), pad(kb), pad(vb), pad(lg_b), s_b0)
        o_a = o_a[:, lead:]
        o_b = o_b[:, lead:]
    else:
        o_a, s_a = _delta_recurrent(qa, ka, va, g_a, beta_a, s_a0)
        o_b, s_b = _gla_recurrent(qb, kb, vb, lg_b, s_b0)
    o_a = rmsnorm(o_a, onorm_a) * jax.nn.silu(za.astype(f32).reshape(bsz, t, HA, DVA))
    o_b = rmsnorm(o_b, onorm_b) * jax.nn.silu(rb.astype(f32).reshape(bsz, t, HB, DVB))
    y_a = o_a.reshape(bsz, t, HA * DVA).astype(x.dtype) @ w_a_out
    y_b = o_b.reshape(bsz, t, HB * DVB).astype(x.dtype) @ w_b_out
    mix = jax.nn.sigmoid(gate_a) * y_a + jax.nn.sigmoid(gate_b) * y_b
    x = x + mix @ w_o
    h = rmsnorm(x, norm_ffn)
    u, gf = jnp.split(h @ w_ffn_in, [D_FF], axis=-1)
    u, conv_f_new = causal_dwconv(u, w_conv_f, conv_f_prev, b_conv_f)
    x = x + (jax.nn.gelu(u) * gf) @ w_ffn_out
    return x, s_a.astype(x.dtype), conv_a_new, s_b.astype(x.dtype), conv_f_new


def setup_inputs(seed: int = 0) -> dict:
    key = jax.random.key(seed)
    ks = jax.random.split(key, 32)
    nrm = lambda k, s, sc: jax.random.normal(k, s, jnp.float32) * sc
    gain = lambda k, s: 1.0 + 0.01 * jax.random.normal(k, s, jnp.float32)
    dt = jnp.exp(jax.random.uniform(ks[10], (DEPTH, HA), jnp.float32) * (jnp.log(0.1) - jnp.log(1e-3)) + jnp.log(1e-3))
    return {
        'x_prompt': nrm(ks[0], (BATCH, SEQ, D_MODEL), 1.0),
        'x_sample': nrm(ks[1], (DEC_BATCH, DEC_SEQ, D_MODEL), 1.0),
        'state_delta': nrm(ks[2], (DEPTH, DEC_BATCH, HA, DKA, DVA), 0.1),
        'state_delta_conv': nrm(ks[3], (DEPTH, DEC_BATCH, CONV_A - 1, C_A), 1.0),
        'state_gla': nrm(ks[4], (DEPTH, DEC_BATCH, HB, DKB, DVB), 0.1),
        'state_ffn_conv': nrm(ks[5], (DEPTH, DEC_BATCH, CONV_F - 1, D_FF), 1.0),
        'meta_tokens': nrm(ks[6], (N_META, D_MODEL), 1.0),
        'norm_mix': gain(ks[7], (DEPTH, D_MODEL)),
        'w_in': nrm(ks[8], (DEPTH, D_MODEL, D_IN), D_MODEL ** -0.5),
        'w_conv_a': nrm(ks[9], (DEPTH, CONV_A, C_A), CONV_A ** -0.5),
        'a_log': jnp.log(jax.random.uniform(ks[11], (DEPTH, HA), jnp.float32, 1.0, 16.0)),
        'dt_bias': dt + jnp.log(-jnp.expm1(-dt)),
        'w_gk2': nrm(ks[12], (DEPTH, GATE_RANK, HB * DKB), GATE_RANK ** -0.5),
        'b_gk': nrm(ks[13], (DEPTH, HB * DKB), 0.1),
        'onorm_a': gain(ks[14], (DEPTH, DVA)),
        'onorm_b': gain(ks[15], (DEPTH, DVB)),
        'w_a_out': nrm(ks[16], (DEPTH, HA * DVA, D_MODEL), (HA * DVA) ** -0.5),
        'w_b_out': nrm(ks[17], (DEPTH, HB * DVB, D_MODEL), (HB * DVB) ** -0.5),
        'w_o': nrm(ks[18], (DEPTH, D_MODEL, D_MODEL), D_MODEL ** -0.5),
        'norm_ffn': gain(ks[19], (DEPTH, D_MODEL)),
        'w_ffn_in': nrm(ks[20], (DEPTH, D_MODEL, 2 * D_FF), D_MODEL ** -0.5),
        'w_conv_f': nrm(ks[21], (DEPTH, CONV_F, D_FF), CONV_F ** -0.5),
        'b_conv_f': nrm(ks[22], (DEPTH, D_FF), 0.01),
        'w_ffn_out': nrm(ks[23], (DEPTH, D_FF, D_MODEL), D_FF ** -0.5),
        'norm_final': gain(ks[24], (D_MODEL,)),
    }


def reference(x_prompt, x_sample, state_delta, state_delta_conv, state_gla, state_ffn_conv,
              meta_tokens, norm_mix, w_in, w_conv_a, a_log, dt_bias, w_gk2, b_gk, onorm_a, onorm_b,
              w_a_out, w_b_out, w_o, norm_ffn, w_ffn_in, w_conv_f, b_conv_f, w_ffn_out, norm_final):
    bsz = x_prompt.shape[0]
    meta = jnp.broadcast_to(meta_tokens.astype(x_prompt.dtype)[None], (bsz, N_META, D_MODEL))
    xp = jnp.concatenate([meta, x_prompt], axis=1)
    xs = x_sample
    new_p = ([], [], [], [])
    new_s = ([], [], [], [])
    for l in range(DEPTH):
        lw = (norm_mix[l], w_in[l], w_conv_a[l], a_log[l], dt_bias[l], w_gk2[l], b_gk[l],
              onorm_a[l], onorm_b[l], w_a_out[l], w_b_out[l], w_o[l], norm_ffn[l],
              w_ffn_in[l], w_conv_f[l], b_conv_f[l], w_ffn_out[l])
        xp, sa, ca, sb, cf = _block(
            xp, jnp.zeros((bsz, HA, DKA, DVA), jnp.float32),
            jnp.zeros((bsz, CONV_A - 1, C_A), xp.dtype),
            jnp.zeros((bsz, HB, DKB, DVB), jnp.float32),
            jnp.zeros((bsz, CONV_F - 1, D_FF), xp.dtype), True, *lw)
        for lst, val in zip(new_p, (sa, ca, sb, cf)):
            lst.append(val)
        xs, sa, ca, sb, cf = _block(xs, state_delta[l], state_delta_conv[l], state_gla[l],
                                    state_ffn_conv[l], False, *lw)
        for lst, val in zip(new_s, (sa, ca, sb, cf)):
            lst.append(val)
    y_prompt = rmsnorm(xp, norm_final)[:, N_META:]
    y_sample = rmsnorm(xs, norm_final)
    return (y_prompt, y_sample,
            jnp.stack(new_p[0]), jnp.stack(new_p[1]), jnp.stack(new_p[2]), jnp.stack(new_p[3]),
            jnp.stack(new_s[0]), jnp.stack(new_s[1]), jnp.stack(new_s[2]), jnp.stack(new_s[3]))
```

```python
import os
import numpy as np
import concourse.bass as bass
import concourse.mybir as mybir
from concourse.bass_utils import run_bass_kernel_spmd
from contextlib import ExitStack

F32 = mybir.dt.float32
BF16 = mybir.dt.bfloat16
AF = mybir.ActivationFunctionType
ALU = mybir.AluOpType

D = 1024
NPC = 2064
NT = 2080
DEC0 = 2064
TILES = [(0, 16)] + [(16 + 128 * i, 128) for i in range(16)] + [(2064, 16)]
SLABS = [(416 * s, 416) for s in range(5)]
QA, KA, VA, ZA, BA, AA, QB, KB, VB, RB, LRB, GA, GB, WEND = 0, 512, 1024, 1536, 2048, 2052, 2056, 2568, 3080, 4104, 5128, 5144, 6168, 7192
DFF = 2816
EPS = 1e-6
SB_BASE = 16640
SB_END = 229376


class T:
    __slots__ = ("name", "w", "r", "dsem")

    def __init__(self, name=""):
        self.name = name
        self.w = None
        self.r = {}
        self.dsem = None


class Em:
    ENG = ("pe", "act", "dve", "pool", "sp")

    def __init__(self, nc, stack):
        self.nc = nc
        self.stack = stack
        self.eng = {"pe": nc.tensor, "act": nc.scalar, "dve": nc.vector, "pool": nc.gpsimd, "sp": nc.sync}
        self.sem = {k: stack.enter_context(nc.semaphore("s_" + k)) for k in self.ENG}
        self.cnt = {k: 0 for k in self.ENG}
        self.known = {k: {} for k in self.ENG}
        self.dsems = []
        self.nops = 0

    def _wait(self, E, deps):
        kn = self.known[E]
        best = {}
        for d in deps:
            if d is None:
                continue
            key, sem, val = d
            if E == "pe" and key == "pe":
                continue
            if kn.get(key, 0) >= val:
                continue
            if key not in best or best[key][2] < val:
                best[key] = d
        for key, (k2, sem, val) in best.items():
            self.eng[E].wait_ge(sem, val)
            kn[key] = val

    def _deps(self, reads, writes):
        deps = []
        for t in reads:
            deps.append(t.w)
        for t in writes:
            deps.append(t.w)
            deps.extend(t.r.values())
        return deps

    def op(self, E, fn, reads=(), writes=()):
        self._wait(E, self._deps(reads, writes))
        inst = fn(self.eng[E])
        self.cnt[E] += 1
        inst.then_inc(self.sem[E], 1)
        tok = (E, self.sem[E], self.cnt[E])
        for t in reads:
            t.r[E] = tok
        for t in writes:
            t.w = tok
            t.r = {}
        self.nops += 1
        return inst

    def _dsem(self, t):
        if t.dsem is None:
            nm = "d%d" % len(self.dsems)
            s = self.stack.enter_context(self.nc.semaphore(nm))
            t.dsem = [s, 0, nm]
            self.dsems.append(t.dsem)
        return t.dsem

    def dma(self, Q, out, in_, reads=(), writes=(), own=None):
        self._wait(Q, self._deps(reads, writes))
        ds = self._dsem(own)
        inst = self.eng[Q].dma_start(out=out, in_=in_)
        ds[1] += 16
        inst.then_inc(ds[0], 16)
        tok = (ds[2], ds[0], ds[1])
        for t in reads:
            t.r[ds[2]] = tok
        for t in writes:
            t.w = tok
            t.r = {}
        self.nops += 1
        return inst

    def barrier(self):
        deps = [(k, self.sem[k], self.cnt[k]) for k in self.ENG if self.cnt[k] > 0]
        deps += [(d[2], d[0], d[1]) for d in self.dsems if d[1] > 0]
        self._wait("sp", deps)
        inst = self.eng["sp"].nop()
        self.cnt["sp"] += 1
        inst.then_inc(self.sem["sp"], 1)
        tok = ("sp", self.sem["sp"], self.cnt["sp"])
        for E in self.ENG:
            if E == "sp":
                continue
            self._wait(E, [tok])
            for d in deps:
                self.known[E][d[0]] = max(self.known[E].get(d[0], 0), d[2])

    def finish(self):
        deps = [(k, self.sem[k], self.cnt[k]) for k in self.ENG if self.cnt[k] > 0 and k != "sp"]
        deps += [(d[2], d[0], d[1]) for d in self.dsems if d[1] > 0]
        self._wait("sp", deps)


class SbAlloc:
    def __init__(self, nc):
        self.nc = nc
        self.top = SB_BASE
        self.n = 0
        self.peak = SB_BASE

    def __call__(self, name, shape, dt):
        es = 2 if dt == BF16 else 4
        nb = es
        for s in shape[1:]:
            nb *= s
        nb = (nb + 63) // 64 * 64
        off = self.top
        assert off + nb <= SB_END, ("SBUF overflow", name, off, nb)
        self.top += nb
        self.peak = max(self.peak, self.top)
        self.n += 1
        return self.nc.alloc_sbuf_tensor_at("%s_%d" % (name, self.n), list(shape), dt, offset=off)

    def mark(self):
        return self.top

    def release(self, m):
        self.top = m


def _consts():
    i = np.arange(128)
    same = (i[:, None] // 64) == (i[None, :] // 64)
    c = {}
    c["c_ident"] = np.eye(128, dtype=np.float32)
    c["c_tribd"] = (same & (i[:, None] <= i[None, :])).astype(np.float32)
    c["c_blk"] = same.astype(np.float32)
    c["c_mbig"] = np.where(same & (i[None, :] < i[:, None]), 0.0, 1e30).astype(np.float32)
    c["c_mneg"] = np.where(same & (i[None, :] >= i[:, None]), 0.0, -1e30).astype(np.float32)
    c["c_tris"] = np.where(i[:, None] <= i[None, :], -1.0 / 16.0, 0.0).astype(np.float32)
    c["c_trisuf"] = np.where(i[:, None] > i[None, :], -1.0 / 16.0, 0.0).astype(np.float32)
    c["c_m01"] = (i[None, :] >= i[:, None]).astype(np.float32)
    sel = np.zeros((16, 16, 128), np.float32)
    for b in range(16):
        sel[b, b, :] = 1.0
    c["c_sel"] = sel.reshape(16, 2048)
    return c


IN_SHAPES = {
    "xp": [2048, D], "meta": [16, D], "xs": [16, D],
    "sd": [16, 4, 128, 128], "sdc": [16, 3, 1536], "sg": [16, 4, 128, 256], "sfc": [16, 2, DFF],
    "w_in": [D, WEND], "w_conv_a": [4, 1536], "a_log": [1, 4], "dt_bias": [1, 4], "w_gk2": [16, 512],
    "b_gk": [1, 512], "onorm_a": [1, 128], "onorm_b": [1, 256], "w_a_out": [512, D], "w_b_out": [D, D],
    "w_o": [D, D], "norm_mix": [1, D], "norm_ffn": [1, D], "w_ffn_in": [D, 2 * DFF], "w_conv_f": [3, DFF],
    "b_conv_f": [1, DFF], "w_ffn_out": [DFF, D], "norm_final": [1, D],
    "c_ident": [128, 128], "c_tribd": [128, 128], "c_blk": [128, 128], "c_mbig": [128, 128], "c_mneg": [128, 128],
    "c_tris": [128, 128], "c_trisuf": [128, 128], "c_m01": [128, 128], "c_sel": [16, 2048],
}
OUT_SHAPES = {
    "y_p": [2048, D], "y_s": [16, D], "nd_p": [4, 128, 128], "ndc_p": [3, 1536], "ng_p": [4, 128, 256],
    "nfc_p": [2, DFF], "nd_s": [16, 4, 128, 128], "ndc_s": [16, 3, 1536], "ng_s": [16, 4, 128, 256], "nfc_s": [16, 2, DFF],
}


def build(stage=99, dbg=None):
    nc = bass.Bass("TRN2", target_bir_lowering=False)
    I = {k: nc.dram_tensor(k, v, F32, kind="ExternalInput").ap() for k, v in IN_SHAPES.items()}
    O = {k: nc.dram_tensor(k, v, F32, kind="ExternalOutput").ap() for k, v in OUT_SHAPES.items()}
    dbg_out = {}
    DBG = os.environ.get("KDBG", "").split(",")

    def dump(name, tensor, shape, dt):
        if name not in DBG:
            return
        o_ = nc.dram_tensor("dbg_" + name, list(shape), dt, kind="ExternalOutput").ap()
        em_[0].barrier()
        t_ = T("dbg" + name)
        em_[0].dma("sp", o_, tensor, (), (), t_)
        em_[0].barrier()
    em_ = [None]
    st = ExitStack()
    with st:
        em = Em(nc, st)
        em_[0] = em
        sb = SbAlloc(nc)
        PB = [st.enter_context(nc.psum_tensor("pb%d" % b, [128, 512], F32)) for b in range(8)]
        PT = [T("pb%d" % b) for b in range(8)]

        def pbf(b):
            return PB[b][:, :].bitcast(BF16)

        def ACT(out, in_, func, reads=(), writes=(), **kw):
            return em.op("act", lambda e: e.activation(out=out, in_=in_, func=func, **kw), reads, writes)

        def COPY(E, out, in_, reads=(), writes=()):
            if E == "act":
                return em.op("act", lambda e: e.activation(out=out, in_=in_, func=AF.Copy), reads, writes)
            return em.op(E, lambda e: e.tensor_copy(out=out, in_=in_), reads, writes)

        def TT(E, out, in0, in1, op, reads=(), writes=()):
            return em.op(E, lambda e: e.tensor_tensor(out=out, in0=in0, in1=in1, op=op), reads, writes)

        def TS(E, out, in0, s1, op0, s2=None, op1=None, reads=(), writes=()):
            if op1 is None:
                return em.op(E, lambda e: e.tensor_scalar(out=out, in0=in0, scalar1=s1, scalar2=None, op0=op0), reads, writes)
            return em.op(E, lambda e: e.tensor_scalar(out=out, in0=in0, scalar1=s1, scalar2=s2, op0=op0, op1=op1), reads, writes)

        def STT(out, in0, scalar, in1, op0, op1, reads=(), writes=()):
            return em.op("dve", lambda e: e.scalar_tensor_tensor(out=out, in0=in0, scalar=scalar, in1=in1, op0=op0, op1=op1), reads, writes)

        def MM(out, lhsT, rhs, start=True, stop=True, reads=(), writes=()):
            return em.op("pe", lambda e: e.matmul(out, lhsT=lhsT, rhs=rhs, start=start, stop=stop), reads, writes)

        def TR(out, in_, ident, reads=(), writes=()):
            return em.op("pe", lambda e: e.transpose(out, in_, ident), reads, writes)

        def MEMSET(E, ap, val, writes=()):
            return em.op(E, lambda e: e.memset(ap, val), (), writes)

        dq = ["sp", "act"]
        dqi = [0]

        def DMA(out, in_, reads=(), writes=(), own=None, q=None):
            if q is None:
                q = dq[dqi[0] % len(dq)]
                dqi[0] += 1
            return em.dma(q, out, in_, reads, writes, own)

        cn = {}
        Tc = T("consts")
        for k in ["c_ident", "c_tribd", "c_blk", "c_mbig", "c_mneg", "c_tris", "c_trisuf", "c_m01"]:
            cn[k] = sb(k, [128, 128], F32)
            DMA(cn[k][:, :], I[k][:, :], writes=[Tc], own=Tc)
        ident_f = cn["c_ident"]
        ident_b = sb("ident_b", [128, 128], BF16)
        COPY("dve", ident_b[:, :], ident_f[:, :], [Tc], [Tc])
        ones_f = sb("ones_f", [128, 128], F32)
        MEMSET("pool", ones_f[:, :], 1.0, [Tc])
        m01_b = sb("m01_b", [128, 128], BF16)
        COPY("dve", m01_b[:, :], cn["c_m01"][:, :], [Tc], [Tc])

        eps_col = sb("eps_col", [128, 4], F32)
        MEMSET("pool", eps_col[:, 0:1], EPS, [Tc])
        MEMSET("pool", eps_col[:, 1:2], float(np.log(128.0 ** -0.5)), [Tc])
        MEMSET("pool", eps_col[:, 2:3], 1.0, [Tc])
        vec = sb("vecs", [128, 16], F32)
        Tvec = T("vec")
        DMA(vec[:, 0:4], I["dt_bias"][0:1, :].partition_broadcast(128), writes=[Tvec], own=Tvec)
        DMA(vec[:, 4:8], I["a_log"][0:1, :].partition_broadcast(128), writes=[Tvec], own=Tvec)
        ACT(vec[:, 4:8], vec[:, 4:8], AF.Exp, [Tvec], [Tvec])
        TS("dve", vec[:, 4:8], vec[:, 4:8], -1.0, ALU.mult, reads=[Tvec], writes=[Tvec])
        onA_bc = sb("onA_bc", [128, 128], F32)
        DMA(onA_bc[:, :], I["onorm_a"][0:1, :].partition_broadcast(128), writes=[Tvec], own=Tvec)
        onB_bc = sb("onB_bc", [128, 256], F32)
        DMA(onB_bc[:, :], I["onorm_b"][0:1, :].partition_broadcast(128), writes=[Tvec], own=Tvec)
        oncol = sb("oncol", [128, 4], F32)
        with nc.allow_non_contiguous_dma(reason="tiny column loads"):
            DMA(oncol[:, 0:1], I["onorm_a"].rearrange("o d -> d o"), writes=[Tvec], own=Tvec)
            DMA(oncol[:, 1:3], I["onorm_b"].rearrange("o (c d) -> d (o c)", c=2), writes=[Tvec], own=Tvec)

        wcaT = sb("wcaT", [128, 12, 4], F32)
        Twca = T("wca")
        m0 = sb.mark()
        wca_sb = sb("wca_sb", [4, 1536], F32)
        Ttmp = T("tmp")
        DMA(wca_sb[:, :], I["w_conv_a"][:, :], writes=[Ttmp], own=Ttmp)
        for cc in range(12):
            TR(PB[0][:, cc * 4:cc * 4 + 4], wca_sb[0:4, cc * 128:(cc + 1) * 128], ident_f[0:4, 0:4], [Ttmp, Tc], [PT[0]])
        COPY("dve", wcaT[:, :, :].rearrange("p a b -> p (a b)"), PB[0][:, 0:48], [], [PT[0], Twca])
        sb.release(m0)

        F32R = mybir.dt.float32r
        KR_GBC = os.environ.get("KR_GBC", "1") == "1"
        KR_GLA = os.environ.get("KR_GLA", "1") == "1"
        KR_A1 = os.environ.get("KR_A1", "1") == "1"
        ones_r = sb("ones_r", [128, 128], F32R)
        COPY("dve", ones_r[:, :], ones_f[:, :], [Tc], [Tc])
        tris_r = sb("tris_r", [128, 128], F32R)
        COPY("dve", tris_r[:, :], cn["c_tris"][:, :], [Tc], [Tc])
        trisuf_r = sb("trisuf_r", [128, 128], F32R)
        COPY("dve", trisuf_r[:, :], cn["c_trisuf"][:, :], [Tc], [Tc])
        tribd_r0 = sb("tribd_r0", [128, 128], F32R)
        COPY("dve", tribd_r0[:, :], cn["c_tribd"][:, :], [Tc], [Tc])
        em.barrier()

        hT_off = sb.top
        hT = sb("hT", [128, 8, NT], BF16)
        ThT = [T("hT%d" % i) for i in range(18)]

        def tile_src(ti):
            if ti == 0:
                return I["meta"][0:16, :]
            if ti == 17:
                return I["xs"][0:16, :]
            return I["xp"][128 * (ti - 1):128 * ti, :]

        def run_window(make_gen, items, width):
            items = list(items)
            active = []
            nxt = 0
            while active or nxt < len(items):
                while len(active) < width and nxt < len(items):
                    active.append(make_gen(items[nxt]))
                    nxt += 1
                keep = []
                for g in active:
                    try:
                        next(g)
                        keep.append(g)
                    except StopIteration:
                        pass
                active = keep

        def norm_tiles(dstT, dstTT, get_x, nwt, Tnwt, bank0):
            NB = 4
            mk = sb.mark()
            hb = [sb("hb%d" % i, [128, D], BF16) for i in range(NB)]
            Thb = [T("hb%d" % i) for i in range(NB)]
            jk = sb("jk", [128, D], BF16)
            Tjk = T("jk")
            st_ = sb("nst", [128, 3, 18], F32)
            Tst = T("nst")
            MEMSET("pool", st_[:, 0, :], 1.0, [Tst])
            xs = [get_x(ti) for ti in range(18)]
            for ti in range(18):
                c0, n = TILES[ti]
                xa, xr = xs[ti]
                ACT(jk[0:n, :], xa, AF.Square, xr, [Tjk, Tst], accum_out=st_[0:n, 0, ti:ti + 1])
            ACT(st_[:, 2, 0:1], eps_col[:, 0:1], AF.Copy, [], [Tst])
            ACT(st_[:, 1, :], st_[:, 0, :], AF.Ln, [], [Tst], scale=1.0 / D, bias=eps_col[:, 0:1])
            ACT(st_[:, 2, :], st_[:, 1, :], AF.Exp, [], [Tst], scale=-0.5)

            def gen(ti):
                c0, n = TILES[ti]
                xa, xr = xs[ti]
                s = ti % NB
                STT(hb[s][0:n, :], xa, st_[0:n, 2, ti:ti + 1], nwt[0:n, :], ALU.mult, ALU.mult, xr + [Tst, Tnwt], [Thb[s]])
                yield
                bk = bank0 + s
                for kc in range(8):
                    TR(pbf(bk)[:, kc * 128:kc * 128 + n], hb[s][0:n, kc * 128:(kc + 1) * 128], ident_b[0:n, 0:n], [Thb[s]], [PT[bk]])
                yield
                COPY("act" if ti % 2 else "dve", dstT[:, :, c0:c0 + n],
                     pbf(bk).rearrange("p (k c) -> p k c", k=8)[:, :, 0:n], [], [PT[bk], dstTT[ti]])
                yield
            run_window(gen, range(18), NB)
            sb.release(mk)


        mP0 = sb.mark()
        nw_bc = sb("nw_bc", [128, D], F32)
        Tnw = T("nw")
        DMA(nw_bc[:, :], I["norm_mix"][0:1, :].partition_broadcast(128), writes=[Tnw], own=Tnw)
        xst = [sb("xst%d" % i, [128, D], F32) for i in range(18)]
        Txst = [T("xst%d" % i) for i in range(18)]

        def get_x0(ti):
            c0, n = TILES[ti]
            s = ti
            DMA(xst[s][0:n, :], tile_src(ti), writes=[Txst[s]], own=Txst[s])
            return xst[s][0:n, :], [Txst[s]]

        norm_tiles(hT, ThT, get_x0, nw_bc, Tnw, 0)
        em.barrier()
        sb.release(mP0)

        wsl = [None] * 3
        Tw = [None] * 3
        wctr = [0]

        def alloc_wsl():
            for i in range(3):
                wsl[i] = sb("wsl%d" % i, [128, 8, 512], BF16)
                Tw[i] = T("wsl%d" % i)

        def wload(src3, nk, ncol):
            s = wctr[0] % 3
            wctr[0] += 1
            if isinstance(src3, tuple):
                for i_, sr in enumerate(src3):
                    em.dma("pool", wsl[s][:, 0:nk, i_ * ncol:(i_ + 1) * ncol], sr, (), [Tw[s]], Tw[s])
            else:
                em.dma("pool", wsl[s][:, 0:nk, 0:ncol], src3, (), [Tw[s]], Tw[s])
            return s

        def w_in_blk(c0, ncol):
            return I["w_in"][:, c0:c0 + ncol].rearrange("(k p) n -> p k n", p=128)

        def run_jobs(jobs):
            slots = {}
            for i in range(min(2, len(jobs))):
                slots[i] = wload(*jobs[i][0:3])
            for i, jb in enumerate(jobs):
                if i + 2 < len(jobs):
                    slots[i + 2] = wload(*jobs[i + 2][0:3])
                jb[3](slots[i])

        def tiles_in(c0, n):
            return [ti for ti, (a, m) in enumerate(TILES) if a < c0 + n and c0 < a + m]

        pbr = [0]

        def fm_proj(slot, j, ncols, src, srcT, nk, slabs, epi, banks=(0, 1, 2, 3)):
            for (c0, n) in slabs:
                bk = banks[pbr[0] % len(banks)]
                pbr[0] += 1
                rd = [Tw[slot]] + [srcT[ti] for ti in tiles_in(c0, n)]
                for kc in range(nk):
                    MM(PB[bk][0:ncols, 0:n], wsl[slot][:, kc, j * 128:j * 128 + ncols], src[:, kc, c0:c0 + n],
                       kc == 0, kc == nk - 1, rd, [PT[bk]])
                epi(PB[bk][0:ncols, 0:n], c0, n, bk)

        oaT = sb("oaT", [128, 4, NT], BF16)
        mA = sb.mark()
        kvT = sb("kvT", [128, 8, NT], BF16)

        class _QKV:
            def __getitem__(self, key):
                p, cc, cols = key
                if isinstance(cc, slice):
                    assert cc.start == 0 and cc.stop == 4
                    return oaT[p, cc, cols]
                if cc < 4:
                    return oaT[p, cc, cols]
                return kvT[p, cc - 4, cols]
        qkvT = _QKV()
        Tq = [[T("qkv%d_%d" % (g, i)) for i in range(18)] for g in range(3)]
        decq = sb("decq", [128, 12, 16], F32)
        Tdecq = T("decq")
        lastpre = sb("lastpre", [128, 12, 3], F32)
        Tlast = T("lastpre")
        decpre = sb("decpre", [128, 12, 16], F32)
        Tdecpre = T("decpre")
        bg_all = sb("bg_all", [128, 18, 8], F32)
        Tbg = T("bg_all")
        MEMSET("pool", bg_all[:, :, :].rearrange("p a b -> p (a b)"), 0.0, [Tbg])
        prevT = sb("prevT", [128, 12, 3, 16], F32)
        TprevT = T("prevT")

        mk = sb.mark()
        sdc_sb = sb("sdc_sb", [16, 3, 1536], F32)
        Tsdc = T("sdc")
        DMA(sdc_sb[:, :, :], I["sdc"][:, :, :], writes=[Tsdc], own=Tsdc)
        DMA(O["ndc_s"][:, 0:2, :], sdc_sb[:, 1:3, :], reads=[Tsdc], own=Tsdc)
        for cc in range(12):
            for r in range(3):
                TR(PB[4][:, (r * 16):(r * 16 + 16)], sdc_sb[0:16, r, cc * 128:(cc + 1) * 128], ident_f[0:16, 0:16], [Tsdc], [PT[4]])
            COPY("act", prevT[:, cc, :, :].rearrange("p a b -> p (a b)"), PB[4][:, 0:48], [], [PT[4], TprevT])
        em.barrier()
        sb.release(mk)

        mA1 = sb.mark()
        alloc_wsl()
        pre = [sb("pre%d" % i, [128, 3 + NT], F32) for i in range(2)]
        Tpre = [T("pre%d" % i) for i in range(2)]
        for i in range(2):
            MEMSET("pool", pre[i][:, 0:3], 0.0, [Tpre[i]])
        cv = sb("cv", [128, NT], F32)
        Tcv = T("cv")
        sv_l = [sb("sv%d" % i, [128, NT], F32) for i in range(2)]
        Tsv_l = [T("sv%d" % i) for i in range(2)]
        sq_l = [sb("sq%d" % i, [128, NT], F32) for i in range(2)]
        Tsq_l = [T("sq%d" % i) for i in range(2)]
        rn = [sb("rn%d" % i, [128, 512], F32) for i in range(2)]
        Trn = [T("rn%d" % i) for i in range(2)]
        rnc = [0]

        def conv_proj(cc, slot, j):
            ps_ = cc % 2

            def epi(ps, c0, n, bk):
                COPY("dve" if (c0 // 416) in (1, 3) else "act", pre[ps_][:, 3 + c0:3 + c0 + n], ps, [], [PT[bk], Tpre[ps_]])
            fm_proj(slot, j, 128, hT, ThT, 8, SLABS, epi)

        def conv_C(cc):
            ps_ = cc % 2
            p = pre[ps_]
            TS("dve", cv[:, 0:NPC], p[:, 0:NPC], wcaT[:, cc, 0:1], ALU.mult, reads=[Tpre[ps_], Twca], writes=[Tcv])
            for i in range(1, 4):
                STT(cv[:, 0:NPC], p[:, i:i + NPC], wcaT[:, cc, i:i + 1], cv[:, 0:NPC], ALU.mult, ALU.add, [Tpre[ps_], Twca], [Tcv])
            TS("dve", cv[:, DEC0:NT], p[:, 3 + DEC0:3 + NT], wcaT[:, cc, 3:4], ALU.mult, reads=[Tpre[ps_], Twca], writes=[Tcv])
            for i in range(3):
                STT(cv[:, DEC0:NT], prevT[:, cc, i, :], wcaT[:, cc, i:i + 1], cv[:, DEC0:NT], ALU.mult, ALU.add, [TprevT, Twca], [Tcv])
            COPY("pool", lastpre[:, cc, :], p[:, 3 + NPC - 3:3 + NPC], [Tpre[ps_]], [Tlast])
            COPY("pool", decpre[:, cc, :], p[:, 3 + DEC0:3 + NT], [Tpre[ps_]], [Tdecpre])

        def conv_S(cc):
            g = cc // 4
            if g == 2:
                ACT(qkvT[:, cc, :], cv[:, :], AF.Silu, [Tcv], Tq[g])
                ACT(decq[:, cc, :], cv[:, DEC0:NT], AF.Silu, [Tcv], [Tdecq])
                return
            sv, Tsv, sq, Tsq = sv_l[cc % 2], Tsv_l[cc % 2], sq_l[cc % 2], Tsq_l[cc % 2]
            ACT(sv[:, :], cv[:, :], AF.Silu, [Tcv], [Tsv])
            ACT(sq[:, :].bitcast(F32R) if KR_A1 else sq[:, :], sv[:, :], AF.Square, [Tsv], [Tsq])

        def conv_N(cc):
            g = cc // 4
            if g == 2:
                return
            sv, Tsv, sq, Tsq = sv_l[cc % 2], Tsv_l[cc % 2], sq_l[cc % 2], Tsq_l[cc % 2]
            for (c0, n) in SLABS:
                bk = 4 + (rnc[0] % 2)
                r = rnc[0] % 2
                rnc[0] += 1
                if KR_A1:
                    MM(PB[bk][:, 0:n], ones_r[:, :], sq[:, c0:c0 + n].bitcast(F32R), True, True, [Tsq], [PT[bk]])
                else:
                    MM(PB[bk][:, 0:n], ones_f[:, :], sq[:, c0:c0 + n], True, True, [Tsq], [PT[bk]])
                ACT(rn[r][:, 0:n], PB[bk][:, 0:n], AF.Ln, [], [PT[bk], Trn[r]], bias=eps_col[:, 0:1])
                if g == 0:
                    ACT(rn[r][:, 0:n], rn[r][:, 0:n], AF.Exp, [], [Trn[r]], scale=-0.5, bias=eps_col[:, 1:2])
                else:
                    ACT(rn[r][:, 0:n], rn[r][:, 0:n], AF.Exp, [], [Trn[r]], scale=-0.5)
                TT("dve", qkvT[:, cc, c0:c0 + n], sv[:, c0:c0 + n], rn[r][:, 0:n], ALU.mult, [Tsv, Trn[r]],
                   [Tq[g][ti] for ti in tiles_in(c0, n)])
                if c0 <= DEC0 < c0 + n:
                    TT("dve", decq[:, cc, :], sv[:, DEC0:NT], rn[r][:, DEC0 - c0:NT - c0], ALU.mult, [Tsv, Trn[r]], [Tdecq])

        def job_qkv(g):
            def fn(slot):
                for j in range(4):
                    conv_chunk(g * 4 + j, slot, j)
            return (w_in_blk(g * 512, 512), 8, 512, fn)

        def job_bg(slot):
            for ti in range(18):
                c0, n = TILES[ti]
                for kc in range(8):
                    MM(PB[6][0:n, ti * 8:ti * 8 + 8], hT[:, kc, c0:c0 + n], wsl[slot][:, kc, 0:8], kc == 0, kc == 7,
                       [Tw[slot], ThT[ti]], [PT[6]])
            for ti in range(18):
                c0, n = TILES[ti]
                COPY("dve", bg_all[0:n, ti, :], PB[6][0:n, ti * 8:ti * 8 + 8], [], [PT[6], Tbg])

        qslots = [wload(w_in_blk(g * 512, 512), 8, 512) for g in range(3)]
        conv_proj(0, qslots[0], 0)
        conv_proj(1, qslots[0], 1)
        conv_C(0)
        conv_S(0)
        for cc in range(12):
            if cc + 2 < 12:
                conv_proj(cc + 2, qslots[(cc + 2) // 4], (cc + 2) % 4)
            if cc + 1 < 12:
                conv_C(cc + 1)
            conv_N(cc)
            if cc + 1 < 12:
                conv_S(cc + 1)
        job_bg(wload(w_in_blk(BA, 8), 8, 8))

        mk = sb.mark()
        cvo = sb("cvo", [16, 1536], F32)
        Tcvo = T("cvo")
        cvo2 = sb("cvo2", [3, 1536], F32)
        Tcvo2 = T("cvo2")
        for cc in range(12):
            TR(PB[7][0:16, (cc % 4) * 128:(cc % 4) * 128 + 128], decpre[:, cc, :], ident_f[:, :], [Tdecpre], [PT[7]])
            if cc % 4 == 3:
                COPY("dve", cvo[0:16, (cc - 3) * 128:(cc + 1) * 128], PB[7][0:16, 0:512], [], [PT[7], Tcvo])
        DMA(O["ndc_s"][:, 2, :], cvo[0:16, :], reads=[Tcvo], own=Tcvo)
        for cc in range(12):
            TR(PB[7][0:3, (cc % 4) * 128:(cc % 4) * 128 + 128], lastpre[:, cc, :], ident_f[:, :], [Tlast], [PT[7]])
            if cc % 4 == 3:
                COPY("dve", cvo2[0:3, (cc - 3) * 128:(cc + 1) * 128], PB[7][0:3, 0:512], [], [PT[7], Tcvo2])
        DMA(O["ndc_p"][:, :], cvo2[0:3, :], reads=[Tcvo2], own=Tcvo2)
        em.barrier()
        sb.release(mk)
        sb.release(mA1)
        if stage <= 1:
            em.finish()
            print("sbuf peak", sb.peak, "ops", em.nops, "dsems", len(em.dsems))
            return nc

        mA2 = sb.mark()
        hd = []
        for h in range(4):
            d_ = {}
            CH_BF = os.environ.get("KCHAIN", "bf16") == "bf16"
            for nm in ["gcol", "tm", "Ds", "u", "eGbc"]:
                d_[nm] = sb("%s%d" % (nm, h), [128, 128], F32)
            for nm in ["A0", "A1", "B0", "B1", "P0", "P1"]:
                d_[nm] = sb("%s%d" % (nm, h), [128, 128], BF16 if CH_BF else F32)
            for nm in ["Pbf", "kbg", "kd", "vb", "wT", "atm", "qg", "vn", "on"]:
                d_[nm] = sb("%s%d" % (nm, h), [128, 128], BF16)
            d_["S"] = sb("S%d" % h, [128, 128], F32)
            d_["Sbf"] = sb("Sbf%d" % h, [128, 128], BF16)
            d_["col"] = sb("col%d" % h, [128, 8], F32)
            d_["T"] = {nm: T("%s%d" % (nm, h)) for nm in list(d_.keys())}
            d_["hb"] = [dict(u=d_["u"], wT=d_["wT"], atm=d_["atm"], qg=d_["qg"], kd=d_["kd"], col=d_["col"]),
                        dict(u=sb("u%db" % h, [128, 128], F32), wT=sb("wT%db" % h, [128, 128], BF16), atm=sb("atm%db" % h, [128, 128], BF16),
                             qg=sb("qg%db" % h, [128, 128], BF16), kd=sb("kd%db" % h, [128, 128], BF16), col=sb("col%db" % h, [128, 8], F32))]
            d_["hT"] = [{k: d_["T"][k] for k in d_["hb"][0]}, {k: T("%s%db" % (k, h)) for k in d_["hb"][1]}]
            hd.append(d_)
            MEMSET("pool", d_["S"][:, :], 0.0, [d_["T"]["S"]])
            MEMSET("pool", d_["Sbf"][:, :], 0.0, [d_["T"]["Sbf"]])
        sc = sb("sc", [128, 64], F32)
        Tsc = T("sc")
        ojk = sb("ojk", [128, 128], BF16)
        Tojk = T("ojk")
        sc2 = sb("sc2", [128, 16], F32)
        Tsc2 = T("sc2")
        F32R = mybir.dt.float32r
        tribd_r = sb("tribd_r", [128, 128], F32R)
        Ttr = T("tribd_r")
        COPY("dve", tribd_r[:, :], cn["c_tribd"][:, :], [], [Ttr])
        Sd = [sb("Sd%d" % i, [128, 4, 128], F32) for i in range(2)]
        TSd = [T("Sd%d" % i) for i in range(2)]
        Sn = [sb("Sn%d" % i, [128, 4, 128], F32) for i in range(2)]
        TSn = [T("Sn%d" % i) for i in range(2)]
        dX = sb("dX", [16, 192], F32)
        TdX = T("dX")
        dbc = sb("dbc", [128, 192], F32)
        Tdbc = T("dbc")
        dvb = sb("dvb", [128, 4, 16], F32)
        Tdvb = T("dvb")
        ddiag = sb("ddiag", [128, 128], F32)
        Tddiag = T("ddiag")
        dtmp = sb("dtmp", [128, 128], F32)
        Tdtmp = T("dtmp")
        derr = sb("derr", [128, 4], F32)
        Tderr = T("derr")
        odec = sb("odec", [128, 64], F32)
        Todec = T("odec")

        sca = sb("sca", [128, 18, 40], F32)
        Tsca = T("sca")
        gall = sb("gall", [128, 72], F32)
        Tgall = T("gall")

        def all_scalars():
            V = lambda a_, b_: sca[:, :, a_:b_]
            ACT(V(0, 4), bg_all[:, :, 0:4], AF.Sigmoid, [Tbg], [Tsca])
            TS("dve", V(4, 8), V(0, 4), -1.0, ALU.mult, reads=[], writes=[Tsca])
            for ti in range(18):
                TT("dve", sca[:, ti, 32:36], bg_all[:, ti, 4:8], vec[:, 0:4], ALU.add, [Tbg], [Tsca])
            ACT(V(32, 36), V(32, 36), AF.Exp, [], [Tsca])
            ACT(V(32, 36), V(32, 36), AF.Ln, [], [Tsca], bias=eps_col[:, 2:3])
            for ti in range(18):
                TT("dve", sca[:, ti, 8:12], sca[:, ti, 32:36], vec[:, 4:8], ALU.mult, [], [Tsca])
            COPY("dve", gall[:, :].rearrange("p (a b) -> p a b", a=18), V(8, 12), [Tsca], [Tgall])
            MM(PB[1][:, 0:72], cn["c_tribd"][:, :], gall[:, :], True, True, [Tgall], [PT[1]])
            MM(PB[1][:, 128:200], cn["c_blk"][:, :], gall[:, :], True, True, [Tgall], [PT[1]])
            COPY("dve", V(12, 16), PB[1][:, 0:72].rearrange("p (a b) -> p a b", a=18), [], [PT[1], Tsca])
            COPY("dve", V(16, 20), PB[1][:, 128:200].rearrange("p (a b) -> p a b", a=18), [], [PT[1], Tsca])
            MM(PB[1][0:16, 256:260], cn["c_blk"][0:16, 0:16], gall[0:16, 0:4], True, True, [Tgall], [PT[1]])
            COPY("dve", sca[0:16, 0, 16:20], PB[1][0:16, 256:260], [], [PT[1], Tsca])
            ACT(V(20, 24), V(12, 16), AF.Exp, [], [Tsca])
            TT("dve", V(32, 36), V(16, 20), V(12, 16), ALU.subtract, [], [Tsca])
            ACT(V(24, 28), V(32, 36), AF.Exp, [], [Tsca])
            TT("dve", V(28, 32), V(0, 4), V(20, 24), ALU.mult, [], [Tsca])

        def delta_prep(ti, heads):
            c0, n = TILES[ti]
            pb_ = ti % 2
            chunks = [(0, min(n, 64))] + ([(64, 64)] if n == 128 else [])
            nk = 5 if n == 128 else 3
            KF = os.environ.get("KF32R", "2")
            CH_BF = os.environ.get("KCHAIN", "bf16") == "bf16"
            USE_R = KF in ("1", "2") and not CH_BF
            USE_RG = KF in ("1", "3")
            R = (lambda ap: ap.bitcast(F32R)) if (n == 128 and USE_R) else (lambda ap: ap)
            RO = (lambda ap: ap.bitcast(F32R)) if USE_R else (lambda ap: ap)
            sc = sca[:, ti, :]
            Tsc = Tsca
            cs = slice(c0, c0 + n)
            qT = lambda h: qkvT[:, h, cs]
            kT = lambda h: qkvT[:, 4 + h, cs]
            vT = lambda h: qkvT[:, 8 + h, cs]
            rq = [Tq[0][ti]]
            rk = [Tq[1][ti]]
            rv = [Tq[2][ti]]
            H = heads
            D_ = lambda h: hd[h]
            TD_ = lambda h: hd[h]["T"]
            HB = lambda h: hd[h]["hb"][pb_]
            HT = lambda h: hd[h]["hT"][pb_]
            Wb = lambda h: 2 * h
            Gb = lambda h: 2 * h + 1
            X = lambda h, r, w: PB[Wb(h)][r, 0:w]
            Y = lambda h, r, w: PB[Wb(h)][r, 128:128 + w]
            for h in H:
                TS("pool", D_(h)["gcol"][0:n, :].bitcast(F32R) if KR_GBC else RO(D_(h)["gcol"][0:n, :]), ones_f[0:n, :], sc[0:n, 8 + h:9 + h], ALU.mult,
                   1.0, ALU.mult, reads=[Tsc], writes=[TD_(h)["gcol"]])
            yield
            for h in H:
                if n == 128 and KR_GBC:
                    MM(PB[Gb(h)][:, 0:n], D_(h)["gcol"][0:n, :].bitcast(F32R), tribd_r0[0:n, 0:n], True, True, [TD_(h)["gcol"]], [PT[Gb(h)]])
                else:
                    MM(PB[Gb(h)][:, 0:n], D_(h)["gcol"][0:n, :], cn["c_tribd"][0:n, 0:n], True, True, [TD_(h)["gcol"]], [PT[Gb(h)]])
                MM(X(h, slice(0, n), n), kT(h), kT(h), True, True, rk, [PT[Wb(h)]])
            yield
            for h in H:
                STT(D_(h)["tm"][0:n, 0:n], PB[Gb(h)][0:n, 0:n], sc[0:n, 12 + h:13 + h], cn["c_mbig"][0:n, 0:n], ALU.subtract, ALU.max,
                    [Tsc], [PT[Gb(h)], TD_(h)["tm"]])
            yield
            for h in H:
                ACT(D_(h)["Ds"][0:n, 0:n], D_(h)["tm"][0:n, 0:n], AF.Exp, [TD_(h)["tm"]], [TD_(h)["Ds"]], scale=-1.0)
            yield
            for h in H:
                STT(RO(D_(h)["A0"][0:n, 0:n]), X(h, slice(0, n), n), sc[0:n, 4 + h:5 + h], D_(h)["Ds"][0:n, 0:n], ALU.mult, ALU.mult,
                    [Tsc, TD_(h)["Ds"]], [PT[Wb(h)], TD_(h)["A0"]])
            yield
            if CH_BF:
                YB = lambda h: pbf(Wb(h))[0:n, 256:256 + n]
            else:
                YB = lambda h: Y(h, slice(0, n), n)
            for h in H:
                TR(YB(h), D_(h)["A0"][0:n, 0:n], (ident_b if CH_BF else ident_f)[0:n, 0:n], [TD_(h)["A0"]], [PT[Wb(h)]])
            yield
            for h in H:
                STT(D_(h)["tm"][0:n, 0:n], PB[Gb(h)][0:n, 0:n], sc[0:n, 12 + h:13 + h], cn["c_mneg"][0:n, 0:n], ALU.subtract, ALU.min,
                    [Tsc], [PT[Gb(h)], TD_(h)["tm"]])
            yield
            for h in H:
                COPY("act", RO(D_(h)["B0"][0:n, 0:n]), YB(h), [], [PT[Wb(h)], TD_(h)["B0"]])
            yield
            for h in H:
                TT("dve", RO(D_(h)["P0"][0:n, 0:n]), YB(h), ident_f[0:n, 0:n], ALU.add, [], [PT[Wb(h)], TD_(h)["P0"]])
            yield
            for h in H:
                ACT(D_(h)["Ds"][0:n, 0:n], D_(h)["tm"][0:n, 0:n], AF.Exp, [TD_(h)["tm"]], [TD_(h)["Ds"]])
                ACT(D_(h)["eGbc"][:, 0:n], PB[Gb(h)][:, 0:n], AF.Exp, [], [PT[Gb(h)], TD_(h)["eGbc"]])
                for ci, (r0, m) in enumerate(chunks):
                    ACT(HB(h)["col"][:, ci:ci + 1], PB[Gb(h)][:, r0 + m - 1:r0 + m], AF.Exp, [], [PT[Gb(h)], HT(h)["col"]])
            yield
            for k in range(nk):
                a, b_ = "A%d" % (k % 2), "A%d" % ((k + 1) % 2)
                ba, bb = "B%d" % (k % 2), "B%d" % ((k + 1) % 2)
                pa, pb2 = "P%d" % (k % 2), "P%d" % ((k + 1) % 2)
                for h in H:
                    d_, TD = D_(h), TD_(h)
                    MM(X(h, slice(0, n), n), R(d_[ba][0:n, 0:n]), R(d_[a][0:n, 0:n]), True, True, [TD[ba], TD[a]], [PT[Wb(h)]])
                    if k < nk - 1:
                        MM(PB[Gb(h)][0:n, 256:256 + n], R(d_[a][0:n, 0:n]), R(d_[ba][0:n, 0:n]), True, True, [TD[ba], TD[a]], [PT[Gb(h)]])
                yield
                for h in H:
                    d_, TD = D_(h), TD_(h)
                    COPY("act", RO(d_[b_][0:n, 0:n]), X(h, slice(0, n), n), [], [PT[Wb(h)], TD[b_]])
                yield
                if k < nk - 1:
                    for h in H:
                        d_, TD = D_(h), TD_(h)
                        COPY("act" if h >= 2 else "dve", RO(d_[bb][0:n, 0:n]), PB[Gb(h)][0:n, 256:256 + n], [], [PT[Gb(h)], TD[bb]])
                    yield
                for h in H:
                    d_, TD = D_(h), TD_(h)
                    MM(Y(h, slice(0, n), n), R(d_[b_][0:n, 0:n]), R(d_[pa][0:n, 0:n]), True, True, [TD[b_], TD[pa]], [PT[Wb(h)]])
                yield
                for h in H:
                    d_, TD = D_(h), TD_(h)
                    TT("dve", RO(d_[pb2][0:n, 0:n]), Y(h, slice(0, n), n), d_[pa][0:n, 0:n], ALU.add, [TD[pa]], [PT[Wb(h)], TD[pb2]])
                yield
            pf = "P%d" % (nk % 2)
            if CH_BF:
                PBF = lambda h: D_(h)[pf]
                TPBF = lambda h: TD_(h)[pf]
            else:
                PBF = lambda h: D_(h)["Pbf"]
                TPBF = lambda h: TD_(h)["Pbf"]
                for h in H:
                    COPY("act", D_(h)["Pbf"][0:n, 0:n], D_(h)[pf][0:n, 0:n], [TD_(h)[pf]], [TD_(h)["Pbf"]])
                yield
            for h in H:
                TR(pbf(Wb(h))[0:n, 0:128], kT(h), ident_b[:, :], rk, [PT[Wb(h)]])
                TR(pbf(Wb(h))[0:n, 128:256], vT(h), ident_b[:, :], rv, [PT[Wb(h)]])
            yield
            for h in H:
                ACT(D_(h)["kbg"][0:n, :], pbf(Wb(h))[0:n, 0:128], AF.Copy, [Tsc], [PT[Wb(h)], TD_(h)["kbg"]], scale=sc[0:n, 28 + h:29 + h])
            yield
            for h in H:
                TS("dve", HB(h)["kd"][0:n, :], pbf(Wb(h))[0:n, 0:128], sc[0:n, 24 + h:25 + h], ALU.mult, reads=[Tsc], writes=[PT[Wb(h)], HT(h)["kd"]])
            yield
            for h in H:
                ACT(D_(h)["vb"][0:n, :], pbf(Wb(h))[0:n, 128:256], AF.Copy, [Tsc], [PT[Wb(h)], TD_(h)["vb"]], scale=sc[0:n, 0 + h:1 + h])
            yield
            for h in H:
                TT("dve", HB(h)["qg"][:, 0:n], qT(h), D_(h)["eGbc"][:, 0:n], ALU.mult, rq + [TD_(h)["eGbc"]], [HT(h)["qg"]])
            yield
            for h in H:
                d_, TD = D_(h), TD_(h)
                MM(Y(h, slice(0, n), 128), PBF(h)[0:n, 0:n], d_["vb"][0:n, :], True, True, [TPBF(h), TD["vb"]], [PT[Wb(h)]])
                MM(X(h, slice(0, 128), n), d_["kbg"][0:n, :], PBF(h)[0:n, 0:n], True, True, [TPBF(h), TD["kbg"]], [PT[Wb(h)]])
            yield
            for h in H:
                COPY("act", HB(h)["u"][0:n, :], Y(h, slice(0, n), 128), [], [PT[Wb(h)], HT(h)["u"]])
            yield
            for h in H:
                COPY("dve", HB(h)["wT"][:, 0:n], X(h, slice(0, 128), n), [], [PT[Wb(h)], HT(h)["wT"]])
            yield
            for h in H:
                MM(Y(h, slice(0, n), n), kT(h), qT(h), True, True, rk + rq, [PT[Wb(h)]])
            yield
            for h in H:
                TT("dve", HB(h)["atm"][0:n, 0:n], Y(h, slice(0, n), n), D_(h)["Ds"][0:n, 0:n], ALU.mult, [TD_(h)["Ds"]], [PT[Wb(h)], HT(h)["atm"]])
            yield

        def delta_rec(ti):
            c0, n = TILES[ti]
            pb_ = ti % 2
            cs = slice(c0, c0 + n)
            chunks = [(0, min(n, 64))] + ([(64, 64)] if n == 128 else [])
            H = range(4)
            D_ = lambda h: hd[h]
            TD_ = lambda h: hd[h]["T"]
            HB = lambda h: hd[h]["hb"][pb_]
            HT = lambda h: hd[h]["hT"][pb_]
            Wb = lambda h: 2 * h
            Gb = lambda h: 2 * h + 1
            for ci, (r0, m) in enumerate(chunks):
                rs = slice(r0, r0 + m)
                for h in H:
                    d_, TD = D_(h), TD_(h)
                    MM(PB[Wb(h)][rs, 256:384], HB(h)["wT"][:, rs], d_["Sbf"][:, :], True, True, [HT(h)["wT"], TD["Sbf"]], [PT[Wb(h)]])
                yield
                for h in H:
                    d_, TD = D_(h), TD_(h)
                    TT("dve", d_["vn"][rs, :], HB(h)["u"][rs, :], PB[Wb(h)][rs, 256:384], ALU.subtract, [HT(h)["u"]], [PT[Wb(h)], TD["vn"]])
                yield
                for h in H:
                    d_, TD = D_(h), TD_(h)
                    MM(PB[Wb(h)][:, 256:384], HB(h)["kd"][rs, :], d_["vn"][rs, :], True, True, [HT(h)["kd"], TD["vn"]], [PT[Wb(h)]])
                    MM(PB[Gb(h)][rs, 128:256], HB(h)["qg"][:, rs], d_["Sbf"][:, :], True, False, [HT(h)["qg"], TD["Sbf"]], [PT[Gb(h)]])
                    MM(PB[Gb(h)][rs, 128:256], HB(h)["atm"][rs, rs], d_["vn"][rs, :], False, True, [HT(h)["atm"], TD["vn"]], [PT[Gb(h)]])
                yield
                for h in H:
                    d_, TD = D_(h), TD_(h)
                    STT(d_["S"][:, :], d_["S"][:, :], HB(h)["col"][:, ci:ci + 1], PB[Wb(h)][:, 256:384], ALU.mult, ALU.add,
                        [HT(h)["col"]], [PT[Wb(h)], TD["S"]])
                yield
                for h in H:
                    d_, TD = D_(h), TD_(h)
                    COPY("pool", d_["Sbf"][:, :], d_["S"][:, :], [TD["S"]], [TD["Sbf"]])
                yield
            for h in H:
                ACT(ojk[0:n, :], PB[Gb(h)][0:n, 128:256], AF.Square, [], [PT[Gb(h)], Tojk, Tsc2], accum_out=sc2[0:n, h:h + 1])
            ACT(sc2[0:n, 12:13], eps_col[0:n, 0:1], AF.Copy, [], [Tsc2])
            yield
            ACT(sc2[0:n, 4:8], sc2[0:n, 0:4], AF.Ln, [], [Tsc2], scale=1.0 / 128, bias=eps_col[0:n, 0:1])
            ACT(sc2[0:n, 8:12], sc2[0:n, 4:8], AF.Exp, [], [Tsc2], scale=-0.5)
            yield
            for h in H:
                STT(D_(h)["on"][0:n, :], PB[Gb(h)][0:n, 128:256], sc2[0:n, 8 + h:9 + h], onA_bc[0:n, :], ALU.mult, ALU.mult,
                    [Tsc2], [PT[Gb(h)], TD_(h)["on"]])
            yield
            for h in H:
                TR(pbf(Wb(h))[:, 768:768 + n], D_(h)["on"][0:n, :], ident_b[0:n, 0:n], [TD_(h)["on"]], [PT[Wb(h)]])
            yield
            for h in H:
                COPY("act" if h % 2 else "dve", qkvT[:, h, cs], pbf(Wb(h))[:, 768:768 + n], [], [PT[Wb(h)], Tq[0][ti]])
            yield

        def interleave(gens, weights=None):
            if weights is None:
                weights = [1] * len(gens)
            gens = [(g, w) for g, w in zip(gens, weights) if g is not None]
            while gens:
                nxt = []
                for g, w in gens:
                    alive = True
                    for _ in range(w):
                        try:
                            next(g)
                        except StopIteration:
                            alive = False
                            break
                    if alive:
                        nxt.append((g, w))
                gens = nxt

        def dec_delta_prep():
            ti = 17
            COPY("dve", sc[0:16, 0:12], sca[0:16, 17, 0:12], [Tsca], [Tsc])
            ACT(sc[0:16, 20:24], sc[0:16, 8:12], AF.Exp, [], [Tsc])
            TT("dve", sc[0:16, 32:36], sc[0:16, 20:24], sc[0:16, 4:8], ALU.mult, [], [Tsc])
            for h in range(4):
                TS("dve", dX[0:16, h * 16:(h + 1) * 16], ident_f[0:16, 0:16], sc[0:16, 20 + h:21 + h], ALU.mult, reads=[Tsc, Tc], writes=[TdX])
                TS("dve", dX[0:16, 64 + h * 16:64 + (h + 1) * 16], ident_f[0:16, 0:16], sc[0:16, 32 + h:33 + h], ALU.mult, reads=[Tsc, Tc], writes=[TdX])
                TS("dve", dX[0:16, 128 + h * 16:128 + (h + 1) * 16], ident_f[0:16, 0:16], sc[0:16, 0 + h:1 + h], ALU.mult, reads=[Tsc, Tc], writes=[TdX])
            MM(PB[7][:, 0:192], ones_f[0:16, :], dX[0:16, :], True, True, [TdX, Tc], [PT[7]])
            COPY("dve", dbc[:, :], PB[7][:, 0:192], [], [PT[7], Tdbc])
            TT("dve", dvb[:, :, :], decq[:, 8:12, :], dbc[:, 128:192].rearrange("p (h b) -> p h b", h=4), ALU.mult, [Tdecq, Tdbc], [Tdvb])

        def dec_delta(b):
            s = b % 2
            DMA(Sd[s][:, :, :], I["sd"][b].rearrange("h k v -> k h v"), writes=[TSd[s]], own=TSd[s])
            yield
            for h in range(4):
                W, G_ = 2 * h, 2 * h + 1
                hb = h * 16 + b
                MM(PB[W][:, 448:449], Sd[s][:, h, :], decq[:, 4 + h, b:b + 1], True, True, [TSd[s], Tdecq], [PT[W]])
                STT(derr[:, h:h + 1], PB[W][:, 448:449], dbc[:, 64 + hb:65 + hb], dvb[:, h, b:b + 1], ALU.mult, ALU.add,
                    [Tdbc, Tdvb], [PT[W], Tderr])
                yield
                TS("pool", ddiag[:, :].bitcast(F32R), ident_f[:, :], derr[:, h:h + 1], ALU.mult, 1.0, ALU.mult, reads=[Tderr], writes=[Tddiag])
                yield
                MM(PB[G_][:, 384:512], ones_r[:, :], ddiag[:, :].bitcast(F32R), True, True, [Tddiag], [PT[G_]])
                ACT(dtmp[:, :], PB[G_][:, 384:512], AF.Copy, [Tdecq], [PT[G_], Tdtmp], scale=decq[:, 4 + h, b:b + 1])
                yield
                STT(Sn[s][:, h, :], Sd[s][:, h, :], dbc[:, hb:hb + 1], dtmp[:, :], ALU.mult, ALU.add, [TSd[s], Tdbc, Tdtmp], [TSn[s]])
                yield
                MM(PB[W][:, 449:450], Sn[s][:, h, :], decq[:, 0 + h, b:b + 1], True, True, [TSn[s], Tdecq], [PT[W]])
                COPY("act", odec[:, hb:hb + 1], PB[W][:, 449:450], [], [PT[W], Todec])
                yield
            DMA(O["nd_s"][b].rearrange("h k v -> k h v"), Sn[s][:, :, :], reads=[TSn[s]], own=TSn[s])
            yield

        def dec_delta_finish():
            TT("dve", dtmp[:, 0:64], odec[:, :], odec[:, :], ALU.mult, [Todec], [Tdtmp])
            MM(PB[7][:, 0:64], ones_f[:, :], dtmp[:, 0:64], True, True, [Tdtmp, Tc], [PT[7]])
            ACT(dtmp[:, 64:128], PB[7][:, 0:64], AF.Ln, [], [PT[7], Tdtmp], scale=1.0 / 128, bias=eps_col[:, 0:1])
            ACT(dtmp[:, 64:128], dtmp[:, 64:128], AF.Exp, [], [Tdtmp], scale=-0.5)
            TT("dve", dtmp[:, 0:64], odec[:, :], dtmp[:, 64:128], ALU.mult, [Todec], [Tdtmp])
            TS("dve", qkvT[:, 0:4, DEC0:NT], dtmp[:, 0:64].rearrange("p (h b) -> p h b", h=4), oncol[:, 0:1], ALU.mult,
               reads=[Tdtmp, Tvec], writes=[Tq[0][17]])

        with nc.allow_non_contiguous_dma(reason="state tiles 512B rows"):
            print("A2 sbuf top", sb.top, "free", SB_END - sb.top)
            all_scalars()
            dec_delta_prep()
            interleave([delta_prep(0, (0, 1)), delta_prep(0, (2, 3))])
            for ti in range(17):
                interleave([delta_rec(ti),
                            delta_prep(ti + 1, (0, 1)) if ti + 1 < 17 else None,
                            delta_prep(ti + 1, (2, 3)) if ti + 1 < 17 else None,
                            dec_delta(ti - 1) if ti >= 1 else None], [2, 2, 2, 1])
            dec_delta_finish()
            for h in range(4):
                DMA(O["nd_p"][h], hd[h]["S"][:, :], reads=[hd[h]["T"]["S"]], own=hd[h]["T"]["S"])
        em.barrier()
        sb.release(mA2)
        if stage <= 2:
            em.finish()
            print("sbuf peak", sb.peak, "ops", em.nops, "dsems", len(em.dsems))
            return nc
        sb.release(mA)

        obT = sb("obT", [128, 8, NT], BF16)
        Tqk = [T("qkb%d" % i) for i in range(18)]
        mB = sb.mark()
        vtok = sb("vtok", [128, 18, 1024], BF16)
        Tvt = [T("vtok%d" % i) for i in range(18)]
        lrbT = sb("lrbT", [17, NT], F32)
        Tlrb = T("lrbT")
        MEMSET("pool", lrbT[:, :], 1.0, [Tlrb])
        wgk = sb("wgk", [17, 512], F32)
        Twgk = T("wgk")
        DMA(wgk[0:16, :], I["w_gk2"][:, :], writes=[Twgk], own=Twgk)
        DMA(wgk[16:17, :], I["b_gk"][0:1, :], writes=[Twgk], own=Twgk)
        decqb = sb("decqb", [128, 8, 16], F32)
        Tdecqb = T("decqb")
        mB1 = sb.mark()
        alloc_wsl()

        def job_lrb(slot):
            def epi(ps, c0, n, bk):
                COPY("act", lrbT[0:16, c0:c0 + n], ps, [], [PT[bk], Tlrb])
            fm_proj(slot, 0, 16, hT, ThT, 8, SLABS, epi, banks=(0, 1, 2, 3, 6, 7))

        def job_qkb(g):
            def fn(slot):
                for j in range(4):
                    cc = g * 4 + j

                    def epi(ps, c0, n, bk, cc=cc):
                        if c0 <= DEC0 < c0 + n:
                            psd = ps[:, DEC0 - c0:NT - c0]
                            if g == 0:
                                ACT(decqb[:, cc, :], psd, AF.Copy, [], [PT[bk], Tdecqb], scale=float(128.0 ** -0.5))
                            else:
                                COPY("act", decqb[:, cc, :], psd, [], [PT[bk], Tdecqb])
                        COPY("act" if (cc + c0 // 416) % 2 else "dve", obT[:, cc, c0:c0 + n], ps, [], [PT[bk]] + [Tqk[ti] for ti in tiles_in(c0, n)])
                    fm_proj(slot, j, 128, hT, ThT, 8, SLABS, epi, banks=(0, 1, 2, 3, 6, 7))
            return (w_in_blk(QB + g * 512, 512), 8, 512, fn)

        def job_vb(blk):
            def fn(slot):
                for ti in range(18):
                    c0, n = TILES[ti]
                    bk = 4 + ti % 2
                    for kc in range(8):
                        MM(PB[bk][0:n, 0:512], hT[:, kc, c0:c0 + n], wsl[slot][:, kc, 0:512], kc == 0, kc == 7, [Tw[slot], ThT[ti]], [PT[bk]])
                    COPY("act" if ti % 2 else "dve", vtok[0:n, ti, blk * 512:(blk + 1) * 512], PB[bk][0:n, 0:512], [], [PT[bk], Tvt[ti]])
            return (w_in_blk(VB + blk * 512, 512), 8, 512, fn)

        run_jobs([(w_in_blk(LRB, 16), 8, 16, job_lrb), job_qkb(0), job_qkb(1), job_vb(0), job_vb(1)])
        em.barrier()
        sb.release(mB1)
        if stage <= 3:
            em.finish()
            print("sbuf peak", sb.peak, "ops", em.nops, "dsems", len(em.dsems))
            return nc

        mB2 = sb.mark()
        sel_b = sb("sel_b", [16, 2048], BF16)
        Tsel = T("sel")
        mk = sb.mark()
        sel_f = sb("sel_f", [16, 2048], F32)
        DMA(sel_f[:, :], I["c_sel"][:, :], writes=[Tsel], own=Tsel)
        COPY("dve", sel_b[:, :], sel_f[:, :], [], [Tsel])
        em.barrier()
        sb.release(mk)
        ltok = sb("ltok", [128, 512], F32)
        Tlt = T("ltok")
        eE = sb("eE", [128, 512], F32)
        TeE = T("eE")
        khat_l = [sb("khat%d" % i, [128, 512], BF16) for i in range(2)]
        Tkh_l = [T("khat%d" % i) for i in range(2)]
        gsc = sb("gsc", [128, 16], F32)
        Tgsc = T("gsc")
        ojk2 = sb("ojk2", [128, 256], BF16)
        Tojk2 = T("ojk2")
        gh = []
        for h in range(4):
            d_ = {}
            d_["eB"] = sb("eB%d" % h, [128, 128], F32)
            d_["eNB"] = sb("eNB%d" % h, [128, 128], F32)
            for nm in ["qt", "kt", "atm"]:
                d_[nm] = sb("g%s%d" % (nm, h), [128, 128], BF16)
            d_["S"] = sb("gS%d" % h, [128, 256], F32)
            d_["Sbf"] = sb("gSbf%d" % h, [128, 256], BF16)
            d_["on"] = sb("gon%d" % h, [128, 256], BF16)
            d_["col"] = sb("gcol%d" % h, [128, 4], F32)
            d_["T"] = {nm: T("g%s%d" % (nm, h)) for nm in list(d_.keys())}
            d_["hb"] = [dict(qt=d_["qt"], atm=d_["atm"], col=d_["col"]),
                        dict(qt=sb("gqt%db" % h, [128, 128], BF16), atm=sb("gatm%db" % h, [128, 128], BF16), col=sb("gcol%db" % h, [128, 4], F32))]
            d_["hT"] = [{k: d_["T"][k] for k in d_["hb"][0]}, {k: T("g%s%db" % (k, h)) for k in d_["hb"][1]}]
            gh.append(d_)
            MEMSET("pool", d_["S"][:, :], 0.0, [d_["T"]["S"]])
            MEMSET("pool", d_["Sbf"][:, :], 0.0, [d_["T"]["Sbf"]])
        gdec = sb("gdec", [128, 4, 16], F32)
        Tgdec = T("gdec")
        Sg = [sb("Sg%d" % i, [128, 256], F32) for i in range(4)]
        TSg = [T("Sg%d" % i) for i in range(4)]
        Sgn = [sb("Sgn%d" % i, [128, 256], F32) for i in range(2)]
        TSgn = [T("Sgn%d" % i) for i in range(2)]
        gtmp = sb("gtmp", [128, 256], F32)
        Tgtmp = T("gtmp")
        odecb = sb("odecb", [128, 8, 16], F32)
        Todecb = T("odecb")

        def gla_prep(ti):
            c0, n = TILES[ti]
            cs = slice(c0, c0 + n)
            pb_ = ti % 2
            khat, Tkh = khat_l[pb_], Tkh_l[pb_]
            H = range(4)
            MM(PB[0][0:n, 0:512], lrbT[0:17, cs], wgk[0:17, :], True, True, [Tlrb, Twgk], [PT[0]])
            yield
            LW = (lambda ap: ap.bitcast(F32R)) if KR_GLA else (lambda ap: ap)
            LR = (lambda ap: ap.bitcast(F32R)) if (KR_GLA and n == 128) else (lambda ap: ap)
            ACT(LW(ltok[0:n, :]), PB[0][0:n, 0:512], AF.Exp, [], [PT[0], Tlt], scale=-1.0)
            yield
            ACT(LW(ltok[0:n, :]), ltok[0:n, :], AF.Ln, [], [Tlt], bias=eps_col[0:n, 2:3])
            yield
            MM(PB[0][0:n, 0:512], (trisuf_r if (KR_GLA and n == 128) else cn["c_trisuf"])[0:n, 0:n], LR(ltok[0:n, :]), True, True, [Tlt], [PT[0]])
            for h in H:
                TR(pbf(1)[0:n, h * 128:(h + 1) * 128], obT[:, 4 + h, cs], ident_b[:, :], [Tqk[ti]], [PT[1]])
            yield
            ACT(eE[0:n, :], PB[0][0:n, 0:512], AF.Exp, [], [PT[0], TeE])
            yield
            TT("dve", khat[0:n, :], pbf(1)[0:n, 0:512], eE[0:n, :], ALU.mult, [TeE], [PT[1], Tkh])
            for h in H:
                MM(PB[2 + h][:, 0:n], LR(ltok[0:n, h * 128:(h + 1) * 128]), (tris_r if (KR_GLA and n == 128) else cn["c_tris"])[0:n, 0:n], True, True, [Tlt], [PT[2 + h]])
            yield
            for h in H:
                d_, TD, bk = gh[h], gh[h]["T"], 2 + h
                ACT(d_["eB"][:, 0:n], PB[bk][:, 0:n], AF.Exp, [], [PT[bk], TD["eB"]], bias=eps_col[:, 1:2])
                ACT(d_["eNB"][:, 0:n], PB[bk][:, 0:n], AF.Exp, [], [PT[bk], TD["eNB"]], scale=-1.0)
                ACT(d_["hb"][pb_]["col"][:, 0:1], PB[bk][:, n - 1:n], AF.Exp, [], [PT[bk], d_["hT"][pb_]["col"]])
                yield
            for h in H:
                d_, TD = gh[h], gh[h]["T"]
                TT("dve", d_["hb"][pb_]["qt"][:, 0:n], obT[:, h, cs], d_["eB"][:, 0:n], ALU.mult, [Tqk[ti], TD["eB"]], [d_["hT"][pb_]["qt"]])
                TT("dve", d_["kt"][:, 0:n], obT[:, 4 + h, cs], d_["eNB"][:, 0:n], ALU.mult, [Tqk[ti], TD["eNB"]], [TD["kt"]])
            yield
            for h in H:
                d_, TD, bk = gh[h], gh[h]["T"], 2 + h
                MM(PB[bk][0:n, 128:128 + n], d_["kt"][:, 0:n], d_["hb"][pb_]["qt"][:, 0:n], True, True, [TD["kt"], d_["hT"][pb_]["qt"]], [PT[bk]])
            yield
            for h in H:
                d_, TD, bk = gh[h], gh[h]["T"], 2 + h
                TT("dve", d_["hb"][pb_]["atm"][0:n, 0:n], PB[bk][0:n, 128:128 + n], cn["c_m01"][0:n, 0:n], ALU.mult, [], [PT[bk], d_["hT"][pb_]["atm"]])
            yield

        def gla_rec(ti):
            c0, n = TILES[ti]
            cs = slice(c0, c0 + n)
            pb_ = ti % 2
            khat, Tkh = khat_l[pb_], Tkh_l[pb_]
            H = range(4)
            for h in H:
                d_, TD, bk = gh[h], gh[h]["T"], 2 + h
                hb_, hT_ = d_["hb"][pb_], d_["hT"][pb_]
                sbk = 6 + h % 2
                vt = vtok[0:n, ti, h * 256:(h + 1) * 256]
                MM(PB[sbk][:, (h // 2) * 256:(h // 2) * 256 + 256], khat[0:n, h * 128:(h + 1) * 128], vt, True, True, [Tkh, Tvt[ti]], [PT[sbk]])
                MM(PB[bk][0:n, 256:512], hb_["qt"][:, 0:n], d_["Sbf"][:, :], True, False, [hT_["qt"], TD["Sbf"]], [PT[bk]])
                MM(PB[bk][0:n, 256:512], hb_["atm"][0:n, 0:n], vt, False, True, [hT_["atm"], Tvt[ti]], [PT[bk]])
            yield
            for h in H:
                d_, TD = gh[h], gh[h]["T"]
                sbk = 6 + h % 2
                STT(d_["S"][:, :], d_["S"][:, :], d_["hb"][pb_]["col"][:, 0:1], PB[sbk][:, (h // 2) * 256:(h // 2) * 256 + 256], ALU.mult, ALU.add,
                    [d_["hT"][pb_]["col"]], [PT[sbk], TD["S"]])
            yield
            for h in H:
                d_, TD = gh[h], gh[h]["T"]
                COPY("pool", d_["Sbf"][:, :], d_["S"][:, :], [TD["S"]], [TD["Sbf"]])
            yield
            for h in H:
                bk = 2 + h
                ACT(ojk2[0:n, :], PB[bk][0:n, 256:512], AF.Square, [], [PT[bk], Tojk2, Tgsc], accum_out=gsc[0:n, h:h + 1])
            ACT(gsc[0:n, 12:13], eps_col[0:n, 0:1], AF.Copy, [], [Tgsc])
            yield
            ACT(gsc[0:n, 4:8], gsc[0:n, 0:4], AF.Ln, [], [Tgsc], scale=1.0 / 256, bias=eps_col[0:n, 0:1])
            ACT(gsc[0:n, 8:12], gsc[0:n, 4:8], AF.Exp, [], [Tgsc], scale=-0.5)
            yield
            for h in H:
                d_, TD, bk = gh[h], gh[h]["T"], 2 + h
                STT(d_["on"][0:n, :], PB[bk][0:n, 256:512], gsc[0:n, 8 + h:9 + h], onB_bc[0:n, :], ALU.mult, ALU.mult,
                    [Tgsc], [PT[bk], TD["on"]])
            yield
            for h in H:
                d_, TD = gh[h], gh[h]["T"]
                sbk = 6 + h % 2
                o0 = (h // 2) * 512
                for half in range(2):
                    TR(pbf(sbk)[:, o0 + half * 128:o0 + half * 128 + n], d_["on"][0:n, half * 128:(half + 1) * 128], ident_b[0:n, 0:n],
                       [TD["on"]], [PT[sbk]])
            yield
            for h in H:
                sbk = 6 + h % 2
                o0 = (h // 2) * 512
                COPY("act" if h % 2 else "dve", obT[:, 2 * h:2 * h + 2, cs],
                     pbf(sbk)[:, o0:o0 + 256].rearrange("p (a c) -> p a c", a=2)[:, :, 0:n], [], [PT[sbk], Tqk[ti]])
            yield

        def dec_gla_prep():
            for h in range(4):
                MM(PB[0][:, h * 16:(h + 1) * 16], wgk[0:17, h * 128:(h + 1) * 128], lrbT[0:17, DEC0:NT], True, True, [Tlrb, Twgk], [PT[0]])
            ACT(gdec[:, :, :].rearrange("p a b -> p (a b)"), PB[0][:, 0:64], AF.Exp, [], [PT[0], Tgdec], scale=-1.0)
            ACT(gdec[:, :, :].rearrange("p a b -> p (a b)"), gdec[:, :, :].rearrange("p a b -> p (a b)"), AF.Ln, [], [Tgdec], bias=eps_col[:, 2:3])
            ACT(gdec[:, :, :].rearrange("p a b -> p (a b)"), gdec[:, :, :].rearrange("p a b -> p (a b)"), AF.Exp, [], [Tgdec], scale=-1.0 / 16.0)

        dgc = [0]

        def dec_gla(b):
            for h in range(4):
                DMA(Sg[h][:, :], I["sg"][b, h], writes=[TSg[h]], own=TSg[h])
            yield
            for h in range(4):
                s = dgc[0] % 2
                dgc[0] += 1
                bk = 1
                MM(PB[bk][:, 256:512], sel_b[0:16, b * 128:(b + 1) * 128], vtok[0:16, 17, h * 256:(h + 1) * 256], True, True, [Tsel, Tvt[17]], [PT[bk]])
                ACT(gtmp[:, :], PB[bk][:, 256:512], AF.Copy, [Tdecqb], [PT[bk], Tgtmp], scale=decqb[:, 4 + h, b:b + 1])
                yield
                STT(Sgn[s][:, :], Sg[h][:, :], gdec[:, h, b:b + 1], gtmp[:, :], ALU.mult, ALU.add, [TSg[h], Tgdec, Tgtmp], [TSgn[s]])
                yield
                for half in range(2):
                    MM(PB[bk][:, 256 + half:257 + half], Sgn[s][:, half * 128:(half + 1) * 128], decqb[:, h, b:b + 1], True, True, [TSgn[s], Tdecqb], [PT[bk]])
                COPY("act", odecb[:, 2 * h:2 * h + 2, b], PB[bk][:, 256:258], [], [PT[bk], Todecb])
                DMA(O["ng_s"][b, h], Sgn[s][:, :], reads=[TSgn[s]], own=TSgn[s])
                yield

        def dec_gla_finish():
            of = odecb[:, :, :].rearrange("p a b -> p (a b)")
            TT("dve", gtmp[:, 0:128], of, of, ALU.mult, [Todecb], [Tgtmp])
            for h in range(4):
                for half in range(2):
                    cidx = (2 * h + half) * 16
                    MM(PB[0][:, h * 16:(h + 1) * 16], ones_f[:, :], gtmp[:, cidx:cidx + 16], half == 0, half == 1, [Tgtmp, Tc], [PT[0]])
            ACT(gtmp[:, 128:192], PB[0][:, 0:64], AF.Ln, [], [PT[0], Tgtmp], scale=1.0 / 256, bias=eps_col[:, 0:1])
            ACT(gtmp[:, 128:192], gtmp[:, 128:192], AF.Exp, [], [Tgtmp], scale=-0.5)
            for h in range(4):
                for half in range(2):
                    STT(obT[:, 2 * h + half, DEC0:NT], odecb[:, 2 * h + half, :], oncol[:, 1 + half:2 + half], gtmp[:, 128 + h * 16:128 + (h + 1) * 16],
                        ALU.mult, ALU.mult, [Todecb, Tvec, Tgtmp], [Tqk[17]])

        print("B2 sbuf top", sb.top, "free", SB_END - sb.top)
        dec_gla_prep()
        interleave([gla_prep(0)])
        for ti in range(17):
            interleave([gla_rec(ti), gla_prep(ti + 1) if ti + 1 < 17 else None, dec_gla(ti - 1) if ti >= 1 else None])
        dec_gla_finish()
        for h in range(4):
            DMA(O["ng_p"][h], gh[h]["S"][:, :], reads=[gh[h]["T"]["S"]], own=gh[h]["T"]["S"])
        em.barrier()
        sb.release(mB)
        if stage <= 4:
            em.finish()
            print("sbuf peak", sb.peak, "ops", em.nops, "dsems", len(em.dsems))
            return nc

        mG = sb.mark()
        alloc_wsl()
        gs = [sb("gs%d" % i, [128, 512], BF16) for i in range(2)]
        Tgs = [T("gs%d" % i) for i in range(2)]
        gsc_ = [0]

        def job_gate(c0w, dst, dstT, base):
            def fn(slot):
                for j in range(4):
                    def epi(ps, c0, n, bk, j=j):
                        r = gsc_[0] % 2
                        gsc_[0] += 1
                        ACT(gs[r][:, 0:n], ps, AF.Silu, [], [PT[bk], Tgs[r]])
                        TT("dve", dst[:, base + j, c0:c0 + n], dst[:, base + j, c0:c0 + n], gs[r][:, 0:n], ALU.mult, [Tgs[r]],
                           [dstT[ti] for ti in tiles_in(c0, n)])
                    fm_proj(slot, j, 128, hT, ThT, 8, SLABS, epi, banks=(0, 1, 2, 3, 4, 5, 6, 7))
            return (w_in_blk(c0w, 512), 8, 512, fn)

        run_jobs([job_gate(ZA, oaT, Tq[0], 0), job_gate(RB, obT, Tqk, 0), job_gate(RB + 512, obT, Tqk, 4)])
        em.barrier()
        sb.release(mG)

        mixT_off = sb.top
        mixT = sb("mixT", [128, 8, NT], BF16)
        Tmix = [T("mix%d" % i) for i in range(18)]
        mixT_end = sb.top
        mC1 = sb.mark()
        wa = sb("wa", [128, 4, 1024], BF16)
        wb = sb("wb", [128, 8, 1024], BF16)
        wga = sb("wga", [128, 8, 1024], BF16)
        wgb = sb("wgb", [128, 8, 1024], BF16)
        Twc = [T("wc%d" % i) for i in range(4)]
        em.dma("pool", wa[:, :, :], I["w_a_out"].rearrange("(k p) n -> p k n", p=128), (), [Twc[0]], Twc[0])
        em.dma("pool", wga[:, :, :], w_in_blk(GA, 1024), (), [Twc[2]], Twc[2])
        em.dma("pool", wb[:, :, :], I["w_b_out"].rearrange("(k p) n -> p k n", p=128), (), [Twc[1]], Twc[1])
        em.dma("pool", wgb[:, :, :], w_in_blk(GB, 1024), (), [Twc[3]], Twc[3])
        sg_ = [[sb("sg%d_%d" % (i, k), [128, 512], F32) for k in range(3)] for i in range(2)]
        Tsg = [[T("sg%d_%d" % (i, k)) for k in range(3)] for i in range(2)]
        cct = 0
        for oc in range(8):
            for (c0, n) in SLABS:
                st_i = cct % 2
                cct += 1
                b0 = 4 * st_i
                tl = tiles_in(c0, n)
                ocs = slice(oc * 128, (oc + 1) * 128)
                for kc in range(4):
                    MM(PB[b0][:, 0:n], wa[:, kc, ocs], oaT[:, kc, c0:c0 + n], kc == 0, kc == 3, [Twc[0]] + [Tq[0][ti] for ti in tl], [PT[b0]])
                for kc in range(8):
                    MM(PB[b0 + 1][:, 0:n], wga[:, kc, ocs], hT[:, kc, c0:c0 + n], kc == 0, kc == 7, [Twc[2]] + [ThT[ti] for ti in tl], [PT[b0 + 1]])
                for kc in range(8):
                    MM(PB[b0 + 2][:, 0:n], wb[:, kc, ocs], obT[:, kc, c0:c0 + n], kc == 0, kc == 7, [Twc[1]] + [Tqk[ti] for ti in tl], [PT[b0 + 2]])
                for kc in range(8):
                    MM(PB[b0 + 3][:, 0:n], wgb[:, kc, ocs], hT[:, kc, c0:c0 + n], kc == 0, kc == 7, [Twc[3]] + [ThT[ti] for ti in tl], [PT[b0 + 3]])
                sA, sB, m1 = sg_[st_i]
                TA, TB, TM = Tsg[st_i]
                ACT(sA[:, 0:n], PB[b0 + 1][:, 0:n], AF.Sigmoid, [], [PT[b0 + 1], TA])
                ACT(sB[:, 0:n], PB[b0 + 3][:, 0:n], AF.Sigmoid, [], [PT[b0 + 3], TB])
                TT("dve", m1[:, 0:n], PB[b0][:, 0:n], sA[:, 0:n], ALU.mult, [TA], [PT[b0], TM])
                TT("dve", sB[:, 0:n], PB[b0 + 2][:, 0:n], sB[:, 0:n], ALU.mult, [], [PT[b0 + 2], TB])
                TT("pool", mixT[:, oc, c0:c0 + n], m1[:, 0:n], sB[:, 0:n], ALU.add, [TM, TB], [Tmix[ti] for ti in tl])
        em.barrier()
        sb.release(mC1)
        dump("mixT", mixT[:, :, :], [128, 8, NT], BF16)
        dump("oaT", oaT[:, :, :], [128, 4, NT], BF16)
        dump("obT", obT[:, :, :], [128, 8, NT], BF16)
        if stage <= 5:
            em.finish()
            print("sbuf peak", sb.peak, "ops", em.nops, "dsems", len(em.dsems))
            return nc

        sb.top = hT_off
        x1 = sb("x1", [128, 18, D], F32)
        Tx1 = [T("x1_%d" % i) for i in range(18)]
        x1_end = sb.top
        assert x1_end <= mixT_off
        sb.top = mixT_end
        mC2 = sb.mark()
        wo = sb("wo", [128, 8, 1024], BF16)
        Two = T("wo")
        em.dma("pool", wo[:, :, :], I["w_o"].rearrange("(k p) n -> p k n", p=128), (), [Two], Two)
        for ti in range(18):
            c0, n = TILES[ti]
            DMA(x1[0:n, ti, :], tile_src(ti), writes=[Tx1[ti]], own=Tx1[ti])
        cct = 0
        for ti in range(18):
            c0, n = TILES[ti]
            for half in range(2):
                bk = cct % 8
                cct += 1
                for kc in range(8):
                    MM(PB[bk][0:n, 0:512], mixT[:, kc, c0:c0 + n], wo[:, kc, half * 512:(half + 1) * 512], kc == 0, kc == 7, [Two, Tmix[ti]], [PT[bk]])
                TT("dve", x1[0:n, ti, half * 512:(half + 1) * 512], x1[0:n, ti, half * 512:(half + 1) * 512], PB[bk][0:n, 0:512], ALU.add, [], [PT[bk], Tx1[ti]])
        em.barrier()
        sb.top = x1_end
        dump("x1", x1[:, :, :], [128, 18, D], F32)
        if stage <= 6:
            em.finish()
            print("sbuf peak", sb.peak, "ops", em.nops, "dsems", len(em.dsems))
            return nc

        h2T = sb("h2T", [128, 8, NT], BF16)
        Th2 = [T("h2T%d" % i) for i in range(18)]
        mD0 = sb.mark()
        nf_bc = sb("nf_bc", [128, D], F32)
        Tnf = T("nf")
        DMA(nf_bc[:, :], I["norm_ffn"][0:1, :].partition_broadcast(128), writes=[Tnf], own=Tnf)

        def get_x1(ti):
            c0, n = TILES[ti]
            return x1[0:n, ti, :], [Tx1[ti]]
        norm_tiles(h2T, Th2, get_x1, nf_bc, Tnf, 0)
        em.barrier()
        dump("nf", nf_bc[:, :], [128, D], F32)
        sb.release(mD0)
        dump("h2T", h2T[:, :, :], [128, 8, NT], BF16)

        mD1 = sb.mark()
        NF = 22
        wcfT = sb("wcfT", [128, NF, 4], F32)
        Twcf = T("wcfT")
        prevF = sb("prevF", [128, NF, 2, 16], F32)
        TprevF = T("prevF")
        lastF = sb("lastF", [128, NF, 2], F32)
        TlastF = T("lastF")
        decF = sb("decF", [128, NF, 16], F32)
        TdecF = T("decF")
        mk = sb.mark()
        wcf_sb = sb("wcf_sb", [4, DFF], F32)
        Twcfs = T("wcfs")
        DMA(wcf_sb[0:3, :], I["w_conv_f"][:, :], writes=[Twcfs], own=Twcfs)
        DMA(wcf_sb[3:4, :], I["b_conv_f"][0:1, :], writes=[Twcfs], own=Twcfs)
        sfc_sb = sb("sfc_sb", [16, 2, DFF], F32)
        Tsfc = T("sfc")
        DMA(sfc_sb[:, :, :], I["sfc"][:, :, :], writes=[Tsfc], own=Tsfc)
        DMA(O["nfc_s"][:, 0, :], sfc_sb[:, 1, :], reads=[Tsfc], own=Tsfc)
        for fc in range(NF):
            TR(PB[0][:, fc * 4:fc * 4 + 4], wcf_sb[0:4, fc * 128:(fc + 1) * 128], ident_f[0:4, 0:4], [Twcfs, Tc], [PT[0]])
        COPY("dve", wcfT[:, :, :].rearrange("p a b -> p (a b)"), PB[0][:, 0:NF * 4], [], [PT[0], Twcf])
        for fc in range(NF):
            bk = 1 + fc % 2
            for r in range(2):
                TR(PB[bk][:, r * 16:r * 16 + 16], sfc_sb[0:16, r, fc * 128:(fc + 1) * 128], ident_f[0:16, 0:16], [Tsfc, Tc], [PT[bk]])
            COPY("act", prevF[:, fc, :, :].rearrange("p a b -> p (a b)"), PB[bk][:, 0:32], [], [PT[bk], TprevF])
        em.barrier()
        sb.release(mk)
        mD1b = sb.mark()
        alloc_wsl()
        preF = [sb("preF%d" % i, [128, 2 + NT], F32) for i in range(2)]
        TpreF = [T("preF%d" % i) for i in range(2)]
        for i in range(2):
            MEMSET("pool", preF[i][:, 0:2], 0.0, [TpreF[i]])
        cvf = sb("cvf", [128, NT], F32)
        Tcvf = T("cvf")
        gu_l = [sb("gu%d" % i, [128, NT], BF16) for i in range(2)]
        Tgu_l = [T("gu%d" % i) for i in range(2)]
        GRP = 6
        aT = sb("aT", [128, GRP, NT], BF16)
        TaT = [T("aT%d" % i) for i in range(18)]

        def job_pair(p, g0):
            def fn(slot):
                for jj in range(2):
                    fc = 2 * p + jj
                    ps_ = fc % 2

                    def epi_u(ps, c0, n, bk, ps_=ps_):
                        COPY("act", preF[ps_][:, 2 + c0:2 + c0 + n], ps, [], [PT[bk], TpreF[ps_]])
                    fm_proj(slot, jj, 128, h2T, Th2, 8, SLABS, epi_u, banks=(0, 1, 2, 3, 4, 5, 6, 7))
                for jj in range(2):
                    fc = 2 * p + jj
                    ps_ = fc % 2
                    gu, Tgu = gu_l[ps_], Tgu_l[ps_]
                    pr = preF[ps_]
                    rdp = [TpreF[ps_], Twcf]
                    TS("dve", cvf[:, 0:NPC], pr[:, 0:NPC], wcfT[:, fc, 0:1], ALU.mult, wcfT[:, fc, 3:4], ALU.add, reads=rdp, writes=[Tcvf])
                    for i in range(1, 3):
                        STT(cvf[:, 0:NPC], pr[:, i:i + NPC], wcfT[:, fc, i:i + 1], cvf[:, 0:NPC], ALU.mult, ALU.add, rdp, [Tcvf])
                    TS("dve", cvf[:, DEC0:NT], pr[:, 2 + DEC0:2 + NT], wcfT[:, fc, 2:3], ALU.mult, wcfT[:, fc, 3:4], ALU.add, reads=rdp, writes=[Tcvf])
                    for i in range(2):
                        STT(cvf[:, DEC0:NT], prevF[:, fc, i, :], wcfT[:, fc, i:i + 1], cvf[:, DEC0:NT], ALU.mult, ALU.add, [TprevF, Twcf], [Tcvf])
                    COPY("pool", lastF[:, fc, :], pr[:, 2 + NPC - 2:2 + NPC], [TpreF[ps_]], [TlastF])
                    COPY("pool", decF[:, fc, :], pr[:, 2 + DEC0:2 + NT], [TpreF[ps_]], [TdecF])
                    ACT(gu[:, :], cvf[:, :], AF.Gelu_apprx_tanh, [Tcvf], [Tgu])

                    def epi_g(ps, c0, n, bk, fc=fc, gu=gu, Tgu=Tgu):
                        TT("dve", aT[:, fc - g0, c0:c0 + n], gu[:, c0:c0 + n], ps, ALU.mult, [Tgu], [PT[bk]] + [TaT[ti] for ti in tiles_in(c0, n)])
                    fm_proj(slot, 2 + jj, 128, h2T, Th2, 8, SLABS, epi_g, banks=(0, 1, 2, 3, 4, 5, 6, 7))
            s_u = I["w_ffn_in"][:, 2 * p * 128:2 * p * 128 + 256].rearrange("(k p) n -> p k n", p=128)
            s_g = I["w_ffn_in"][:, DFF + 2 * p * 128:DFF + 2 * p * 128 + 256].rearrange("(k p) n -> p k n", p=128)
            return ((s_u, s_g), 8, 256, fn)

        dct = [0]

        fst = sb("fst", [128, 18, 4], F32)
        Tfst_l = [T("fst%d" % i) for i in range(18)]
        nfin = gu_l[0][:, 0:2 * D].bitcast(F32)
        Tnfin = Tgu_l[0]

        def final_norm(ti):
            c0, n = TILES[ti]
            if ti == 0:
                return
            s_ = ti % 2
            Tfst = Tfst_l[ti]
            xa = x1[0:n, ti, :]
            yb = preF[s_][0:n, 0:D]
            ACT(cvf[0:n, 0:D], xa, AF.Square, [Tx1[ti]], [Tcvf, Tfst], accum_out=fst[0:n, ti, 0:1])
            ACT(fst[0:n, ti, 3:4], eps_col[0:n, 0:1], AF.Copy, [], [Tfst])
            ACT(fst[0:n, ti, 1:2], fst[0:n, ti, 0:1], AF.Ln, [], [Tfst], scale=1.0 / D, bias=eps_col[0:n, 0:1])
            ACT(fst[0:n, ti, 2:3], fst[0:n, ti, 1:2], AF.Exp, [], [Tfst], scale=-0.5)
            STT(yb, xa, fst[0:n, ti, 2:3], nfin[0:n, :], ALU.mult, ALU.mult, [Tx1[ti], Tfst, Tnfin], [TpreF[s_]])
            if ti == 17:
                DMA(O["y_s"][:, :], yb, reads=[TpreF[s_]], own=TpreF[s_])
            else:
                DMA(O["y_p"][128 * (ti - 1):128 * ti, :], yb, reads=[TpreF[s_]], own=TpreF[s_])

        def job_down(g0, gn, half):
            last = (g0 + gn == NF) and half == 1

            def fn(slot):
                if last:
                    DMA(nfin, I["norm_final"][0:1, :].partition_broadcast(128), writes=[Tnfin], own=Tnfin)
                for ti in range(18):
                    c0, n = TILES[ti]
                    bk = dct[0] % 8
                    dct[0] += 1
                    for k in range(gn):
                        MM(PB[bk][0:n, 0:512], aT[:, k, c0:c0 + n], wsl[slot][:, k, 0:512], k == 0, k == gn - 1, [Tw[slot], TaT[ti]], [PT[bk]])
                    TT("dve", x1[0:n, ti, half * 512:(half + 1) * 512], x1[0:n, ti, half * 512:(half + 1) * 512], PB[bk][0:n, 0:512], ALU.add,
                       [], [PT[bk], Tx1[ti]])
                    if last:
                        final_norm(ti)
            src = I["w_ffn_out"][g0 * 128:(g0 + gn) * 128, half * 512:(half + 1) * 512].rearrange("(k p) n -> p k n", p=128)
            return (src, gn, 512, fn)

        print("D1 sbuf top", sb.top, "free", SB_END - sb.top)
        jobs = []
        for g0 in range(0, NF, GRP):
            gn = min(GRP, NF - g0)
            for p in range(g0 // 2, (g0 + gn) // 2):
                jobs.append(job_pair(p, g0))
            jobs.append(job_down(g0, gn, 0))
            jobs.append(job_down(g0, gn, 1))
        run_jobs(jobs)
        em.barrier()
        sb.release(mD1b)

        mk = sb.mark()
        fo = sb("fo", [16, DFF], F32)
        Tfo = T("fo")
        fo2 = sb("fo2", [2, DFF], F32)
        Tfo2 = T("fo2")
        for fc in range(NF):
            bk = 4 + fc % 2
            TR(PB[bk][0:16, 0:128], decF[:, fc, :], ident_f[:, :], [TdecF, Tc], [PT[bk]])
            COPY("dve", fo[0:16, fc * 128:(fc + 1) * 128], PB[bk][0:16, 0:128], [], [PT[bk], Tfo])
            TR(PB[bk][0:2, 128:256], lastF[:, fc, :], ident_f[:, :], [TlastF, Tc], [PT[bk]])
            COPY("act", fo2[0:2, fc * 128:(fc + 1) * 128], PB[bk][0:2, 128:256], [], [PT[bk], Tfo2])
        DMA(O["nfc_s"][:, 1, :], fo[0:16, :], reads=[Tfo], own=Tfo)
        DMA(O["nfc_p"][:, :], fo2[0:2, :], reads=[Tfo2], own=Tfo2)
        em.barrier()
        sb.release(mk)
        sb.release(mD1)

        em.finish()
        print("sbuf peak", sb.peak, "ops", em.nops, "dsems", len(em.dsems))
    return nc


_NC_CACHE = {}


def _core_inputs(inp, c):
    f = lambda a: np.ascontiguousarray(a, dtype=np.float32)
    m = {
        "xp": f(inp["x_prompt"][c]), "meta": f(inp["meta_tokens"]), "xs": f(inp["x_sample"][16 * c:16 * c + 16, 0]),
        "sd": f(inp["state_delta"][0, 16 * c:16 * c + 16]), "sdc": f(inp["state_delta_conv"][0, 16 * c:16 * c + 16]),
        "sg": f(inp["state_gla"][0, 16 * c:16 * c + 16]), "sfc": f(inp["state_ffn_conv"][0, 16 * c:16 * c + 16]),
        "w_in": f(inp["w_in"][0]), "w_conv_a": f(inp["w_conv_a"][0]), "a_log": f(inp["a_log"]), "dt_bias": f(inp["dt_bias"]),
        "w_gk2": f(inp["w_gk2"][0]), "b_gk": f(inp["b_gk"]), "onorm_a": f(inp["onorm_a"]), "onorm_b": f(inp["onorm_b"]),
        "w_a_out": f(inp["w_a_out"][0]), "w_b_out": f(inp["w_b_out"][0]), "w_o": f(inp["w_o"][0]),
        "norm_mix": f(inp["norm_mix"]), "norm_ffn": f(inp["norm_ffn"]), "w_ffn_in": f(inp["w_ffn_in"][0]),
        "w_conv_f": f(inp["w_conv_f"][0]), "b_conv_f": f(inp["b_conv_f"]), "w_ffn_out": f(inp["w_ffn_out"][0]),
        "norm_final": f(inp["norm_final"]).reshape(1, D),
    }
    return m


def kernel(**inp):
    stage = int(os.environ.get("KSTAGE", "99"))
    if stage not in _NC_CACHE:
        _NC_CACHE[stage] = build(stage)
    nc = _NC_CACHE[stage]
    cst = _consts()
    in_maps = []
    for c in range(8):
        m = _core_inputs(inp, c)
        m.update(cst)
        in_maps.append(m)
    res = run_bass_kernel_spmd(nc, in_maps, core_ids=list(range(8)))
    R = res.results
    if os.environ.get("KDBG"):
        for k in R[0]:
            if k.startswith("dbg_"):
                np.save("/tmp/%s.npy" % k, np.stack([np.asarray(R[c][k]).astype(np.float32) for c in range(8)]))
    cat = lambda k: np.stack([R[c][k] for c in range(8)], 0)
    y_p = cat("y_p")
    y_s = np.concatenate([R[c]["y_s"] for c in range(8)], 0)[:, None, :]
    nd_p = cat("nd_p")[None]
    ndc_p = cat("ndc_p")[None]
    ng_p = cat("ng_p")[None]
    nfc_p = cat("nfc_p")[None]
    nd_s = np.concatenate([R[c]["nd_s"] for c in range(8)], 0)[None]
    ndc_s = np.concatenate([R[c]["ndc_s"] for c in range(8)], 0)[None]
    ng_s = np.concatenate([R[c]["ng_s"] for c in range(8)], 0)[None]
    nfc_s = np.concatenate([R[c]["nfc_s"] for c in range(8)], 0)[None]
    return (y_p, y_s, nd_p, ndc_p, ng_p, nfc_p, nd_s, ndc_s, ng_s, nfc_s)
```

```python
import os
import numpy as np
import concourse.bass as bass
import concourse.mybir as mybir
from concourse.bass_utils import run_bass_kernel_spmd
from contextlib import ExitStack

F32 = mybir.dt.float32
BF16 = mybir.dt.bfloat16
AF = mybir.ActivationFunctionType
ALU = mybir.AluOpType

D = 1024
NPC = 2064
NT = 2080
DEC0 = 2064
TILES = [(0, 16)] + [(16 + 128 * i, 128) for i in range(16)] + [(2064, 16)]
SLABS = [(416 * s, 416) for s in range(5)]
QA, KA, VA, ZA, BA, AA, QB, KB, VB, RB, LRB, GA, GB, WEND = 0, 512, 1024, 1536, 2048, 2052, 2056, 2568, 3080, 4104, 5128, 5144, 6168, 7192
DFF = 2816
EPS = 1e-6
SB_BASE = 16640
SB_END = 229376


class T:
    __slots__ = ("name", "w", "r", "dsem")

    def __init__(self, name=""):
        self.name = name
        self.w = None
        self.r = {}
        self.dsem = None


class Em:
    ENG = ("pe", "act", "dve", "pool", "sp")

    def __init__(self, nc, stack):
        self.nc = nc
        self.stack = stack
        self.eng = {"pe": nc.tensor, "act": nc.scalar, "dve": nc.vector, "pool": nc.gpsimd, "sp": nc.sync}
        self.sem = {k: stack.enter_context(nc.semaphore("s_" + k)) for k in self.ENG}
        self.cnt = {k: 0 for k in self.ENG}
        self.known = {k: {} for k in self.ENG}
        self.dsems = []
        self.nops = 0

    def _wait(self, E, deps):
        kn = self.known[E]
        best = {}
        for d in deps:
            if d is None:
                continue
            key, sem, val = d
            if E == "pe" and key == "pe":
                continue
            if kn.get(key, 0) >= val:
                continue
            if key not in best or best[key][2] < val:
                best[key] = d
        for key, (k2, sem, val) in best.items():
            self.eng[E].wait_ge(sem, val)
            kn[key] = val

    def _deps(self, reads, writes):
        deps = []
        for t in reads:
            deps.append(t.w)
        for t in writes:
            deps.append(t.w)
            deps.extend(t.r.values())
        return deps

    def op(self, E, fn, reads=(), writes=()):
        self._wait(E, self._deps(reads, writes))
        inst = fn(self.eng[E])
        self.cnt[E] += 1
        inst.then_inc(self.sem[E], 1)
        tok = (E, self.sem[E], self.cnt[E])
        for t in reads:
            t.r[E] = tok
        for t in writes:
            t.w = tok
            t.r = {}
        self.nops += 1
        return inst

    def _dsem(self, t):
        if t.dsem is None:
            nm = "d%d" % len(self.dsems)
            s = self.stack.enter_context(self.nc.semaphore(nm))
            t.dsem = [s, 0, nm]
            self.dsems.append(t.dsem)
        return t.dsem

    def dma(self, Q, out, in_, reads=(), writes=(), own=None):
        self._wait(Q, self._deps(reads, writes))
        ds = self._dsem(own)
        inst = self.eng[Q].dma_start(out=out, in_=in_)
        ds[1] += 16
        inst.then_inc(ds[0], 16)
        tok = (ds[2], ds[0], ds[1])
        for t in reads:
            t.r[ds[2]] = tok
        for t in writes:
            t.w = tok
            t.r = {}
        self.nops += 1
        return inst

    def barrier(self):
        deps = [(k, self.sem[k], self.cnt[k]) for k in self.ENG if self.cnt[k] > 0]
        deps += [(d[2], d[0], d[1]) for d in self.dsems if d[1] > 0]
        self._wait("sp", deps)
        inst = self.eng["sp"].nop()
        self.cnt["sp"] += 1
        inst.then_inc(self.sem["sp"], 1)
        tok = ("sp", self.sem["sp"], self.cnt["sp"])
        for E in self.ENG:
            if E == "sp":
                continue
            self._wait(E, [tok])
            for d in deps:
                self.known[E][d[0]] = max(self.known[E].get(d[0], 0), d[2])

    def finish(self):
        deps = [(k, self.sem[k], self.cnt[k]) for k in self.ENG if self.cnt[k] > 0 and k != "sp"]
        deps += [(d[2], d[0], d[1]) for d in self.dsems if d[1] > 0]
        self._wait("sp", deps)


class SbAlloc:
    def __init__(self, nc):
        self.nc = nc
        self.top = SB_BASE
        self.n = 0
        self.peak = SB_BASE

    def __call__(self, name, shape, dt):
        es = 2 if dt == BF16 else 4
        nb = es
        for s in shape[1:]:
            nb *= s
        nb = (nb + 63) // 64 * 64
        off = self.top
        assert off + nb <= SB_END, ("SBUF overflow", name, off, nb)
        self.top += nb
        self.peak = max(self.peak, self.top)
        self.n += 1
        return self.nc.alloc_sbuf_tensor_at("%s_%d" % (name, self.n), list(shape), dt, offset=off)

    def mark(self):
        return self.top

    def release(self, m):
        self.top = m


def _consts():
    i = np.arange(128)
    same = (i[:, None] // 64) == (i[None, :] // 64)
    c = {}
    c["c_ident"] = np.eye(128, dtype=np.float32)
    c["c_tribd"] = (same & (i[:, None] <= i[None, :])).astype(np.float32)
    c["c_blk"] = same.astype(np.float32)
    c["c_mbig"] = np.where(same & (i[None, :] < i[:, None]), 0.0, 1e30).astype(np.float32)
    c["c_mneg"] = np.where(same & (i[None, :] >= i[:, None]), 0.0, -1e30).astype(np.float32)
    c["c_tris"] = np.where(i[:, None] <= i[None, :], -1.0 / 16.0, 0.0).astype(np.float32)
    c["c_trisuf"] = np.where(i[:, None] > i[None, :], -1.0 / 16.0, 0.0).astype(np.float32)
    c["c_m01"] = (i[None, :] >= i[:, None]).astype(np.float32)
    sel = np.zeros((16, 16, 128), np.float32)
    for b in range(16):
        sel[b, b, :] = 1.0
    c["c_sel"] = sel.reshape(16, 2048)
    return c


IN_SHAPES = {
    "xp": [2048, D], "meta": [16, D], "xs": [16, D],
    "sd": [16, 4, 128, 128], "sdc": [16, 3, 1536], "sg": [16, 4, 128, 256], "sfc": [16, 2, DFF],
    "w_in": [D, WEND], "w_conv_a": [4, 1536], "a_log": [1, 4], "dt_bias": [1, 4], "w_gk2": [16, 512],
    "b_gk": [1, 512], "onorm_a": [1, 128], "onorm_b": [1, 256], "w_a_out": [512, D], "w_b_out": [D, D],
    "w_o": [D, D], "norm_mix": [1, D], "norm_ffn": [1, D], "w_ffn_in": [D, 2 * DFF], "w_conv_f": [3, DFF],
    "b_conv_f": [1, DFF], "w_ffn_out": [DFF, D], "norm_final": [1, D],
    "c_ident": [128, 128], "c_tribd": [128, 128], "c_blk": [128, 128], "c_mbig": [128, 128], "c_mneg": [128, 128],
    "c_tris": [128, 128], "c_trisuf": [128, 128], "c_m01": [128, 128], "c_sel": [16, 2048],
}
OUT_SHAPES = {
    "y_p": [2048, D], "y_s": [16, D], "nd_p": [4, 128, 128], "ndc_p": [3, 1536], "ng_p": [4, 128, 256],
    "nfc_p": [2, DFF], "nd_s": [16, 4, 128, 128], "ndc_s": [16, 3, 1536], "ng_s": [16, 4, 128, 256], "nfc_s": [16, 2, DFF],
}


def build(stage=99, dbg=None):
    nc = bass.Bass("TRN2", target_bir_lowering=False)
    I = {k: nc.dram_tensor(k, v, F32, kind="ExternalInput").ap() for k, v in IN_SHAPES.items()}
    O = {k: nc.dram_tensor(k, v, F32, kind="ExternalOutput").ap() for k, v in OUT_SHAPES.items()}
    dbg_out = {}
    DBG = os.environ.get("KDBG", "").split(",")

    def dump(name, tensor, shape, dt):
        if name not in DBG:
            return
        o_ = nc.dram_tensor("dbg_" + name, list(shape), dt, kind="ExternalOutput").ap()
        em_[0].barrier()
        t_ = T("dbg" + name)
        em_[0].dma("sp", o_, tensor, (), (), t_)
        em_[0].barrier()
    em_ = [None]
    st = ExitStack()
    with st:
        em = Em(nc, st)
        em_[0] = em
        sb = SbAlloc(nc)
        PB = [st.enter_context(nc.psum_tensor("pb%d" % b, [128, 512], F32)) for b in range(8)]
        PT = [T("pb%d" % b) for b in range(8)]

        def pbf(b):
            return PB[b][:, :].bitcast(BF16)

        def ACT(out, in_, func, reads=(), writes=(), **kw):
            return em.op("act", lambda e: e.activation(out=out, in_=in_, func=func, **kw), reads, writes)

        def COPY(E, out, in_, reads=(), writes=()):
            if E == "act":
                return em.op("act", lambda e: e.activation(out=out, in_=in_, func=AF.Copy), reads, writes)
            return em.op(E, lambda e: e.tensor_copy(out=out, in_=in_), reads, writes)

        def TT(E, out, in0, in1, op, reads=(), writes=()):
            return em.op(E, lambda e: e.tensor_tensor(out=out, in0=in0, in1=in1, op=op), reads, writes)

        def TS(E, out, in0, s1, op0, s2=None, op1=None, reads=(), writes=()):
            if op1 is None:
                return em.op(E, lambda e: e.tensor_scalar(out=out, in0=in0, scalar1=s1, scalar2=None, op0=op0), reads, writes)
            return em.op(E, lambda e: e.tensor_scalar(out=out, in0=in0, scalar1=s1, scalar2=s2, op0=op0, op1=op1), reads, writes)

        def STT(out, in0, scalar, in1, op0, op1, reads=(), writes=()):
            return em.op("dve", lambda e: e.scalar_tensor_tensor(out=out, in0=in0, scalar=scalar, in1=in1, op0=op0, op1=op1), reads, writes)

        def MM(out, lhsT, rhs, start=True, stop=True, reads=(), writes=()):
            return em.op("pe", lambda e: e.matmul(out, lhsT=lhsT, rhs=rhs, start=start, stop=stop), reads, writes)

        def TR(out, in_, ident, reads=(), writes=()):
            return em.op("pe", lambda e: e.transpose(out, in_, ident), reads, writes)

        def MEMSET(E, ap, val, writes=()):
            return em.op(E, lambda e: e.memset(ap, val), (), writes)

        dq = ["sp", "act"]
        dqi = [0]

        def DMA(out, in_, reads=(), writes=(), own=None, q=None):
            if q is None:
                q = dq[dqi[0] % len(dq)]
                dqi[0] += 1
            return em.dma(q, out, in_, reads, writes, own)

        cn = {}
        Tc = T("consts")
        for k in ["c_ident", "c_tribd", "c_blk", "c_mbig", "c_mneg", "c_tris", "c_trisuf", "c_m01"]:
            cn[k] = sb(k, [128, 128], F32)
            DMA(cn[k][:, :], I[k][:, :], writes=[Tc], own=Tc)
        ident_f = cn["c_ident"]
        ident_b = sb("ident_b", [128, 128], BF16)
        COPY("dve", ident_b[:, :], ident_f[:, :], [Tc], [Tc])
        ones_f = sb("ones_f", [128, 128], F32)
        MEMSET("pool", ones_f[:, :], 1.0, [Tc])
        m01_b = sb("m01_b", [128, 128], BF16)
        COPY("dve", m01_b[:, :], cn["c_m01"][:, :], [Tc], [Tc])

        eps_col = sb("eps_col", [128, 4], F32)
        MEMSET("pool", eps_col[:, 0:1], EPS, [Tc])
        MEMSET("pool", eps_col[:, 1:2], float(np.log(128.0 ** -0.5)), [Tc])
        MEMSET("pool", eps_col[:, 2:3], 1.0, [Tc])
        vec = sb("vecs", [128, 16], F32)
        Tvec = T("vec")
        DMA(vec[:, 0:4], I["dt_bias"][0:1, :].partition_broadcast(128), writes=[Tvec], own=Tvec)
        DMA(vec[:, 4:8], I["a_log"][0:1, :].partition_broadcast(128), writes=[Tvec], own=Tvec)
        ACT(vec[:, 4:8], vec[:, 4:8], AF.Exp, [Tvec], [Tvec])
        TS("dve", vec[:, 4:8], vec[:, 4:8], -1.0, ALU.mult, reads=[Tvec], writes=[Tvec])
        onA_bc = sb("onA_bc", [128, 128], F32)
        DMA(onA_bc[:, :], I["onorm_a"][0:1, :].partition_broadcast(128), writes=[Tvec], own=Tvec)
        onB_bc = sb("onB_bc", [128, 256], F32)
        DMA(onB_bc[:, :], I["onorm_b"][0:1, :].partition_broadcast(128), writes=[Tvec], own=Tvec)
        oncol = sb("oncol", [128, 4], F32)
        with nc.allow_non_contiguous_dma(reason="tiny column loads"):
            DMA(oncol[:, 0:1], I["onorm_a"].rearrange("o d -> d o"), writes=[Tvec], own=Tvec)
            DMA(oncol[:, 1:3], I["onorm_b"].rearrange("o (c d) -> d (o c)", c=2), writes=[Tvec], own=Tvec)

        wcaT = sb("wcaT", [128, 12, 4], F32)
        Twca = T("wca")
        m0 = sb.mark()
        wca_sb = sb("wca_sb", [4, 1536], F32)
        Ttmp = T("tmp")
        DMA(wca_sb[:, :], I["w_conv_a"][:, :], writes=[Ttmp], own=Ttmp)
        for cc in range(12):
            TR(PB[0][:, cc * 4:cc * 4 + 4], wca_sb[0:4, cc * 128:(cc + 1) * 128], ident_f[0:4, 0:4], [Ttmp, Tc], [PT[0]])
        COPY("dve", wcaT[:, :, :].rearrange("p a b -> p (a b)"), PB[0][:, 0:48], [], [PT[0], Twca])
        sb.release(m0)

        F32R = mybir.dt.float32r
        KR_GBC = os.environ.get("KR_GBC", "1") == "1"
        KR_GLA = os.environ.get("KR_GLA", "1") == "1"
        KR_A1 = os.environ.get("KR_A1", "1") == "1"
        ones_r = sb("ones_r", [128, 128], F32R)
        COPY("dve", ones_r[:, :], ones_f[:, :], [Tc], [Tc])
        tris_r = sb("tris_r", [128, 128], F32R)
        COPY("dve", tris_r[:, :], cn["c_tris"][:, :], [Tc], [Tc])
        trisuf_r = sb("trisuf_r", [128, 128], F32R)
        COPY("dve", trisuf_r[:, :], cn["c_trisuf"][:, :], [Tc], [Tc])
        tribd_r0 = sb("tribd_r0", [128, 128], F32R)
        COPY("dve", tribd_r0[:, :], cn["c_tribd"][:, :], [Tc], [Tc])
        em.barrier()

        hT_off = sb.top
        hT = sb("hT", [128, 8, NT], BF16)
        ThT = [T("hT%d" % i) for i in range(18)]

        def tile_src(ti):
            if ti == 0:
                return I["meta"][0:16, :]
            if ti == 17:
                return I["xs"][0:16, :]
            return I["xp"][128 * (ti - 1):128 * ti, :]

        def run_window(make_gen, items, width):
            items = list(items)
            active = []
            nxt = 0
            while active or nxt < len(items):
                while len(active) < width and nxt < len(items):
                    active.append(make_gen(items[nxt]))
                    nxt += 1
                keep = []
                for g in active:
                    try:
                        next(g)
                        keep.append(g)
                    except StopIteration:
                        pass
                active = keep

        def norm_tiles(dstT, dstTT, get_x, nwt, Tnwt, bank0):
            NB = 4
            mk = sb.mark()
            hb = [sb("hb%d" % i, [128, D], BF16) for i in range(NB)]
            Thb = [T("hb%d" % i) for i in range(NB)]
            jk = sb("jk", [128, D], BF16)
            Tjk = T("jk")
            st_ = sb("nst", [128, 3, 18], F32)
            Tst = T("nst")
            MEMSET("pool", st_[:, 0, :], 1.0, [Tst])
            xs = [get_x(ti) for ti in range(18)]
            for ti in range(18):
                c0, n = TILES[ti]
                xa, xr = xs[ti]
                ACT(jk[0:n, :], xa, AF.Square, xr, [Tjk, Tst], accum_out=st_[0:n, 0, ti:ti + 1])
            ACT(st_[:, 2, 0:1], eps_col[:, 0:1], AF.Copy, [], [Tst])
            ACT(st_[:, 1, :], st_[:, 0, :], AF.Ln, [], [Tst], scale=1.0 / D, bias=eps_col[:, 0:1])
            ACT(st_[:, 2, :], st_[:, 1, :], AF.Exp, [], [Tst], scale=-0.5)

            def gen(ti):
                c0, n = TILES[ti]
                xa, xr = xs[ti]
                s = ti % NB
                STT(hb[s][0:n, :], xa, st_[0:n, 2, ti:ti + 1], nwt[0:n, :], ALU.mult, ALU.mult, xr + [Tst, Tnwt], [Thb[s]])
                yield
                bk = bank0 + s
                for kc in range(8):
                    TR(pbf(bk)[:, kc * 128:kc * 128 + n], hb[s][0:n, kc * 128:(kc + 1) * 128], ident_b[0:n, 0:n], [Thb[s]], [PT[bk]])
                yield
                COPY("act" if ti % 2 else "dve", dstT[:, :, c0:c0 + n],
                     pbf(bk).rearrange("p (k c) -> p k c", k=8)[:, :, 0:n], [], [PT[bk], dstTT[ti]])
                yield
            run_window(gen, range(18), NB)
            sb.release(mk)


        mP0 = sb.mark()
        nw_bc = sb("nw_bc", [128, D], F32)
        Tnw = T("nw")
        DMA(nw_bc[:, :], I["norm_mix"][0:1, :].partition_broadcast(128), writes=[Tnw], own=Tnw)
        xst = [sb("xst%d" % i, [128, D], F32) for i in range(18)]
        Txst = [T("xst%d" % i) for i in range(18)]

        def get_x0(ti):
            c0, n = TILES[ti]
            s = ti
            DMA(xst[s][0:n, :], tile_src(ti), writes=[Txst[s]], own=Txst[s])
            return xst[s][0:n, :], [Txst[s]]

        norm_tiles(hT, ThT, get_x0, nw_bc, Tnw, 0)
        em.barrier()
        sb.release(mP0)

        wsl = [None] * 3
        Tw = [None] * 3
        wctr = [0]

        def alloc_wsl():
            for i in range(3):
                wsl[i] = sb("wsl%d" % i, [128, 8, 512], BF16)
                Tw[i] = T("wsl%d" % i)

        def wload(src3, nk, ncol):
            s = wctr[0] % 3
            wctr[0] += 1
            if isinstance(src3, tuple):
                for i_, sr in enumerate(src3):
                    em.dma("pool", wsl[s][:, 0:nk, i_ * ncol:(i_ + 1) * ncol], sr, (), [Tw[s]], Tw[s])
            else:
                em.dma("pool", wsl[s][:, 0:nk, 0:ncol], src3, (), [Tw[s]], Tw[s])
            return s

        def w_in_blk(c0, ncol):
            return I["w_in"][:, c0:c0 + ncol].rearrange("(k p) n -> p k n", p=128)

        def run_jobs(jobs):
            slots = {}
            for i in range(min(2, len(jobs))):
                slots[i] = wload(*jobs[i][0:3])
            for i, jb in enumerate(jobs):
                if i + 2 < len(jobs):
                    slots[i + 2] = wload(*jobs[i + 2][0:3])
                jb[3](slots[i])

        def tiles_in(c0, n):
            return [ti for ti, (a, m) in enumerate(TILES) if a < c0 + n and c0 < a + m]

        pbr = [0]

        def fm_proj(slot, j, ncols, src, srcT, nk, slabs, epi, banks=(0, 1, 2, 3)):
            for (c0, n) in slabs:
                bk = banks[pbr[0] % len(banks)]
                pbr[0] += 1
                rd = [Tw[slot]] + [srcT[ti] for ti in tiles_in(c0, n)]
                for kc in range(nk):
                    MM(PB[bk][0:ncols, 0:n], wsl[slot][:, kc, j * 128:j * 128 + ncols], src[:, kc, c0:c0 + n],
                       kc == 0, kc == nk - 1, rd, [PT[bk]])
                epi(PB[bk][0:ncols, 0:n], c0, n, bk)

        oaT = sb("oaT", [128, 4, NT], BF16)
        mA = sb.mark()
        kvT = sb("kvT", [128, 8, NT], BF16)

        class _QKV:
            def __getitem__(self, key):
                p, cc, cols = key
                if isinstance(cc, slice):
                    assert cc.start == 0 and cc.stop == 4
                    return oaT[p, cc, cols]
                if cc < 4:
                    return oaT[p, cc, cols]
                return kvT[p, cc - 4, cols]
        qkvT = _QKV()
        Tq = [[T("qkv%d_%d" % (g, i)) for i in range(18)] for g in range(3)]
        decq = sb("decq", [128, 12, 16], F32)
        Tdecq = T("decq")
        lastpre = sb("lastpre", [128, 12, 3], F32)
        Tlast = T("lastpre")
        decpre = sb("decpre", [128, 12, 16], F32)
        Tdecpre = T("decpre")
        bg_all = sb("bg_all", [128, 18, 8], F32)
        Tbg = T("bg_all")
        MEMSET("pool", bg_all[:, :, :].rearrange("p a b -> p (a b)"), 0.0, [Tbg])
        prevT = sb("prevT", [128, 12, 3, 16], F32)
        TprevT = T("prevT")

        mk = sb.mark()
        sdc_sb = sb("sdc_sb", [16, 3, 1536], F32)
        Tsdc = T("sdc")
        DMA(sdc_sb[:, :, :], I["sdc"][:, :, :], writes=[Tsdc], own=Tsdc)
        DMA(O["ndc_s"][:, 0:2, :], sdc_sb[:, 1:3, :], reads=[Tsdc], own=Tsdc)
        for cc in range(12):
            for r in range(3):
                TR(PB[4][:, (r * 16):(r * 16 + 16)], sdc_sb[0:16, r, cc * 128:(cc + 1) * 128], ident_f[0:16, 0:16], [Tsdc], [PT[4]])
            COPY("act", prevT[:, cc, :, :].rearrange("p a b -> p (a b)"), PB[4][:, 0:48], [], [PT[4], TprevT])
        em.barrier()
        sb.release(mk)

        mA1 = sb.mark()
        alloc_wsl()
        pre = [sb("pre%d" % i, [128, 3 + NT], F32) for i in range(2)]
        Tpre = [T("pre%d" % i) for i in range(2)]
        for i in range(2):
            MEMSET("pool", pre[i][:, 0:3], 0.0, [Tpre[i]])
        cv = sb("cv", [128, NT], F32)
        Tcv = T("cv")
        sv_l = [sb("sv%d" % i, [128, NT], F32) for i in range(2)]
        Tsv_l = [T("sv%d" % i) for i in range(2)]
        sq_l = [sb("sq%d" % i, [128, NT], F32) for i in range(2)]
        Tsq_l = [T("sq%d" % i) for i in range(2)]
        rn = [sb("rn%d" % i, [128, 512], F32) for i in range(2)]
        Trn = [T("rn%d" % i) for i in range(2)]
        rnc = [0]

        def conv_proj(cc, slot, j):
            ps_ = cc % 2

            def epi(ps, c0, n, bk):
                COPY("act", pre[ps_][:, 3 + c0:3 + c0 + n], ps, [], [PT[bk], Tpre[ps_]])
            fm_proj(slot, j, 128, hT, ThT, 8, SLABS, epi, banks=(0, 1, 2, 3, 6, 7))

        def conv_C(cc):
            ps_ = cc % 2
            p = pre[ps_]
            TS("dve", cv[:, 0:NPC], p[:, 0:NPC], wcaT[:, cc, 0:1], ALU.mult, reads=[Tpre[ps_], Twca], writes=[Tcv])
            for i in range(1, 4):
                STT(cv[:, 0:NPC], p[:, i:i + NPC], wcaT[:, cc, i:i + 1], cv[:, 0:NPC], ALU.mult, ALU.add, [Tpre[ps_], Twca], [Tcv])
            TS("dve", cv[:, DEC0:NT], p[:, 3 + DEC0:3 + NT], wcaT[:, cc, 3:4], ALU.mult, reads=[Tpre[ps_], Twca], writes=[Tcv])
            for i in range(3):
                STT(cv[:, DEC0:NT], prevT[:, cc, i, :], wcaT[:, cc, i:i + 1], cv[:, DEC0:NT], ALU.mult, ALU.add, [TprevT, Twca], [Tcv])
            COPY("pool", lastpre[:, cc, :], p[:, 3 + NPC - 3:3 + NPC], [Tpre[ps_]], [Tlast])
            COPY("pool", decpre[:, cc, :], p[:, 3 + DEC0:3 + NT], [Tpre[ps_]], [Tdecpre])

        def conv_S(cc):
            g = cc // 4
            if g == 2:
                ACT(qkvT[:, cc, :], cv[:, :], AF.Silu, [Tcv], Tq[g])
                ACT(decq[:, cc, :], cv[:, DEC0:NT], AF.Silu, [Tcv], [Tdecq])
                return
            sv, Tsv, sq, Tsq = sv_l[cc % 2], Tsv_l[cc % 2], sq_l[cc % 2], Tsq_l[cc % 2]
            ACT(sv[:, :], cv[:, :], AF.Silu, [Tcv], [Tsv])
            ACT(sq[:, :].bitcast(F32R) if KR_A1 else sq[:, :], sv[:, :], AF.Square, [Tsv], [Tsq])

        def conv_N(cc):
            g = cc // 4
            if g == 2:
                return
            sv, Tsv, sq, Tsq = sv_l[cc % 2], Tsv_l[cc % 2], sq_l[cc % 2], Tsq_l[cc % 2]
            for (c0, n) in SLABS:
                bk = 4 + (rnc[0] % 2)
                r = rnc[0] % 2
                rnc[0] += 1
                if KR_A1:
                    MM(PB[bk][:, 0:n], ones_r[:, :], sq[:, c0:c0 + n].bitcast(F32R), True, True, [Tsq], [PT[bk]])
                else:
                    MM(PB[bk][:, 0:n], ones_f[:, :], sq[:, c0:c0 + n], True, True, [Tsq], [PT[bk]])
                ACT(rn[r][:, 0:n], PB[bk][:, 0:n], AF.Ln, [], [PT[bk], Trn[r]], bias=eps_col[:, 0:1])
                if g == 0:
                    ACT(rn[r][:, 0:n], rn[r][:, 0:n], AF.Exp, [], [Trn[r]], scale=-0.5, bias=eps_col[:, 1:2])
                else:
                    ACT(rn[r][:, 0:n], rn[r][:, 0:n], AF.Exp, [], [Trn[r]], scale=-0.5)
                TT("dve", qkvT[:, cc, c0:c0 + n], sv[:, c0:c0 + n], rn[r][:, 0:n], ALU.mult, [Tsv, Trn[r]],
                   [Tq[g][ti] for ti in tiles_in(c0, n)])
                if c0 <= DEC0 < c0 + n:
                    TT("dve", decq[:, cc, :], sv[:, DEC0:NT], rn[r][:, DEC0 - c0:NT - c0], ALU.mult, [Tsv, Trn[r]], [Tdecq])

        def job_qkv(g):
            def fn(slot):
                for j in range(4):
                    conv_chunk(g * 4 + j, slot, j)
            return (w_in_blk(g * 512, 512), 8, 512, fn)

        def job_bg(slot):
            for ti in range(18):
                c0, n = TILES[ti]
                for kc in range(8):
                    MM(PB[6][0:n, ti * 8:ti * 8 + 8], hT[:, kc, c0:c0 + n], wsl[slot][:, kc, 0:8], kc == 0, kc == 7,
                       [Tw[slot], ThT[ti]], [PT[6]])
            for ti in range(18):
                c0, n = TILES[ti]
                COPY("dve", bg_all[0:n, ti, :], PB[6][0:n, ti * 8:ti * 8 + 8], [], [PT[6], Tbg])

        qslots = [wload(w_in_blk(g * 512, 512), 8, 512) for g in range(3)]
        conv_proj(0, qslots[0], 0)
        conv_proj(1, qslots[0], 1)
        conv_C(0)
        conv_S(0)
        for cc in range(12):
            if cc + 2 < 12:
                conv_proj(cc + 2, qslots[(cc + 2) // 4], (cc + 2) % 4)
            if cc + 1 < 12:
                conv_C(cc + 1)
            conv_N(cc)
            if cc + 1 < 12:
                conv_S(cc + 1)
        job_bg(wload(w_in_blk(BA, 8), 8, 8))

        mk = sb.mark()
        cvo = sb("cvo", [16, 1536], F32)
        Tcvo = T("cvo")
        cvo2 = sb("cvo2", [3, 1536], F32)
        Tcvo2 = T("cvo2")
        for cc in range(12):
            TR(PB[7][0:16, (cc % 4) * 128:(cc % 4) * 128 + 128], decpre[:, cc, :], ident_f[:, :], [Tdecpre], [PT[7]])
            if cc % 4 == 3:
                COPY("dve", cvo[0:16, (cc - 3) * 128:(cc + 1) * 128], PB[7][0:16, 0:512], [], [PT[7], Tcvo])
        DMA(O["ndc_s"][:, 2, :], cvo[0:16, :], reads=[Tcvo], own=Tcvo)
        for cc in range(12):
            TR(PB[7][0:3, (cc % 4) * 128:(cc % 4) * 128 + 128], lastpre[:, cc, :], ident_f[:, :], [Tlast], [PT[7]])
            if cc % 4 == 3:
                COPY("dve", cvo2[0:3, (cc - 3) * 128:(cc + 1) * 128], PB[7][0:3, 0:512], [], [PT[7], Tcvo2])
        DMA(O["ndc_p"][:, :], cvo2[0:3, :], reads=[Tcvo2], own=Tcvo2)
        em.barrier()
        sb.release(mk)
        sb.release(mA1)
        if stage <= 1:
            em.finish()
            print("sbuf peak", sb.peak, "ops", em.nops, "dsems", len(em.dsems))
            return nc

        mA2 = sb.mark()
        hd = []
        for h in range(4):
            d_ = {}
            CH_BF = os.environ.get("KCHAIN", "bf16") == "bf16"
            for nm in ["gcol", "tm", "Ds", "u", "eGbc"]:
                d_[nm] = sb("%s%d" % (nm, h), [128, 128], F32)
            for nm in ["A0", "A1", "B0", "B1", "P0", "P1"]:
                d_[nm] = sb("%s%d" % (nm, h), [128, 128], BF16 if CH_BF else F32)
            for nm in ["Pbf", "kbg", "kd", "vb", "wT", "atm", "qg", "vn", "on"]:
                d_[nm] = sb("%s%d" % (nm, h), [128, 128], BF16)
            d_["S"] = sb("S%d" % h, [128, 128], F32)
            d_["Sbf"] = sb("Sbf%d" % h, [128, 128], BF16)
            d_["col"] = sb("col%d" % h, [128, 8], F32)
            d_["T"] = {nm: T("%s%d" % (nm, h)) for nm in list(d_.keys())}
            d_["hb"] = [dict(u=d_["u"], wT=d_["wT"], atm=d_["atm"], qg=d_["qg"], kd=d_["kd"], col=d_["col"]),
                        dict(u=sb("u%db" % h, [128, 128], F32), wT=sb("wT%db" % h, [128, 128], BF16), atm=sb("atm%db" % h, [128, 128], BF16),
                             qg=sb("qg%db" % h, [128, 128], BF16), kd=sb("kd%db" % h, [128, 128], BF16), col=sb("col%db" % h, [128, 8], F32))]
            d_["hT"] = [{k: d_["T"][k] for k in d_["hb"][0]}, {k: T("%s%db" % (k, h)) for k in d_["hb"][1]}]
            hd.append(d_)
            MEMSET("pool", d_["S"][:, :], 0.0, [d_["T"]["S"]])
            MEMSET("pool", d_["Sbf"][:, :], 0.0, [d_["T"]["Sbf"]])
        sc = sb("sc", [128, 64], F32)
        Tsc = T("sc")
        ojk = sb("ojk", [128, 128], BF16)
        Tojk = T("ojk")
        sc2 = sb("sc2", [128, 16], F32)
        Tsc2 = T("sc2")
        F32R = mybir.dt.float32r
        tribd_r = sb("tribd_r", [128, 128], F32R)
        Ttr = T("tribd_r")
        COPY("dve", tribd_r[:, :], cn["c_tribd"][:, :], [], [Ttr])
        Sd = [sb("Sd%d" % i, [128, 4, 128], F32) for i in range(2)]
        TSd = [T("Sd%d" % i) for i in range(2)]
        Sn = [sb("Sn%d" % i, [128, 4, 128], F32) for i in range(2)]
        TSn = [T("Sn%d" % i) for i in range(2)]
        dX = sb("dX", [16, 192], F32)
        TdX = T("dX")
        dbc = sb("dbc", [128, 192], F32)
        Tdbc = T("dbc")
        dvb = sb("dvb", [128, 4, 16], F32)
        Tdvb = T("dvb")
        ddiag = sb("ddiag", [128, 128], F32)
        Tddiag = T("ddiag")
        dtmp = sb("dtmp", [128, 128], F32)
        Tdtmp = T("dtmp")
        derr = sb("derr", [128, 4], F32)
        Tderr = T("derr")
        odec = sb("odec", [128, 64], F32)
        Todec = T("odec")

        sca = sb("sca", [128, 18, 40], F32)
        Tsca = T("sca")
        gall = sb("gall", [128, 72], F32)
        Tgall = T("gall")

        def all_scalars():
            V = lambda a_, b_: sca[:, :, a_:b_]
            ACT(V(0, 4), bg_all[:, :, 0:4], AF.Sigmoid, [Tbg], [Tsca])
            TS("dve", V(4, 8), V(0, 4), -1.0, ALU.mult, reads=[], writes=[Tsca])
            for ti in range(18):
                TT("dve", sca[:, ti, 32:36], bg_all[:, ti, 4:8], vec[:, 0:4], ALU.add, [Tbg], [Tsca])
            ACT(V(32, 36), V(32, 36), AF.Exp, [], [Tsca])
            ACT(V(32, 36), V(32, 36), AF.Ln, [], [Tsca], bias=eps_col[:, 2:3])
            for ti in range(18):
                TT("dve", sca[:, ti, 8:12], sca[:, ti, 32:36], vec[:, 4:8], ALU.mult, [], [Tsca])
            COPY("dve", gall[:, :].rearrange("p (a b) -> p a b", a=18), V(8, 12), [Tsca], [Tgall])
            MM(PB[1][:, 0:72], cn["c_tribd"][:, :], gall[:, :], True, True, [Tgall], [PT[1]])
            MM(PB[1][:, 128:200], cn["c_blk"][:, :], gall[:, :], True, True, [Tgall], [PT[1]])
            COPY("dve", V(12, 16), PB[1][:, 0:72].rearrange("p (a b) -> p a b", a=18), [], [PT[1], Tsca])
            COPY("dve", V(16, 20), PB[1][:, 128:200].rearrange("p (a b) -> p a b", a=18), [], [PT[1], Tsca])
            MM(PB[1][0:16, 256:260], cn["c_blk"][0:16, 0:16], gall[0:16, 0:4], True, True, [Tgall], [PT[1]])
            COPY("dve", sca[0:16, 0, 16:20], PB[1][0:16, 256:260], [], [PT[1], Tsca])
            ACT(V(20, 24), V(12, 16), AF.Exp, [], [Tsca])
            TT("dve", V(32, 36), V(16, 20), V(12, 16), ALU.subtract, [], [Tsca])
            ACT(V(24, 28), V(32, 36), AF.Exp, [], [Tsca])
            TT("dve", V(28, 32), V(0, 4), V(20, 24), ALU.mult, [], [Tsca])

        def delta_prep(ti, heads):
            c0, n = TILES[ti]
            pb_ = ti % 2
            chunks = [(0, min(n, 64))] + ([(64, 64)] if n == 128 else [])
            nk = 5 if n == 128 else 3
            KF = os.environ.get("KF32R", "2")
            CH_BF = os.environ.get("KCHAIN", "bf16") == "bf16"
            USE_R = KF in ("1", "2") and not CH_BF
            USE_RG = KF in ("1", "3")
            R = (lambda ap: ap.bitcast(F32R)) if (n == 128 and USE_R) else (lambda ap: ap)
            RO = (lambda ap: ap.bitcast(F32R)) if USE_R else (lambda ap: ap)
            sc = sca[:, ti, :]
            Tsc = Tsca
            cs = slice(c0, c0 + n)
            qT = lambda h: qkvT[:, h, cs]
            kT = lambda h: qkvT[:, 4 + h, cs]
            vT = lambda h: qkvT[:, 8 + h, cs]
            rq = [Tq[0][ti]]
            rk = [Tq[1][ti]]
            rv = [Tq[2][ti]]
            H = heads
            D_ = lambda h: hd[h]
            TD_ = lambda h: hd[h]["T"]
            HB = lambda h: hd[h]["hb"][pb_]
            HT = lambda h: hd[h]["hT"][pb_]
            Wb = lambda h: 2 * h
            Gb = lambda h: 2 * h + 1
            X = lambda h, r, w: PB[Wb(h)][r, 0:w]
            Y = lambda h, r, w: PB[Wb(h)][r, 128:128 + w]
            for h in H:
                TS("pool", D_(h)["gcol"][0:n, :].bitcast(F32R) if KR_GBC else RO(D_(h)["gcol"][0:n, :]), ones_f[0:n, :], sc[0:n, 8 + h:9 + h], ALU.mult,
                   1.0, ALU.mult, reads=[Tsc], writes=[TD_(h)["gcol"]])
            yield
            for h in H:
                if n == 128 and KR_GBC:
                    MM(PB[Gb(h)][:, 0:n], D_(h)["gcol"][0:n, :].bitcast(F32R), tribd_r0[0:n, 0:n], True, True, [TD_(h)["gcol"]], [PT[Gb(h)]])
                else:
                    MM(PB[Gb(h)][:, 0:n], D_(h)["gcol"][0:n, :], cn["c_tribd"][0:n, 0:n], True, True, [TD_(h)["gcol"]], [PT[Gb(h)]])
                MM(X(h, slice(0, n), n), kT(h), kT(h), True, True, rk, [PT[Wb(h)]])
            yield
            for h in H:
                STT(D_(h)["tm"][0:n, 0:n], PB[Gb(h)][0:n, 0:n], sc[0:n, 12 + h:13 + h], cn["c_mbig"][0:n, 0:n], ALU.subtract, ALU.max,
                    [Tsc], [PT[Gb(h)], TD_(h)["tm"]])
            yield
            for h in H:
                ACT(D_(h)["Ds"][0:n, 0:n], D_(h)["tm"][0:n, 0:n], AF.Exp, [TD_(h)["tm"]], [TD_(h)["Ds"]], scale=-1.0)
            yield
            for h in H:
                STT(RO(D_(h)["A0"][0:n, 0:n]), X(h, slice(0, n), n), sc[0:n, 4 + h:5 + h], D_(h)["Ds"][0:n, 0:n], ALU.mult, ALU.mult,
                    [Tsc, TD_(h)["Ds"]], [PT[Wb(h)], TD_(h)["A0"]])
            yield
            if CH_BF:
                YB = lambda h: pbf(Wb(h))[0:n, 256:256 + n]
            else:
                YB = lambda h: Y(h, slice(0, n), n)
            for h in H:
                TR(YB(h), D_(h)["A0"][0:n, 0:n], (ident_b if CH_BF else ident_f)[0:n, 0:n], [TD_(h)["A0"]], [PT[Wb(h)]])
            yield
            for h in H:
                STT(D_(h)["tm"][0:n, 0:n], PB[Gb(h)][0:n, 0:n], sc[0:n, 12 + h:13 + h], cn["c_mneg"][0:n, 0:n], ALU.subtract, ALU.min,
                    [Tsc], [PT[Gb(h)], TD_(h)["tm"]])
            yield
            for h in H:
                COPY("act", RO(D_(h)["B0"][0:n, 0:n]), YB(h), [], [PT[Wb(h)], TD_(h)["B0"]])
            yield
            for h in H:
                TT("dve", RO(D_(h)["P0"][0:n, 0:n]), YB(h), ident_f[0:n, 0:n], ALU.add, [], [PT[Wb(h)], TD_(h)["P0"]])
            yield
            for h in H:
                ACT(D_(h)["Ds"][0:n, 0:n], D_(h)["tm"][0:n, 0:n], AF.Exp, [TD_(h)["tm"]], [TD_(h)["Ds"]])
                ACT(D_(h)["eGbc"][:, 0:n], PB[Gb(h)][:, 0:n], AF.Exp, [], [PT[Gb(h)], TD_(h)["eGbc"]])
                for ci, (r0, m) in enumerate(chunks):
                    ACT(HB(h)["col"][:, ci:ci + 1], PB[Gb(h)][:, r0 + m - 1:r0 + m], AF.Exp, [], [PT[Gb(h)], HT(h)["col"]])
            yield
            for k in range(nk):
                a, b_ = "A%d" % (k % 2), "A%d" % ((k + 1) % 2)
                ba, bb = "B%d" % (k % 2), "B%d" % ((k + 1) % 2)
                pa, pb2 = "P%d" % (k % 2), "P%d" % ((k + 1) % 2)
                for h in H:
                    d_, TD = D_(h), TD_(h)
                    MM(X(h, slice(0, n), n), R(d_[ba][0:n, 0:n]), R(d_[a][0:n, 0:n]), True, True, [TD[ba], TD[a]], [PT[Wb(h)]])
                    if k < nk - 1:
                        MM(PB[Gb(h)][0:n, 256:256 + n], R(d_[a][0:n, 0:n]), R(d_[ba][0:n, 0:n]), True, True, [TD[ba], TD[a]], [PT[Gb(h)]])
                yield
                for h in H:
                    d_, TD = D_(h), TD_(h)
                    COPY("act", RO(d_[b_][0:n, 0:n]), X(h, slice(0, n), n), [], [PT[Wb(h)], TD[b_]])
                yield
                if k < nk - 1:
                    for h in H:
                        d_, TD = D_(h), TD_(h)
                        COPY("act" if h >= 2 else "dve", RO(d_[bb][0:n, 0:n]), PB[Gb(h)][0:n, 256:256 + n], [], [PT[Gb(h)], TD[bb]])
                    yield
                for h in H:
                    d_, TD = D_(h), TD_(h)
                    MM(Y(h, slice(0, n), n), R(d_[b_][0:n, 0:n]), R(d_[pa][0:n, 0:n]), True, True, [TD[b_], TD[pa]], [PT[Wb(h)]])
                yield
                for h in H:
                    d_, TD = D_(h), TD_(h)
                    TT("dve", RO(d_[pb2][0:n, 0:n]), Y(h, slice(0, n), n), d_[pa][0:n, 0:n], ALU.add, [TD[pa]], [PT[Wb(h)], TD[pb2]])
                yield
            pf = "P%d" % (nk % 2)
            if CH_BF:
                PBF = lambda h: D_(h)[pf]
                TPBF = lambda h: TD_(h)[pf]
            else:
                PBF = lambda h: D_(h)["Pbf"]
                TPBF = lambda h: TD_(h)["Pbf"]
                for h in H:
                    COPY("act", D_(h)["Pbf"][0:n, 0:n], D_(h)[pf][0:n, 0:n], [TD_(h)[pf]], [TD_(h)["Pbf"]])
                yield
            for h in H:
                TR(pbf(Wb(h))[0:n, 0:128], kT(h), ident_b[:, :], rk, [PT[Wb(h)]])
                TR(pbf(Wb(h))[0:n, 128:256], vT(h), ident_b[:, :], rv, [PT[Wb(h)]])
            yield
            for h in H:
                ACT(D_(h)["kbg"][0:n, :], pbf(Wb(h))[0:n, 0:128], AF.Copy, [Tsc], [PT[Wb(h)], TD_(h)["kbg"]], scale=sc[0:n, 28 + h:29 + h])
            yield
            for h in H:
                TS("dve", HB(h)["kd"][0:n, :], pbf(Wb(h))[0:n, 0:128], sc[0:n, 24 + h:25 + h], ALU.mult, reads=[Tsc], writes=[PT[Wb(h)], HT(h)["kd"]])
            yield
            for h in H:
                ACT(D_(h)["vb"][0:n, :], pbf(Wb(h))[0:n, 128:256], AF.Copy, [Tsc], [PT[Wb(h)], TD_(h)["vb"]], scale=sc[0:n, 0 + h:1 + h])
            yield
            for h in H:
                TT("dve", HB(h)["qg"][:, 0:n], qT(h), D_(h)["eGbc"][:, 0:n], ALU.mult, rq + [TD_(h)["eGbc"]], [HT(h)["qg"]])
            yield
            for h in H:
                d_, TD = D_(h), TD_(h)
                MM(Y(h, slice(0, n), 128), PBF(h)[0:n, 0:n], d_["vb"][0:n, :], True, True, [TPBF(h), TD["vb"]], [PT[Wb(h)]])
                MM(X(h, slice(0, 128), n), d_["kbg"][0:n, :], PBF(h)[0:n, 0:n], True, True, [TPBF(h), TD["kbg"]], [PT[Wb(h)]])
            yield
            for h in H:
                COPY("act", HB(h)["u"][0:n, :], Y(h, slice(0, n), 128), [], [PT[Wb(h)], HT(h)["u"]])
            yield
            for h in H:
                COPY("dve", HB(h)["wT"][:, 0:n], X(h, slice(0, 128), n), [], [PT[Wb(h)], HT(h)["wT"]])
            yield
            for h in H:
                MM(Y(h, slice(0, n), n), kT(h), qT(h), True, True, rk + rq, [PT[Wb(h)]])
            yield
            for h in H:
                TT("dve", HB(h)["atm"][0:n, 0:n], Y(h, slice(0, n), n), D_(h)["Ds"][0:n, 0:n], ALU.mult, [TD_(h)["Ds"]], [PT[Wb(h)], HT(h)["atm"]])
            yield

        def delta_rec(ti):
            c0, n = TILES[ti]
            pb_ = ti % 2
            cs = slice(c0, c0 + n)
            chunks = [(0, min(n, 64))] + ([(64, 64)] if n == 128 else [])
            H = range(4)
            D_ = lambda h: hd[h]
            TD_ = lambda h: hd[h]["T"]
            HB = lambda h: hd[h]["hb"][pb_]
            HT = lambda h: hd[h]["hT"][pb_]
            Wb = lambda h: 2 * h
            Gb = lambda h: 2 * h + 1
            for ci, (r0, m) in enumerate(chunks):
                rs = slice(r0, r0 + m)
                for h in H:
                    d_, TD = D_(h), TD_(h)
                    MM(PB[Wb(h)][rs, 256:384], HB(h)["wT"][:, rs], d_["Sbf"][:, :], True, True, [HT(h)["wT"], TD["Sbf"]], [PT[Wb(h)]])
                yield
                for h in H:
                    d_, TD = D_(h), TD_(h)
                    TT("dve", d_["vn"][rs, :], HB(h)["u"][rs, :], PB[Wb(h)][rs, 256:384], ALU.subtract, [HT(h)["u"]], [PT[Wb(h)], TD["vn"]])
                yield
                for h in H:
                    d_, TD = D_(h), TD_(h)
                    MM(PB[Wb(h)][:, 256:384], HB(h)["kd"][rs, :], d_["vn"][rs, :], True, True, [HT(h)["kd"], TD["vn"]], [PT[Wb(h)]])
                    MM(PB[Gb(h)][rs, 128:256], HB(h)["qg"][:, rs], d_["Sbf"][:, :], True, False, [HT(h)["qg"], TD["Sbf"]], [PT[Gb(h)]])
                    MM(PB[Gb(h)][rs, 128:256], HB(h)["atm"][rs, rs], d_["vn"][rs, :], False, True, [HT(h)["atm"], TD["vn"]], [PT[Gb(h)]])
                yield
                for h in H:
                    d_, TD = D_(h), TD_(h)
                    STT(d_["S"][:, :], d_["S"][:, :], HB(h)["col"][:, ci:ci + 1], PB[Wb(h)][:, 256:384], ALU.mult, ALU.add,
                        [HT(h)["col"]], [PT[Wb(h)], TD["S"]])
                yield
                for h in H:
                    d_, TD = D_(h), TD_(h)
                    COPY("pool", d_["Sbf"][:, :], d_["S"][:, :], [TD["S"]], [TD["Sbf"]])
                yield
            for h in H:
                ACT(ojk[0:n, :], PB[Gb(h)][0:n, 128:256], AF.Square, [], [PT[Gb(h)], Tojk, Tsc2], accum_out=sc2[0:n, h:h + 1])
            ACT(sc2[0:n, 12:13], eps_col[0:n, 0:1], AF.Copy, [], [Tsc2])
            yield
            ACT(sc2[0:n, 4:8], sc2[0:n, 0:4], AF.Ln, [], [Tsc2], scale=1.0 / 128, bias=eps_col[0:n, 0:1])
            ACT(sc2[0:n, 8:12], sc2[0:n, 4:8], AF.Exp, [], [Tsc2], scale=-0.5)
            yield
            for h in H:
                STT(D_(h)["on"][0:n, :], PB[Gb(h)][0:n, 128:256], sc2[0:n, 8 + h:9 + h], onA_bc[0:n, :], ALU.mult, ALU.mult,
                    [Tsc2], [PT[Gb(h)], TD_(h)["on"]])
            yield
            for h in H:
                TR(pbf(Wb(h))[:, 768:768 + n], D_(h)["on"][0:n, :], ident_b[0:n, 0:n], [TD_(h)["on"]], [PT[Wb(h)]])
            yield
            for h in H:
                COPY("act" if h % 2 else "dve", qkvT[:, h, cs], pbf(Wb(h))[:, 768:768 + n], [], [PT[Wb(h)], Tq[0][ti]])
            yield

        def interleave(gens, weights=None):
            if weights is None:
                weights = [1] * len(gens)
            gens = [(g, w) for g, w in zip(gens, weights) if g is not None]
            while gens:
                nxt = []
                for g, w in gens:
                    alive = True
                    for _ in range(w):
                        try:
                            next(g)
                        except StopIteration:
                            alive = False
                            break
                    if alive:
                        nxt.append((g, w))
                gens = nxt

        def dec_delta_prep():
            ti = 17
            COPY("dve", sc[0:16, 0:12], sca[0:16, 17, 0:12], [Tsca], [Tsc])
            ACT(sc[0:16, 20:24], sc[0:16, 8:12], AF.Exp, [], [Tsc])
            TT("dve", sc[0:16, 32:36], sc[0:16, 20:24], sc[0:16, 4:8], ALU.mult, [], [Tsc])
            for h in range(4):
                TS("dve", dX[0:16, h * 16:(h + 1) * 16], ident_f[0:16, 0:16], sc[0:16, 20 + h:21 + h], ALU.mult, reads=[Tsc, Tc], writes=[TdX])
                TS("dve", dX[0:16, 64 + h * 16:64 + (h + 1) * 16], ident_f[0:16, 0:16], sc[0:16, 32 + h:33 + h], ALU.mult, reads=[Tsc, Tc], writes=[TdX])
                TS("dve", dX[0:16, 128 + h * 16:128 + (h + 1) * 16], ident_f[0:16, 0:16], sc[0:16, 0 + h:1 + h], ALU.mult, reads=[Tsc, Tc], writes=[TdX])
            MM(PB[7][:, 0:192], ones_f[0:16, :], dX[0:16, :], True, True, [TdX, Tc], [PT[7]])
            COPY("dve", dbc[:, :], PB[7][:, 0:192], [], [PT[7], Tdbc])
            TT("dve", dvb[:, :, :], decq[:, 8:12, :], dbc[:, 128:192].rearrange("p (h b) -> p h b", h=4), ALU.mult, [Tdecq, Tdbc], [Tdvb])

        def dec_delta(b):
            s = b % 2
            DMA(Sd[s][:, :, :], I["sd"][b].rearrange("h k v -> k h v"), writes=[TSd[s]], own=TSd[s])
            yield
            for h in range(4):
                W, G_ = 2 * h, 2 * h + 1
                hb = h * 16 + b
                MM(PB[W][:, 448:449], Sd[s][:, h, :], decq[:, 4 + h, b:b + 1], True, True, [TSd[s], Tdecq], [PT[W]])
                STT(derr[:, h:h + 1], PB[W][:, 448:449], dbc[:, 64 + hb:65 + hb], dvb[:, h, b:b + 1], ALU.mult, ALU.add,
                    [Tdbc, Tdvb], [PT[W], Tderr])
                yield
                TS("pool", ddiag[:, :].bitcast(F32R), ident_f[:, :], derr[:, h:h + 1], ALU.mult, 1.0, ALU.mult, reads=[Tderr], writes=[Tddiag])
                yield
                MM(PB[G_][:, 384:512], ones_r[:, :], ddiag[:, :].bitcast(F32R), True, True, [Tddiag], [PT[G_]])
                ACT(dtmp[:, :], PB[G_][:, 384:512], AF.Copy, [Tdecq], [PT[G_], Tdtmp], scale=decq[:, 4 + h, b:b + 1])
                yield
                STT(Sn[s][:, h, :], Sd[s][:, h, :], dbc[:, hb:hb + 1], dtmp[:, :], ALU.mult, ALU.add, [TSd[s], Tdbc, Tdtmp], [TSn[s]])
                yield
                MM(PB[W][:, 449:450], Sn[s][:, h, :], decq[:, 0 + h, b:b + 1], True, True, [TSn[s], Tdecq], [PT[W]])
                COPY("act", odec[:, hb:hb + 1], PB[W][:, 449:450], [], [PT[W], Todec])
                yield
            DMA(O["nd_s"][b].rearrange("h k v -> k h v"), Sn[s][:, :, :], reads=[TSn[s]], own=TSn[s])
            yield

        def dec_delta_finish():
            TT("dve", dtmp[:, 0:64], odec[:, :], odec[:, :], ALU.mult, [Todec], [Tdtmp])
            MM(PB[7][:, 0:64], ones_f[:, :], dtmp[:, 0:64], True, True, [Tdtmp, Tc], [PT[7]])
            ACT(dtmp[:, 64:128], PB[7][:, 0:64], AF.Ln, [], [PT[7], Tdtmp], scale=1.0 / 128, bias=eps_col[:, 0:1])
            ACT(dtmp[:, 64:128], dtmp[:, 64:128], AF.Exp, [], [Tdtmp], scale=-0.5)
            TT("dve", dtmp[:, 0:64], odec[:, :], dtmp[:, 64:128], ALU.mult, [Todec], [Tdtmp])
            TS("dve", qkvT[:, 0:4, DEC0:NT], dtmp[:, 0:64].rearrange("p (h b) -> p h b", h=4), oncol[:, 0:1], ALU.mult,
               reads=[Tdtmp, Tvec], writes=[Tq[0][17]])

        with nc.allow_non_contiguous_dma(reason="state tiles 512B rows"):
            print("A2 sbuf top", sb.top, "free", SB_END - sb.top)
            all_scalars()
            dec_delta_prep()
            interleave([delta_prep(0, (0, 1)), delta_prep(0, (2, 3))])
            for ti in range(17):
                interleave([delta_rec(ti),
                            delta_prep(ti + 1, (0, 1)) if ti + 1 < 17 else None,
                            delta_prep(ti + 1, (2, 3)) if ti + 1 < 17 else None,
                            dec_delta(ti - 1) if ti >= 1 else None], [2, 2, 2, 1])
            dec_delta_finish()
            for h in range(4):
                DMA(O["nd_p"][h], hd[h]["S"][:, :], reads=[hd[h]["T"]["S"]], own=hd[h]["T"]["S"])
        em.barrier()
        sb.release(mA2)
        if stage <= 2:
            em.finish()
            print("sbuf peak", sb.peak, "ops", em.nops, "dsems", len(em.dsems))
            return nc
        sb.release(mA)

        obT = sb("obT", [128, 8, NT], BF16)
        Tqk = [T("qkb%d" % i) for i in range(18)]
        mB = sb.mark()
        vtok = sb("vtok", [128, 18, 1024], BF16)
        Tvt = [T("vtok%d" % i) for i in range(18)]
        lrbT = sb("lrbT", [17, NT], F32)
        Tlrb = T("lrbT")
        MEMSET("pool", lrbT[:, :], 1.0, [Tlrb])
        wgk = sb("wgk", [17, 512], F32)
        Twgk = T("wgk")
        DMA(wgk[0:16, :], I["w_gk2"][:, :], writes=[Twgk], own=Twgk)
        DMA(wgk[16:17, :], I["b_gk"][0:1, :], writes=[Twgk], own=Twgk)
        decqb = sb("decqb", [128, 8, 16], F32)
        Tdecqb = T("decqb")
        mB1 = sb.mark()
        alloc_wsl()

        def job_lrb(slot):
            def epi(ps, c0, n, bk):
                COPY("act", lrbT[0:16, c0:c0 + n], ps, [], [PT[bk], Tlrb])
            fm_proj(slot, 0, 16, hT, ThT, 8, SLABS, epi, banks=(0, 1, 2, 3, 6, 7))

        def job_qkb(g):
            def fn(slot):
                for j in range(4):
                    cc = g * 4 + j

                    def epi(ps, c0, n, bk, cc=cc):
                        if c0 <= DEC0 < c0 + n:
                            psd = ps[:, DEC0 - c0:NT - c0]
                            if g == 0:
                                ACT(decqb[:, cc, :], psd, AF.Copy, [], [PT[bk], Tdecqb], scale=float(128.0 ** -0.5))
                            else:
                                COPY("act", decqb[:, cc, :], psd, [], [PT[bk], Tdecqb])
                        COPY("act" if (cc + c0 // 416) % 2 else "dve", obT[:, cc, c0:c0 + n], ps, [], [PT[bk]] + [Tqk[ti] for ti in tiles_in(c0, n)])
                    fm_proj(slot, j, 128, hT, ThT, 8, SLABS, epi, banks=(0, 1, 2, 3, 6, 7))
            return (w_in_blk(QB + g * 512, 512), 8, 512, fn)

        def job_vb(blk):
            def fn(slot):
                for ti in range(18):
                    c0, n = TILES[ti]
                    bk = 4 + ti % 2
                    for kc in range(8):
                        MM(PB[bk][0:n, 0:512], hT[:, kc, c0:c0 + n], wsl[slot][:, kc, 0:512], kc == 0, kc == 7, [Tw[slot], ThT[ti]], [PT[bk]])
                    COPY("act" if ti % 2 else "dve", vtok[0:n, ti, blk * 512:(blk + 1) * 512], PB[bk][0:n, 0:512], [], [PT[bk], Tvt[ti]])
            return (w_in_blk(VB + blk * 512, 512), 8, 512, fn)

        run_jobs([(w_in_blk(LRB, 16), 8, 16, job_lrb), job_qkb(0), job_qkb(1), job_vb(0), job_vb(1)])
        em.barrier()
        sb.release(mB1)
        if stage <= 3:
            em.finish()
            print("sbuf peak", sb.peak, "ops", em.nops, "dsems", len(em.dsems))
            return nc

        mB2 = sb.mark()
        sel_b = sb("sel_b", [16, 2048], BF16)
        Tsel = T("sel")
        mk = sb.mark()
        sel_f = sb("sel_f", [16, 2048], F32)
        DMA(sel_f[:, :], I["c_sel"][:, :], writes=[Tsel], own=Tsel)
        COPY("dve", sel_b[:, :], sel_f[:, :], [], [Tsel])
        em.barrier()
        sb.release(mk)
        ltok = sb("ltok", [128, 512], F32)
        Tlt = T("ltok")
        eE = sb("eE", [128, 512], F32)
        TeE = T("eE")
        khat_l = [sb("khat%d" % i, [128, 512], BF16) for i in range(2)]
        Tkh_l = [T("khat%d" % i) for i in range(2)]
        gsc = sb("gsc", [128, 16], F32)
        Tgsc = T("gsc")
        ojk2 = sb("ojk2", [128, 256], BF16)
        Tojk2 = T("ojk2")
        gh = []
        for h in range(4):
            d_ = {}
            d_["eB"] = sb("eB%d" % h, [128, 128], F32)
            d_["eNB"] = sb("eNB%d" % h, [128, 128], F32)
            for nm in ["qt", "kt", "atm"]:
                d_[nm] = sb("g%s%d" % (nm, h), [128, 128], BF16)
            d_["S"] = sb("gS%d" % h, [128, 256], F32)
            d_["Sbf"] = sb("gSbf%d" % h, [128, 256], BF16)
            d_["on"] = sb("gon%d" % h, [128, 256], BF16)
            d_["col"] = sb("gcol%d" % h, [128, 4], F32)
            d_["T"] = {nm: T("g%s%d" % (nm, h)) for nm in list(d_.keys())}
            d_["hb"] = [dict(qt=d_["qt"], atm=d_["atm"], col=d_["col"]),
                        dict(qt=sb("gqt%db" % h, [128, 128], BF16), atm=sb("gatm%db" % h, [128, 128], BF16), col=sb("gcol%db" % h, [128, 4], F32))]
            d_["hT"] = [{k: d_["T"][k] for k in d_["hb"][0]}, {k: T("g%s%db" % (k, h)) for k in d_["hb"][1]}]
            gh.append(d_)
            MEMSET("pool", d_["S"][:, :], 0.0, [d_["T"]["S"]])
            MEMSET("pool", d_["Sbf"][:, :], 0.0, [d_["T"]["Sbf"]])
        gdec = sb("gdec", [128, 4, 16], F32)
        Tgdec = T("gdec")
        Sg = [sb("Sg%d" % i, [128, 256], F32) for i in range(4)]
        TSg = [T("Sg%d" % i) for i in range(4)]
        Sgn = [sb("Sgn%d" % i, [128, 256], F32) for i in range(2)]
        TSgn = [T("Sgn%d" % i) for i in range(2)]
        gtmp = sb("gtmp", [128, 256], F32)
        Tgtmp = T("gtmp")
        odecb = sb("odecb", [128, 8, 16], F32)
        Todecb = T("odecb")

        def gla_prep(ti):
            c0, n = TILES[ti]
            cs = slice(c0, c0 + n)
            pb_ = ti % 2
            khat, Tkh = khat_l[pb_], Tkh_l[pb_]
            H = range(4)
            MM(PB[0][0:n, 0:512], lrbT[0:17, cs], wgk[0:17, :], True, True, [Tlrb, Twgk], [PT[0]])
            yield
            LW = (lambda ap: ap.bitcast(F32R)) if KR_GLA else (lambda ap: ap)
            LR = (lambda ap: ap.bitcast(F32R)) if (KR_GLA and n == 128) else (lambda ap: ap)
            ACT(LW(ltok[0:n, :]), PB[0][0:n, 0:512], AF.Exp, [], [PT[0], Tlt], scale=-1.0)
            yield
            ACT(LW(ltok[0:n, :]), ltok[0:n, :], AF.Ln, [], [Tlt], bias=eps_col[0:n, 2:3])
            yield
            MM(PB[0][0:n, 0:512], (trisuf_r if (KR_GLA and n == 128) else cn["c_trisuf"])[0:n, 0:n], LR(ltok[0:n, :]), True, True, [Tlt], [PT[0]])
            for h in H:
                TR(pbf(1)[0:n, h * 128:(h + 1) * 128], obT[:, 4 + h, cs], ident_b[:, :], [Tqk[ti]], [PT[1]])
            yield
            ACT(eE[0:n, :], PB[0][0:n, 0:512], AF.Exp, [], [PT[0], TeE])
            yield
            TT("dve", khat[0:n, :], pbf(1)[0:n, 0:512], eE[0:n, :], ALU.mult, [TeE], [PT[1], Tkh])
            for h in H:
                MM(PB[2 + h][:, 0:n], LR(ltok[0:n, h * 128:(h + 1) * 128]), (tris_r if (KR_GLA and n == 128) else cn["c_tris"])[0:n, 0:n], True, True, [Tlt], [PT[2 + h]])
            yield
            for h in H:
                d_, TD, bk = gh[h], gh[h]["T"], 2 + h
                ACT(d_["eB"][:, 0:n], PB[bk][:, 0:n], AF.Exp, [], [PT[bk], TD["eB"]], bias=eps_col[:, 1:2])
                ACT(d_["eNB"][:, 0:n], PB[bk][:, 0:n], AF.Exp, [], [PT[bk], TD["eNB"]], scale=-1.0)
                ACT(d_["hb"][pb_]["col"][:, 0:1], PB[bk][:, n - 1:n], AF.Exp, [], [PT[bk], d_["hT"][pb_]["col"]])
                yield
            for h in H:
                d_, TD = gh[h], gh[h]["T"]
                TT("dve", d_["hb"][pb_]["qt"][:, 0:n], obT[:, h, cs], d_["eB"][:, 0:n], ALU.mult, [Tqk[ti], TD["eB"]], [d_["hT"][pb_]["qt"]])
                TT("dve", d_["kt"][:, 0:n], obT[:, 4 + h, cs], d_["eNB"][:, 0:n], ALU.mult, [Tqk[ti], TD["eNB"]], [TD["kt"]])
            yield
            for h in H:
                d_, TD, bk = gh[h], gh[h]["T"], 2 + h
                MM(PB[bk][0:n, 128:128 + n], d_["kt"][:, 0:n], d_["hb"][pb_]["qt"][:, 0:n], True, True, [TD["kt"], d_["hT"][pb_]["qt"]], [PT[bk]])
            yield
            for h in H:
                d_, TD, bk = gh[h], gh[h]["T"], 2 + h
                TT("dve", d_["hb"][pb_]["atm"][0:n, 0:n], PB[bk][0:n, 128:128 + n], cn["c_m01"][0:n, 0:n], ALU.mult, [], [PT[bk], d_["hT"][pb_]["atm"]])
            yield

        def gla_rec(ti):
            c0, n = TILES[ti]
            cs = slice(c0, c0 + n)
            pb_ = ti % 2
            khat, Tkh = khat_l[pb_], Tkh_l[pb_]
            H = range(4)
            for h in H:
                d_, TD, bk = gh[h], gh[h]["T"], 2 + h
                hb_, hT_ = d_["hb"][pb_], d_["hT"][pb_]
                sbk = 6 + h % 2
                vt = vtok[0:n, ti, h * 256:(h + 1) * 256]
                MM(PB[sbk][:, (h // 2) * 256:(h // 2) * 256 + 256], khat[0:n, h * 128:(h + 1) * 128], vt, True, True, [Tkh, Tvt[ti]], [PT[sbk]])
                MM(PB[bk][0:n, 256:512], hb_["qt"][:, 0:n], d_["Sbf"][:, :], True, False, [hT_["qt"], TD["Sbf"]], [PT[bk]])
                MM(PB[bk][0:n, 256:512], hb_["atm"][0:n, 0:n], vt, False, True, [hT_["atm"], Tvt[ti]], [PT[bk]])
            yield
            for h in H:
                d_, TD = gh[h], gh[h]["T"]
                sbk = 6 + h % 2
                STT(d_["S"][:, :], d_["S"][:, :], d_["hb"][pb_]["col"][:, 0:1], PB[sbk][:, (h // 2) * 256:(h // 2) * 256 + 256], ALU.mult, ALU.add,
                    [d_["hT"][pb_]["col"]], [PT[sbk], TD["S"]])
            yield
            for h in H:
                d_, TD = gh[h], gh[h]["T"]
                COPY("pool", d_["Sbf"][:, :], d_["S"][:, :], [TD["S"]], [TD["Sbf"]])
            yield
            for h in H:
                bk = 2 + h
                ACT(ojk2[0:n, :], PB[bk][0:n, 256:512], AF.Square, [], [PT[bk], Tojk2, Tgsc], accum_out=gsc[0:n, h:h + 1])
            ACT(gsc[0:n, 12:13], eps_col[0:n, 0:1], AF.Copy, [], [Tgsc])
            yield
            ACT(gsc[0:n, 4:8], gsc[0:n, 0:4], AF.Ln, [], [Tgsc], scale=1.0 / 256, bias=eps_col[0:n, 0:1])
            ACT(gsc[0:n, 8:12], gsc[0:n, 4:8], AF.Exp, [], [Tgsc], scale=-0.5)
            yield
            for h in H:
                d_, TD, bk = gh[h], gh[h]["T"], 2 + h
                STT(d_["on"][0:n, :], PB[bk][0:n, 256:512], gsc[0:n, 8 + h:9 + h], onB_bc[0:n, :], ALU.mult, ALU.mult,
                    [Tgsc], [PT[bk], TD["on"]])
            yield
            for h in H:
                d_, TD = gh[h], gh[h]["T"]
                sbk = 6 + h % 2
                o0 = (h // 2) * 512
                for half in range(2):
                    TR(pbf(sbk)[:, o0 + half * 128:o0 + half * 128 + n], d_["on"][0:n, half * 128:(half + 1) * 128], ident_b[0:n, 0:n],
                       [TD["on"]], [PT[sbk]])
            yield
            for h in H:
                sbk = 6 + h % 2
                o0 = (h // 2) * 512
                COPY("act" if h % 2 else "dve", obT[:, 2 * h:2 * h + 2, cs],
                     pbf(sbk)[:, o0:o0 + 256].rearrange("p (a c) -> p a c", a=2)[:, :, 0:n], [], [PT[sbk], Tqk[ti]])
            yield

        def dec_gla_prep():
            for h in range(4):
                MM(PB[0][:, h * 16:(h + 1) * 16], wgk[0:17, h * 128:(h + 1) * 128], lrbT[0:17, DEC0:NT], True, True, [Tlrb, Twgk], [PT[0]])
            ACT(gdec[:, :, :].rearrange("p a b -> p (a b)"), PB[0][:, 0:64], AF.Exp, [], [PT[0], Tgdec], scale=-1.0)
            ACT(gdec[:, :, :].rearrange("p a b -> p (a b)"), gdec[:, :, :].rearrange("p a b -> p (a b)"), AF.Ln, [], [Tgdec], bias=eps_col[:, 2:3])
            ACT(gdec[:, :, :].rearrange("p a b -> p (a b)"), gdec[:, :, :].rearrange("p a b -> p (a b)"), AF.Exp, [], [Tgdec], scale=-1.0 / 16.0)

        dgc = [0]

        def dec_gla(b):
            for h in range(4):
                DMA(Sg[h][:, :], I["sg"][b, h], writes=[TSg[h]], own=TSg[h])
            yield
            for h in range(4):
                s = dgc[0] % 2
                dgc[0] += 1
                bk = 1
                MM(PB[bk][:, 256:512], sel_b[0:16, b * 128:(b + 1) * 128], vtok[0:16, 17, h * 256:(h + 1) * 256], True, True, [Tsel, Tvt[17]], [PT[bk]])
                ACT(gtmp[:, :], PB[bk][:, 256:512], AF.Copy, [Tdecqb], [PT[bk], Tgtmp], scale=decqb[:, 4 + h, b:b + 1])
                yield
                STT(Sgn[s][:, :], Sg[h][:, :], gdec[:, h, b:b + 1], gtmp[:, :], ALU.mult, ALU.add, [TSg[h], Tgdec, Tgtmp], [TSgn[s]])
                yield
                for half in range(2):
                    MM(PB[bk][:, 256 + half:257 + half], Sgn[s][:, half * 128:(half + 1) * 128], decqb[:, h, b:b + 1], True, True, [TSgn[s], Tdecqb], [PT[bk]])
                COPY("act", odecb[:, 2 * h:2 * h + 2, b], PB[bk][:, 256:258], [], [PT[bk], Todecb])
                DMA(O["ng_s"][b, h], Sgn[s][:, :], reads=[TSgn[s]], own=TSgn[s])
                yield

        def dec_gla_finish():
            of = odecb[:, :, :].rearrange("p a b -> p (a b)")
            TT("dve", gtmp[:, 0:128], of, of, ALU.mult, [Todecb], [Tgtmp])
            for h in range(4):
                for half in range(2):
                    cidx = (2 * h + half) * 16
                    MM(PB[0][:, h * 16:(h + 1) * 16], ones_f[:, :], gtmp[:, cidx:cidx + 16], half == 0, half == 1, [Tgtmp, Tc], [PT[0]])
            ACT(gtmp[:, 128:192], PB[0][:, 0:64], AF.Ln, [], [PT[0], Tgtmp], scale=1.0 / 256, bias=eps_col[:, 0:1])
            ACT(gtmp[:, 128:192], gtmp[:, 128:192], AF.Exp, [], [Tgtmp], scale=-0.5)
            for h in range(4):
                for half in range(2):
                    STT(obT[:, 2 * h + half, DEC0:NT], odecb[:, 2 * h + half, :], oncol[:, 1 + half:2 + half], gtmp[:, 128 + h * 16:128 + (h + 1) * 16],
                        ALU.mult, ALU.mult, [Todecb, Tvec, Tgtmp], [Tqk[17]])

        print("B2 sbuf top", sb.top, "free", SB_END - sb.top)
        dec_gla_prep()
        interleave([gla_prep(0)])
        for ti in range(17):
            interleave([gla_rec(ti), gla_prep(ti + 1) if ti + 1 < 17 else None, dec_gla(ti - 1) if ti >= 1 else None])
        dec_gla_finish()
        for h in range(4):
            DMA(O["ng_p"][h], gh[h]["S"][:, :], reads=[gh[h]["T"]["S"]], own=gh[h]["T"]["S"])
        em.barrier()
        sb.release(mB)
        if stage <= 4:
            em.finish()
            print("sbuf peak", sb.peak, "ops", em.nops, "dsems", len(em.dsems))
            return nc

        mG = sb.mark()
        alloc_wsl()
        gs = [sb("gs%d" % i, [128, 512], BF16) for i in range(2)]
        Tgs = [T("gs%d" % i) for i in range(2)]
        gsc_ = [0]

        def job_gate(c0w, dst, dstT, base):
            def fn(slot):
                for j in range(4):
                    def epi(ps, c0, n, bk, j=j):
                        r = gsc_[0] % 2
                        gsc_[0] += 1
                        ACT(gs[r][:, 0:n], ps, AF.Silu, [], [PT[bk], Tgs[r]])
                        TT("dve", dst[:, base + j, c0:c0 + n], dst[:, base + j, c0:c0 + n], gs[r][:, 0:n], ALU.mult, [Tgs[r]],
                           [dstT[ti] for ti in tiles_in(c0, n)])
                    fm_proj(slot, j, 128, hT, ThT, 8, SLABS, epi, banks=(0, 1, 2, 3, 4, 5, 6, 7))
            return (w_in_blk(c0w, 512), 8, 512, fn)

        run_jobs([job_gate(ZA, oaT, Tq[0], 0), job_gate(RB, obT, Tqk, 0), job_gate(RB + 512, obT, Tqk, 4)])
        em.barrier()
        sb.release(mG)

        mixT_off = sb.top
        mixT = sb("mixT", [128, 8, NT], BF16)
        Tmix = [T("mix%d" % i) for i in range(18)]
        mixT_end = sb.top
        mC1 = sb.mark()
        wa = sb("wa", [128, 4, 1024], BF16)
        wb = sb("wb", [128, 8, 1024], BF16)
        wga = sb("wga", [128, 8, 1024], BF16)
        wgb = sb("wgb", [128, 8, 1024], BF16)
        Twc = [T("wc%d" % i) for i in range(4)]
        em.dma("pool", wa[:, :, :], I["w_a_out"].rearrange("(k p) n -> p k n", p=128), (), [Twc[0]], Twc[0])
        em.dma("pool", wga[:, :, :], w_in_blk(GA, 1024), (), [Twc[2]], Twc[2])
        em.dma("pool", wb[:, :, :], I["w_b_out"].rearrange("(k p) n -> p k n", p=128), (), [Twc[1]], Twc[1])
        em.dma("pool", wgb[:, :, :], w_in_blk(GB, 1024), (), [Twc[3]], Twc[3])
        sg_ = [[sb("sg%d_%d" % (i, k), [128, 512], F32) for k in range(3)] for i in range(2)]
        Tsg = [[T("sg%d_%d" % (i, k)) for k in range(3)] for i in range(2)]
        cct = 0
        for oc in range(8):
            for (c0, n) in SLABS:
                st_i = cct % 2
                cct += 1
                b0 = 4 * st_i
                tl = tiles_in(c0, n)
                ocs = slice(oc * 128, (oc + 1) * 128)
                for kc in range(4):
                    MM(PB[b0][:, 0:n], wa[:, kc, ocs], oaT[:, kc, c0:c0 + n], kc == 0, kc == 3, [Twc[0]] + [Tq[0][ti] for ti in tl], [PT[b0]])
                for kc in range(8):
                    MM(PB[b0 + 1][:, 0:n], wga[:, kc, ocs], hT[:, kc, c0:c0 + n], kc == 0, kc == 7, [Twc[2]] + [ThT[ti] for ti in tl], [PT[b0 + 1]])
                for kc in range(8):
                    MM(PB[b0 + 2][:, 0:n], wb[:, kc, ocs], obT[:, kc, c0:c0 + n], kc == 0, kc == 7, [Twc[1]] + [Tqk[ti] for ti in tl], [PT[b0 + 2]])
                for kc in range(8):
                    MM(PB[b0 + 3][:, 0:n], wgb[:, kc, ocs], hT[:, kc, c0:c0 + n], kc == 0, kc == 7, [Twc[3]] + [ThT[ti] for ti in tl], [PT[b0 + 3]])
                sA, sB, m1 = sg_[st_i]
                TA, TB, TM = Tsg[st_i]
                ACT(sA[:, 0:n], PB[b0 + 1][:, 0:n], AF.Sigmoid, [], [PT[b0 + 1], TA])
                ACT(sB[:, 0:n], PB[b0 + 3][:, 0:n], AF.Sigmoid, [], [PT[b0 + 3], TB])
                TT("dve", m1[:, 0:n], PB[b0][:, 0:n], sA[:, 0:n], ALU.mult, [TA], [PT[b0], TM])
                TT("dve", sB[:, 0:n], PB[b0 + 2][:, 0:n], sB[:, 0:n], ALU.mult, [], [PT[b0 + 2], TB])
                TT("pool", mixT[:, oc, c0:c0 + n], m1[:, 0:n], sB[:, 0:n], ALU.add, [TM, TB], [Tmix[ti] for ti in tl])
        em.barrier()
        sb.release(mC1)
        dump("mixT", mixT[:, :, :], [128, 8, NT], BF16)
        dump("oaT", oaT[:, :, :], [128, 4, NT], BF16)
        dump("obT", obT[:, :, :], [128, 8, NT], BF16)
        if stage <= 5:
            em.finish()
            print("sbuf peak", sb.peak, "ops", em.nops, "dsems", len(em.dsems))
            return nc

        sb.top = hT_off
        x1 = sb("x1", [128, 18, D], F32)
        Tx1 = [T("x1_%d" % i) for i in range(18)]
        x1_end = sb.top
        assert x1_end <= mixT_off
        sb.top = mixT_end
        mC2 = sb.mark()
        wo = sb("wo", [128, 8, 1024], BF16)
        Two = T("wo")
        em.dma("pool", wo[:, :, :], I["w_o"].rearrange("(k p) n -> p k n", p=128), (), [Two], Two)
        for ti in range(18):
            c0, n = TILES[ti]
            DMA(x1[0:n, ti, :], tile_src(ti), writes=[Tx1[ti]], own=Tx1[ti])
        cct = 0
        for ti in range(18):
            c0, n = TILES[ti]
            for half in range(2):
                bk = cct % 8
                cct += 1
                for kc in range(8):
                    MM(PB[bk][0:n, 0:512], mixT[:, kc, c0:c0 + n], wo[:, kc, half * 512:(half + 1) * 512], kc == 0, kc == 7, [Two, Tmix[ti]], [PT[bk]])
                TT("dve", x1[0:n, ti, half * 512:(half + 1) * 512], x1[0:n, ti, half * 512:(half + 1) * 512], PB[bk][0:n, 0:512], ALU.add, [], [PT[bk], Tx1[ti]])
        em.barrier()
        sb.top = x1_end
        dump("x1", x1[:, :, :], [128, 18, D], F32)
        if stage <= 6:
            em.finish()
            print("sbuf peak", sb.peak, "ops", em.nops, "dsems", len(em.dsems))
            return nc

        h2T = sb("h2T", [128, 8, NT], BF16)
        Th2 = [T("h2T%d" % i) for i in range(18)]
        mD0 = sb.mark()
        nf_bc = sb("nf_bc", [128, D], F32)
        Tnf = T("nf")
        DMA(nf_bc[:, :], I["norm_ffn"][0:1, :].partition_broadcast(128), writes=[Tnf], own=Tnf)

        def get_x1(ti):
            c0, n = TILES[ti]
            return x1[0:n, ti, :], [Tx1[ti]]
        norm_tiles(h2T, Th2, get_x1, nf_bc, Tnf, 0)
        em.barrier()
        dump("nf", nf_bc[:, :], [128, D], F32)
        sb.release(mD0)
        dump("h2T", h2T[:, :, :], [128, 8, NT], BF16)

        mD1 = sb.mark()
        NF = 22
        wcfT = sb("wcfT", [128, NF, 4], F32)
        Twcf = T("wcfT")
        prevF = sb("prevF", [128, NF, 2, 16], F32)
        TprevF = T("prevF")
        lastF = sb("lastF", [128, NF, 2], F32)
        TlastF = T("lastF")
        decF = sb("decF", [128, NF, 16], F32)
        TdecF = T("decF")
        mk = sb.mark()
        wcf_sb = sb("wcf_sb", [4, DFF], F32)
        Twcfs = T("wcfs")
        DMA(wcf_sb[0:3, :], I["w_conv_f"][:, :], writes=[Twcfs], own=Twcfs)
        DMA(wcf_sb[3:4, :], I["b_conv_f"][0:1, :], writes=[Twcfs], own=Twcfs)
        sfc_sb = sb("sfc_sb", [16, 2, DFF], F32)
        Tsfc = T("sfc")
        DMA(sfc_sb[:, :, :], I["sfc"][:, :, :], writes=[Tsfc], own=Tsfc)
        DMA(O["nfc_s"][:, 0, :], sfc_sb[:, 1, :], reads=[Tsfc], own=Tsfc)
        for fc in range(NF):
            TR(PB[0][:, fc * 4:fc * 4 + 4], wcf_sb[0:4, fc * 128:(fc + 1) * 128], ident_f[0:4, 0:4], [Twcfs, Tc], [PT[0]])
        COPY("dve", wcfT[:, :, :].rearrange("p a b -> p (a b)"), PB[0][:, 0:NF * 4], [], [PT[0], Twcf])
        for fc in range(NF):
            bk = 1 + fc % 2
            for r in range(2):
                TR(PB[bk][:, r * 16:r * 16 + 16], sfc_sb[0:16, r, fc * 128:(fc + 1) * 128], ident_f[0:16, 0:16], [Tsfc, Tc], [PT[bk]])
            COPY("act", prevF[:, fc, :, :].rearrange("p a b -> p (a b)"), PB[bk][:, 0:32], [], [PT[bk], TprevF])
        em.barrier()
        sb.release(mk)
        mD1b = sb.mark()
        alloc_wsl()
        preF = [sb("preF%d" % i, [128, 2 + NT], F32) for i in range(2)]
        TpreF = [T("preF%d" % i) for i in range(2)]
        for i in range(2):
            MEMSET("pool", preF[i][:, 0:2], 0.0, [TpreF[i]])
        cvf = sb("cvf", [128, NT], F32)
        Tcvf = T("cvf")
        gu_l = [sb("gu%d" % i, [128, NT], BF16) for i in range(2)]
        Tgu_l = [T("gu%d" % i) for i in range(2)]
        GRP = 6
        aT = sb("aT", [128, GRP, NT], BF16)
        TaT = [T("aT%d" % i) for i in range(18)]

        def job_pair(p, g0):
            def fn(slot):
                for jj in range(2):
                    fc = 2 * p + jj
                    ps_ = fc % 2

                    def epi_u(ps, c0, n, bk, ps_=ps_):
                        COPY("act", preF[ps_][:, 2 + c0:2 + c0 + n], ps, [], [PT[bk], TpreF[ps_]])
                    fm_proj(slot, jj, 128, h2T, Th2, 8, SLABS, epi_u, banks=(0, 1, 2, 3, 4, 5, 6, 7))
                for jj in range(2):
                    fc = 2 * p + jj
                    ps_ = fc % 2
                    gu, Tgu = gu_l[ps_], Tgu_l[ps_]
                    pr = preF[ps_]
                    rdp = [TpreF[ps_], Twcf]
                    TS("dve", cvf[:, 0:NPC], pr[:, 0:NPC], wcfT[:, fc, 0:1], ALU.mult, wcfT[:, fc, 3:4], ALU.add, reads=rdp, writes=[Tcvf])
                    for i in range(1, 3):
                        STT(cvf[:, 0:NPC], pr[:, i:i + NPC], wcfT[:, fc, i:i + 1], cvf[:, 0:NPC], ALU.mult, ALU.add, rdp, [Tcvf])
                    TS("dve", cvf[:, DEC0:NT], pr[:, 2 + DEC0:2 + NT], wcfT[:, fc, 2:3], ALU.mult, wcfT[:, fc, 3:4], ALU.add, reads=rdp, writes=[Tcvf])
                    for i in range(2):
                        STT(cvf[:, DEC0:NT], prevF[:, fc, i, :], wcfT[:, fc, i:i + 1], cvf[:, DEC0:NT], ALU.mult, ALU.add, [TprevF, Twcf], [Tcvf])
                    COPY("pool", lastF[:, fc, :], pr[:, 2 + NPC - 2:2 + NPC], [TpreF[ps_]], [TlastF])
                    COPY("pool", decF[:, fc, :], pr[:, 2 + DEC0:2 + NT], [TpreF[ps_]], [TdecF])
                    ACT(gu[:, :], cvf[:, :], AF.Gelu_apprx_tanh, [Tcvf], [Tgu])

                    def epi_g(ps, c0, n, bk, fc=fc, gu=gu, Tgu=Tgu):
                        TT("dve", aT[:, fc - g0, c0:c0 + n], gu[:, c0:c0 + n], ps, ALU.mult, [Tgu], [PT[bk]] + [TaT[ti] for ti in tiles_in(c0, n)])
                    fm_proj(slot, 2 + jj, 128, h2T, Th2, 8, SLABS, epi_g, banks=(0, 1, 2, 3, 4, 5, 6, 7))
            s_u = I["w_ffn_in"][:, 2 * p * 128:2 * p * 128 + 256].rearrange("(k p) n -> p k n", p=128)
            s_g = I["w_ffn_in"][:, DFF + 2 * p * 128:DFF + 2 * p * 128 + 256].rearrange("(k p) n -> p k n", p=128)
            return ((s_u, s_g), 8, 256, fn)

        dct = [0]

        fst = sb("fst", [128, 18, 4], F32)
        Tfst_l = [T("fst%d" % i) for i in range(18)]
        nfin = gu_l[0][:, 0:2 * D].bitcast(F32)
        Tnfin = Tgu_l[0]

        def final_norm(ti):
            c0, n = TILES[ti]
            if ti == 0:
                return
            s_ = ti % 2
            Tfst = Tfst_l[ti]
            xa = x1[0:n, ti, :]
            yb = preF[s_][0:n, 0:D]
            ACT(cvf[0:n, 0:D], xa, AF.Square, [Tx1[ti]], [Tcvf, Tfst], accum_out=fst[0:n, ti, 0:1])
            ACT(fst[0:n, ti, 3:4], eps_col[0:n, 0:1], AF.Copy, [], [Tfst])
            ACT(fst[0:n, ti, 1:2], fst[0:n, ti, 0:1], AF.Ln, [], [Tfst], scale=1.0 / D, bias=eps_col[0:n, 0:1])
            ACT(fst[0:n, ti, 2:3], fst[0:n, ti, 1:2], AF.Exp, [], [Tfst], scale=-0.5)
            STT(yb, xa, fst[0:n, ti, 2:3], nfin[0:n, :], ALU.mult, ALU.mult, [Tx1[ti], Tfst, Tnfin], [TpreF[s_]])
            if ti == 17:
                DMA(O["y_s"][:, :], yb, reads=[TpreF[s_]], own=TpreF[s_])
            else:
                DMA(O["y_p"][128 * (ti - 1):128 * ti, :], yb, reads=[TpreF[s_]], own=TpreF[s_])

        def job_down(g0, gn, half):
            last = (g0 + gn == NF) and half == 1

            def fn(slot):
                if last:
                    DMA(nfin, I["norm_final"][0:1, :].partition_broadcast(128), writes=[Tnfin], own=Tnfin)
                for ti in range(18):
                    c0, n = TILES[ti]
                    bk = dct[0] % 8
                    dct[0] += 1
                    for k in range(gn):
                        MM(PB[bk][0:n, 0:512], aT[:, k, c0:c0 + n], wsl[slot][:, k, 0:512], k == 0, k == gn - 1, [Tw[slot], TaT[ti]], [PT[bk]])
                    TT("dve", x1[0:n, ti, half * 512:(half + 1) * 512], x1[0:n, ti, half * 512:(half + 1) * 512], PB[bk][0:n, 0:512], ALU.add,
                       [], [PT[bk], Tx1[ti]])
                    if last:
                        final_norm(ti)
            src = I["w_ffn_out"][g0 * 128:(g0 + gn) * 128, half * 512:(half + 1) * 512].rearrange("(k p) n -> p k n", p=128)
            return (src, gn, 512, fn)

        print("D1 sbuf top", sb.top, "free", SB_END - sb.top)
        jobs = []
        for g0 in range(0, NF, GRP):
            gn = min(GRP, NF - g0)
            for p in range(g0 // 2, (g0 + gn) // 2):
                jobs.append(job_pair(p, g0))
            jobs.append(job_down(g0, gn, 0))
            jobs.append(job_down(g0, gn, 1))
        run_jobs(jobs)
        em.barrier()
        sb.release(mD1b)

        mk = sb.mark()
        fo = sb("fo", [16, DFF], F32)
        Tfo = T("fo")
        fo2 = sb("fo2", [2, DFF], F32)
        Tfo2 = T("fo2")
        for fc in range(NF):
            bk = 4 + fc % 2
            TR(PB[bk][0:16, 0:128], decF[:, fc, :], ident_f[:, :], [TdecF, Tc], [PT[bk]])
            COPY("dve", fo[0:16, fc * 128:(fc + 1) * 128], PB[bk][0:16, 0:128], [], [PT[bk], Tfo])
            TR(PB[bk][0:2, 128:256], lastF[:, fc, :], ident_f[:, :], [TlastF, Tc], [PT[bk]])
            COPY("act", fo2[0:2, fc * 128:(fc + 1) * 128], PB[bk][0:2, 128:256], [], [PT[bk], Tfo2])
        DMA(O["nfc_s"][:, 1, :], fo[0:16, :], reads=[Tfo], own=Tfo)
        DMA(O["nfc_p"][:, :], fo2[0:2, :], reads=[Tfo2], own=Tfo2)
        em.barrier()
        sb.release(mk)
        sb.release(mD1)

        em.finish()
        print("sbuf peak", sb.peak, "ops", em.nops, "dsems", len(em.dsems))
    return nc


_NC_CACHE = {}


def _core_inputs(inp, c):
    f = lambda a: np.ascontiguousarray(a, dtype=np.float32)
    m = {
        "xp": f(inp["x_prompt"][c]), "meta": f(inp["meta_tokens"]), "xs": f(inp["x_sample"][16 * c:16 * c + 16, 0]),
        "sd": f(inp["state_delta"][0, 16 * c:16 * c + 16]), "sdc": f(inp["state_delta_conv"][0, 16 * c:16 * c + 16]),
        "sg": f(inp["state_gla"][0, 16 * c:16 * c + 16]), "sfc": f(inp["state_ffn_conv"][0, 16 * c:16 * c + 16]),
        "w_in": f(inp["w_in"][0]), "w_conv_a": f(inp["w_conv_a"][0]), "a_log": f(inp["a_log"]), "dt_bias": f(inp["dt_bias"]),
        "w_gk2": f(inp["w_gk2"][0]), "b_gk": f(inp["b_gk"]), "onorm_a": f(inp["onorm_a"]), "onorm_b": f(inp["onorm_b"]),
        "w_a_out": f(inp["w_a_out"][0]), "w_b_out": f(inp["w_b_out"][0]), "w_o": f(inp["w_o"][0]),
        "norm_mix": f(inp["norm_mix"]), "norm_ffn": f(inp["norm_ffn"]), "w_ffn_in": f(inp["w_ffn_in"][0]),
        "w_conv_f": f(inp["w_conv_f"][0]), "b_conv_f": f(inp["b_conv_f"]), "w_ffn_out": f(inp["w_ffn_out"][0]),
        "norm_final": f(inp["norm_final"]).reshape(1, D),
    }
    return m


def kernel(**inp):
    stage = int(os.environ.get("KSTAGE", "99"))
    if stage not in _NC_CACHE:
        _NC_CACHE[stage] = build(stage)
    nc = _NC_CACHE[stage]
    cst = _consts()
    in_maps = []
    for c in range(8):
        m = _core_inputs(inp, c)
        m.update(cst)
        in_maps.append(m)
    res = run_bass_kernel_spmd(nc, in_maps, core_ids=list(range(8)))
    R = res.results
    if os.environ.get("KDBG"):
        for k in R[0]:
            if k.startswith("dbg_"):
                np.save("/tmp/%s.npy" % k, np.stack([np.asarray(R[c][k]).astype(np.float32) for c in range(8)]))
    cat = lambda k: np.stack([R[c][k] for c in range(8)], 0)
    y_p = cat("y_p")
    y_s = np.concatenate([R[c]["y_s"] for c in range(8)], 0)[:, None, :]
    nd_p = cat("nd_p")[None]
    ndc_p = cat("ndc_p")[None]
    ng_p = cat("ng_p")[None]
    nfc_p = cat("nfc_p")[None]
    nd_s = np.concatenate([R[c]["nd_s"] for c in range(8)], 0)[None]
    ndc_s = np.concatenate([R[c]["ndc_s"] for c in range(8)], 0)[None]
    ng_s = np.concatenate([R[c]["ng_s"] for c in range(8)], 0)[None]
    nfc_s = np.concatenate([R[c]["nfc_s"] for c in range(8)], 0)[None]
    return (y_p, y_s, nd_p, ndc_p, ng_p, nfc_p, nd_s, ndc_s, ng_s, nfc_s)
```

```python
import os
import numpy as np
import concourse.bass as bass
import concourse.mybir as mybir
from concourse.bass_utils import run_bass_kernel_spmd
from contextlib import ExitStack

F32 = mybir.dt.float32
BF16 = mybir.dt.bfloat16
AF = mybir.ActivationFunctionType
ALU = mybir.AluOpType

D = 1024
NPC = 2064
NT = 2080
DEC0 = 2064
TILES = [(0, 16)] + [(16 + 128 * i, 128) for i in range(16)] + [(2064, 16)]
SLABS = [(416 * s, 416) for s in range(5)]
QA, KA, VA, ZA, BA, AA, QB, KB, VB, RB, LRB, GA, GB, WEND = 0, 512, 1024, 1536, 2048, 2052, 2056, 2568, 3080, 4104, 5128, 5144, 6168, 7192
DFF = 2816
EPS = 1e-6
SB_BASE = 16640
SB_END = 229376


class T:
    __slots__ = ("name", "w", "r", "dsem")

    def __init__(self, name=""):
        self.name = name
        self.w = None
        self.r = {}
        self.dsem = None


class Em:
    ENG = ("pe", "act", "dve", "pool", "sp")

    def __init__(self, nc, stack):
        self.nc = nc
        self.stack = stack
        self.eng = {"pe": nc.tensor, "act": nc.scalar, "dve": nc.vector, "pool": nc.gpsimd, "sp": nc.sync}
        self.sem = {k: stack.enter_context(nc.semaphore("s_" + k)) for k in self.ENG}
        self.cnt = {k: 0 for k in self.ENG}
        self.known = {k: {} for k in self.ENG}
        self.dsems = []
        self.nops = 0

    def _wait(self, E, deps):
        kn = self.known[E]
        best = {}
        for d in deps:
            if d is None:
                continue
            key, sem, val = d
            if E == "pe" and key == "pe":
                continue
            if kn.get(key, 0) >= val:
                continue
            if key not in best or best[key][2] < val:
                best[key] = d
        for key, (k2, sem, val) in best.items():
            self.eng[E].wait_ge(sem, val)
            kn[key] = val

    def _deps(self, reads, writes):
        deps = []
        for t in reads:
            deps.append(t.w)
        for t in writes:
            deps.append(t.w)
            deps.extend(t.r.values())
        return deps

    def op(self, E, fn, reads=(), writes=()):
        self._wait(E, self._deps(reads, writes))
        inst = fn(self.eng[E])
        self.cnt[E] += 1
        inst.then_inc(self.sem[E], 1)
        tok = (E, self.sem[E], self.cnt[E])
        for t in reads:
            t.r[E] = tok
        for t in writes:
            t.w = tok
            t.r = {}
        self.nops += 1
        return inst

    def _dsem(self, t):
        if t.dsem is None:
            nm = "d%d" % len(self.dsems)
            s = self.stack.enter_context(self.nc.semaphore(nm))
            t.dsem = [s, 0, nm]
            self.dsems.append(t.dsem)
        return t.dsem

    def dma(self, Q, out, in_, reads=(), writes=(), own=None):
        self._wait(Q, self._deps(reads, writes))
        ds = self._dsem(own)
        inst = self.eng[Q].dma_start(out=out, in_=in_)
        ds[1] += 16
        inst.then_inc(ds[0], 16)
        tok = (ds[2], ds[0], ds[1])
        for t in reads:
            t.r[ds[2]] = tok
        for t in writes:
            t.w = tok
            t.r = {}
        self.nops += 1
        return inst

    def barrier(self):
        deps = [(k, self.sem[k], self.cnt[k]) for k in self.ENG if self.cnt[k] > 0]
        deps += [(d[2], d[0], d[1]) for d in self.dsems if d[1] > 0]
        self._wait("sp", deps)
        inst = self.eng["sp"].nop()
        self.cnt["sp"] += 1
        inst.then_inc(self.sem["sp"], 1)
        tok = ("sp", self.sem["sp"], self.cnt["sp"])
        for E in self.ENG:
            if E == "sp":
                continue
            self._wait(E, [tok])
            for d in deps:
                self.known[E][d[0]] = max(self.known[E].get(d[0], 0), d[2])

    def finish(self):
        deps = [(k, self.sem[k], self.cnt[k]) for k in self.ENG if self.cnt[k] > 0 and k != "sp"]
        deps += [(d[2], d[0], d[1]) for d in self.dsems if d[1] > 0]
        self._wait("sp", deps)


class SbAlloc:
    def __init__(self, nc):
        self.nc = nc
        self.top = SB_BASE
        self.n = 0
        self.peak = SB_BASE

    def __call__(self, name, shape, dt):
        es = 2 if dt == BF16 else 4
        nb = es
        for s in shape[1:]:
            nb *= s
        nb = (nb + 63) // 64 * 64
        off = self.top
        assert off + nb <= SB_END, ("SBUF overflow", name, off, nb)
        self.top += nb
        self.peak = max(self.peak, self.top)
        self.n += 1
        return self.nc.alloc_sbuf_tensor_at("%s_%d" % (name, self.n), list(shape), dt, offset=off)

    def mark(self):
        return self.top

    def release(self, m):
        self.top = m


def _consts():
    i = np.arange(128)
    same = (i[:, None] // 64) == (i[None, :] // 64)
    c = {}
    c["c_ident"] = np.eye(128, dtype=np.float32)
    c["c_tribd"] = (same & (i[:, None] <= i[None, :])).astype(np.float32)
    c["c_blk"] = same.astype(np.float32)
    c["c_mbig"] = np.where(same & (i[None, :] < i[:, None]), 0.0, 1e30).astype(np.float32)
    c["c_mneg"] = np.where(same & (i[None, :] >= i[:, None]), 0.0, -1e30).astype(np.float32)
    c["c_tris"] = np.where(i[:, None] <= i[None, :], -1.0 / 16.0, 0.0).astype(np.float32)
    c["c_trisuf"] = np.where(i[:, None] > i[None, :], -1.0 / 16.0, 0.0).astype(np.float32)
    c["c_m01"] = (i[None, :] >= i[:, None]).astype(np.float32)
    sel = np.zeros((16, 16, 128), np.float32)
    for b in range(16):
        sel[b, b, :] = 1.0
    c["c_sel"] = sel.reshape(16, 2048)
    return c


IN_SHAPES = {
    "xp": [2048, D], "meta": [16, D], "xs": [16, D],
    "sd": [16, 4, 128, 128], "sdc": [16, 3, 1536], "sg": [16, 4, 128, 256], "sfc": [16, 2, DFF],
    "w_in": [D, WEND], "w_conv_a": [4, 1536], "a_log": [1, 4], "dt_bias": [1, 4], "w_gk2": [16, 512],
    "b_gk": [1, 512], "onorm_a": [1, 128], "onorm_b": [1, 256], "w_a_out": [512, D], "w_b_out": [D, D],
    "w_o": [D, D], "norm_mix": [1, D], "norm_ffn": [1, D], "w_ffn_in": [D, 2 * DFF], "w_conv_f": [3, DFF],
    "b_conv_f": [1, DFF], "w_ffn_out": [DFF, D], "norm_final": [1, D],
    "c_ident": [128, 128], "c_tribd": [128, 128], "c_blk": [128, 128], "c_mbig": [128, 128], "c_mneg": [128, 128],
    "c_tris": [128, 128], "c_trisuf": [128, 128], "c_m01": [128, 128], "c_sel": [16, 2048],
}
OUT_SHAPES = {
    "y_p": [2048, D], "y_s": [16, D], "nd_p": [4, 128, 128], "ndc_p": [3, 1536], "ng_p": [4, 128, 256],
    "nfc_p": [2, DFF], "nd_s": [16, 4, 128, 128], "ndc_s": [16, 3, 1536], "ng_s": [16, 4, 128, 256], "nfc_s": [16, 2, DFF],
}


def build(stage=99, dbg=None):
    nc = bass.Bass("TRN2", target_bir_lowering=False)
    I = {k: nc.dram_tensor(k, v, F32, kind="ExternalInput").ap() for k, v in IN_SHAPES.items()}
    O = {k: nc.dram_tensor(k, v, F32, kind="ExternalOutput").ap() for k, v in OUT_SHAPES.items()}
    dbg_out = {}
    DBG = os.environ.get("KDBG", "").split(",")

    def dump(name, tensor, shape, dt):
        if name not in DBG:
            return
        o_ = nc.dram_tensor("dbg_" + name, list(shape), dt, kind="ExternalOutput").ap()
        em_[0].barrier()
        t_ = T("dbg" + name)
        em_[0].dma("sp", o_, tensor, (), (), t_)
        em_[0].barrier()
    em_ = [None]
    st = ExitStack()
    with st:
        em = Em(nc, st)
        em_[0] = em
        sb = SbAlloc(nc)
        PB = [st.enter_context(nc.psum_tensor("pb%d" % b, [128, 512], F32)) for b in range(8)]
        PT = [T("pb%d" % b) for b in range(8)]

        def pbf(b):
            return PB[b][:, :].bitcast(BF16)

        def ACT(out, in_, func, reads=(), writes=(), **kw):
            return em.op("act", lambda e: e.activation(out=out, in_=in_, func=func, **kw), reads, writes)

        def COPY(E, out, in_, reads=(), writes=()):
            if E == "act":
                return em.op("act", lambda e: e.activation(out=out, in_=in_, func=AF.Copy), reads, writes)
            return em.op(E, lambda e: e.tensor_copy(out=out, in_=in_), reads, writes)

        def TT(E, out, in0, in1, op, reads=(), writes=()):
            return em.op(E, lambda e: e.tensor_tensor(out=out, in0=in0, in1=in1, op=op), reads, writes)

        def TS(E, out, in0, s1, op0, s2=None, op1=None, reads=(), writes=()):
            if op1 is None:
                return em.op(E, lambda e: e.tensor_scalar(out=out, in0=in0, scalar1=s1, scalar2=None, op0=op0), reads, writes)
            return em.op(E, lambda e: e.tensor_scalar(out=out, in0=in0, scalar1=s1, scalar2=s2, op0=op0, op1=op1), reads, writes)

        def STT(out, in0, scalar, in1, op0, op1, reads=(), writes=()):
            return em.op("dve", lambda e: e.scalar_tensor_tensor(out=out, in0=in0, scalar=scalar, in1=in1, op0=op0, op1=op1), reads, writes)

        def MM(out, lhsT, rhs, start=True, stop=True, reads=(), writes=()):
            return em.op("pe", lambda e: e.matmul(out, lhsT=lhsT, rhs=rhs, start=start, stop=stop), reads, writes)

        def TR(out, in_, ident, reads=(), writes=()):
            return em.op("pe", lambda e: e.transpose(out, in_, ident), reads, writes)

        def MEMSET(E, ap, val, writes=()):
            return em.op(E, lambda e: e.memset(ap, val), (), writes)

        dq = ["sp", "act"]
        dqi = [0]

        def DMA(out, in_, reads=(), writes=(), own=None, q=None):
            if q is None:
                q = dq[dqi[0] % len(dq)]
                dqi[0] += 1
            return em.dma(q, out, in_, reads, writes, own)

        cn = {}
        Tc = T("consts")
        for k in ["c_ident", "c_tribd", "c_blk", "c_mbig", "c_mneg", "c_tris", "c_trisuf", "c_m01"]:
            cn[k] = sb(k, [128, 128], F32)
            DMA(cn[k][:, :], I[k][:, :], writes=[Tc], own=Tc)
        ident_f = cn["c_ident"]
        ident_b = sb("ident_b", [128, 128], BF16)
        COPY("dve", ident_b[:, :], ident_f[:, :], [Tc], [Tc])
        ones_f = sb("ones_f", [128, 128], F32)
        MEMSET("pool", ones_f[:, :], 1.0, [Tc])
        m01_b = sb("m01_b", [128, 128], BF16)
        COPY("dve", m01_b[:, :], cn["c_m01"][:, :], [Tc], [Tc])

        eps_col = sb("eps_col", [128, 4], F32)
        MEMSET("pool", eps_col[:, 0:1], EPS, [Tc])
        MEMSET("pool", eps_col[:, 1:2], float(np.log(128.0 ** -0.5)), [Tc])
        MEMSET("pool", eps_col[:, 2:3], 1.0, [Tc])
        vec = sb("vecs", [128, 16], F32)
        Tvec = T("vec")
        DMA(vec[:, 0:4], I["dt_bias"][0:1, :].partition_broadcast(128), writes=[Tvec], own=Tvec)
        DMA(vec[:, 4:8], I["a_log"][0:1, :].partition_broadcast(128), writes=[Tvec], own=Tvec)
        ACT(vec[:, 4:8], vec[:, 4:8], AF.Exp, [Tvec], [Tvec])
        TS("dve", vec[:, 4:8], vec[:, 4:8], -1.0, ALU.mult, reads=[Tvec], writes=[Tvec])
        onA_bc = sb("onA_bc", [128, 128], F32)
        DMA(onA_bc[:, :], I["onorm_a"][0:1, :].partition_broadcast(128), writes=[Tvec], own=Tvec)
        onB_bc = sb("onB_bc", [128, 256], F32)
        DMA(onB_bc[:, :], I["onorm_b"][0:1, :].partition_broadcast(128), writes=[Tvec], own=Tvec)
        oncol = sb("oncol", [128, 4], F32)
        with nc.allow_non_contiguous_dma(reason="tiny column loads"):
            DMA(oncol[:, 0:1], I["onorm_a"].rearrange("o d -> d o"), writes=[Tvec], own=Tvec)
            DMA(oncol[:, 1:3], I["onorm_b"].rearrange("o (c d) -> d (o c)", c=2), writes=[Tvec], own=Tvec)

        wcaT = sb("wcaT", [128, 12, 4], F32)
        Twca = T("wca")
        m0 = sb.mark()
        wca_sb = sb("wca_sb", [4, 1536], F32)
        Ttmp = T("tmp")
        DMA(wca_sb[:, :], I["w_conv_a"][:, :], writes=[Ttmp], own=Ttmp)
        for cc in range(12):
            TR(PB[0][:, cc * 4:cc * 4 + 4], wca_sb[0:4, cc * 128:(cc + 1) * 128], ident_f[0:4, 0:4], [Ttmp, Tc], [PT[0]])
        COPY("dve", wcaT[:, :, :].rearrange("p a b -> p (a b)"), PB[0][:, 0:48], [], [PT[0], Twca])
        sb.release(m0)

        F32R = mybir.dt.float32r
        KR_GBC = os.environ.get("KR_GBC", "1") == "1"
        KR_GLA = os.environ.get("KR_GLA", "1") == "1"
        KR_A1 = os.environ.get("KR_A1", "1") == "1"
        ones_r = sb("ones_r", [128, 128], F32R)
        COPY("dve", ones_r[:, :], ones_f[:, :], [Tc], [Tc])
        tris_r = sb("tris_r", [128, 128], F32R)
        COPY("dve", tris_r[:, :], cn["c_tris"][:, :], [Tc], [Tc])
        trisuf_r = sb("trisuf_r", [128, 128], F32R)
        COPY("dve", trisuf_r[:, :], cn["c_trisuf"][:, :], [Tc], [Tc])
        tribd_r0 = sb("tribd_r0", [128, 128], F32R)
        COPY("dve", tribd_r0[:, :], cn["c_tribd"][:, :], [Tc], [Tc])
        em.barrier()

        hT_off = sb.top
        hT = sb("hT", [128, 8, NT], BF16)
        ThT = [T("hT%d" % i) for i in range(18)]

        def tile_src(ti):
            if ti == 0:
                return I["meta"][0:16, :]
            if ti == 17:
                return I["xs"][0:16, :]
            return I["xp"][128 * (ti - 1):128 * ti, :]

        def run_window(make_gen, items, width):
            items = list(items)
            active = []
            nxt = 0
            while active or nxt < len(items):
                while len(active) < width and nxt < len(items):
                    active.append(make_gen(items[nxt]))
                    nxt += 1
                keep = []
                for g in active:
                    try:
                        next(g)
                        keep.append(g)
                    except StopIteration:
                        pass
                active = keep

        def norm_tiles(dstT, dstTT, get_x, nwt, Tnwt, bank0):
            NB = 4
            mk = sb.mark()
            hb = [sb("hb%d" % i, [128, D], BF16) for i in range(NB)]
            Thb = [T("hb%d" % i) for i in range(NB)]
            jk = sb("jk", [128, D], BF16)
            Tjk = T("jk")
            st_ = sb("nst", [128, 3, 18], F32)
            Tst = T("nst")
            MEMSET("pool", st_[:, 0, :], 1.0, [Tst])
            xs = [get_x(ti) for ti in range(18)]
            for ti in range(18):
                c0, n = TILES[ti]
                xa, xr = xs[ti]
                ACT(jk[0:n, :], xa, AF.Square, xr, [Tjk, Tst], accum_out=st_[0:n, 0, ti:ti + 1])
            ACT(st_[:, 2, 0:1], eps_col[:, 0:1], AF.Copy, [], [Tst])
            ACT(st_[:, 1, :], st_[:, 0, :], AF.Ln, [], [Tst], scale=1.0 / D, bias=eps_col[:, 0:1])
            ACT(st_[:, 2, :], st_[:, 1, :], AF.Exp, [], [Tst], scale=-0.5)

            def gen(ti):
                c0, n = TILES[ti]
                xa, xr = xs[ti]
                s = ti % NB
                STT(hb[s][0:n, :], xa, st_[0:n, 2, ti:ti + 1], nwt[0:n, :], ALU.mult, ALU.mult, xr + [Tst, Tnwt], [Thb[s]])
                yield
                bk = bank0 + s
                for kc in range(8):
                    TR(pbf(bk)[:, kc * 128:kc * 128 + n], hb[s][0:n, kc * 128:(kc + 1) * 128], ident_b[0:n, 0:n], [Thb[s]], [PT[bk]])
                yield
                COPY("act" if ti % 2 else "dve", dstT[:, :, c0:c0 + n],
                     pbf(bk).rearrange("p (k c) -> p k c", k=8)[:, :, 0:n], [], [PT[bk], dstTT[ti]])
                yield
            run_window(gen, range(18), NB)
            sb.release(mk)


        mP0 = sb.mark()
        nw_bc = sb("nw_bc", [128, D], F32)
        Tnw = T("nw")
        DMA(nw_bc[:, :], I["norm_mix"][0:1, :].partition_broadcast(128), writes=[Tnw], own=Tnw)
        xst = [sb("xst%d" % i, [128, D], F32) for i in range(18)]
        Txst = [T("xst%d" % i) for i in range(18)]

        def get_x0(ti):
            c0, n = TILES[ti]
            s = ti
            DMA(xst[s][0:n, :], tile_src(ti), writes=[Txst[s]], own=Txst[s])
            return xst[s][0:n, :], [Txst[s]]

        norm_tiles(hT, ThT, get_x0, nw_bc, Tnw, 0)
        em.barrier()
        sb.release(mP0)

        wsl = [None] * 3
        Tw = [None] * 3
        wctr = [0]

        def alloc_wsl():
            for i in range(3):
                wsl[i] = sb("wsl%d" % i, [128, 8, 512], BF16)
                Tw[i] = T("wsl%d" % i)

        def wload(src3, nk, ncol):
            s = wctr[0] % 3
            wctr[0] += 1
            if isinstance(src3, tuple):
                for i_, sr in enumerate(src3):
                    em.dma("pool", wsl[s][:, 0:nk, i_ * ncol:(i_ + 1) * ncol], sr, (), [Tw[s]], Tw[s])
            else:
                em.dma("pool", wsl[s][:, 0:nk, 0:ncol], src3, (), [Tw[s]], Tw[s])
            return s

        def w_in_blk(c0, ncol):
            return I["w_in"][:, c0:c0 + ncol].rearrange("(k p) n -> p k n", p=128)

        def run_jobs(jobs):
            slots = {}
            for i in range(min(2, len(jobs))):
                slots[i] = wload(*jobs[i][0:3])
            for i, jb in enumerate(jobs):
                if i + 2 < len(jobs):
                    slots[i + 2] = wload(*jobs[i + 2][0:3])
                jb[3](slots[i])

        def tiles_in(c0, n):
            return [ti for ti, (a, m) in enumerate(TILES) if a < c0 + n and c0 < a + m]

        pbr = [0]

        def fm_proj(slot, j, ncols, src, srcT, nk, slabs, epi, banks=(0, 1, 2, 3)):
            for (c0, n) in slabs:
                bk = banks[pbr[0] % len(banks)]
                pbr[0] += 1
                rd = [Tw[slot]] + [srcT[ti] for ti in tiles_in(c0, n)]
                for kc in range(nk):
                    MM(PB[bk][0:ncols, 0:n], wsl[slot][:, kc, j * 128:j * 128 + ncols], src[:, kc, c0:c0 + n],
                       kc == 0, kc == nk - 1, rd, [PT[bk]])
                epi(PB[bk][0:ncols, 0:n], c0, n, bk)

        oaT = sb("oaT", [128, 4, NT], BF16)
        mA = sb.mark()
        kvT = sb("kvT", [128, 8, NT], BF16)

        class _QKV:
            def __getitem__(self, key):
                p, cc, cols = key
                if isinstance(cc, slice):
                    assert cc.start == 0 and cc.stop == 4
                    return oaT[p, cc, cols]
                if cc < 4:
                    return oaT[p, cc, cols]
                return kvT[p, cc - 4, cols]
        qkvT = _QKV()
        Tq = [[T("qkv%d_%d" % (g, i)) for i in range(18)] for g in range(3)]
        decq = sb("decq", [128, 12, 16], F32)
        Tdecq = T("decq")
        lastpre = sb("lastpre", [128, 12, 3], F32)
        Tlast = T("lastpre")
        decpre = sb("decpre", [128, 12, 16], F32)
        Tdecpre = T("decpre")
        bg_all = sb("bg_all", [128, 18, 8], F32)
        Tbg = T("bg_all")
        MEMSET("pool", bg_all[:, :, :].rearrange("p a b -> p (a b)"), 0.0, [Tbg])
        prevT = sb("prevT", [128, 12, 3, 16], F32)
        TprevT = T("prevT")

        mk = sb.mark()
        sdc_sb = sb("sdc_sb", [16, 3, 1536], F32)
        Tsdc = T("sdc")
        DMA(sdc_sb[:, :, :], I["sdc"][:, :, :], writes=[Tsdc], own=Tsdc)
        DMA(O["ndc_s"][:, 0:2, :], sdc_sb[:, 1:3, :], reads=[Tsdc], own=Tsdc)
        for cc in range(12):
            for r in range(3):
                TR(PB[4][:, (r * 16):(r * 16 + 16)], sdc_sb[0:16, r, cc * 128:(cc + 1) * 128], ident_f[0:16, 0:16], [Tsdc], [PT[4]])
            COPY("act", prevT[:, cc, :, :].rearrange("p a b -> p (a b)"), PB[4][:, 0:48], [], [PT[4], TprevT])
        em.barrier()
        sb.release(mk)

        mA1 = sb.mark()
        alloc_wsl()
        pre = [sb("pre%d" % i, [128, 3 + NT], F32) for i in range(2)]
        Tpre = [T("pre%d" % i) for i in range(2)]
        for i in range(2):
            MEMSET("pool", pre[i][:, 0:3], 0.0, [Tpre[i]])
        cv = sb("cv", [128, NT], F32)
        Tcv = T("cv")
        sv_l = [sb("sv%d" % i, [128, NT], F32) for i in range(2)]
        Tsv_l = [T("sv%d" % i) for i in range(2)]
        sq_l = [sb("sq%d" % i, [128, NT], F32) for i in range(2)]
        Tsq_l = [T("sq%d" % i) for i in range(2)]
        rn = [sb("rn%d" % i, [128, 512], F32) for i in range(2)]
        Trn = [T("rn%d" % i) for i in range(2)]
        rnc = [0]

        def conv_proj(cc, slot, j):
            ps_ = cc % 2

            def epi(ps, c0, n, bk):
                COPY("act", pre[ps_][:, 3 + c0:3 + c0 + n], ps, [], [PT[bk], Tpre[ps_]])
            fm_proj(slot, j, 128, hT, ThT, 8, SLABS, epi, banks=(0, 1, 2, 3, 6, 7))

        def conv_C(cc):
            ps_ = cc % 2
            p = pre[ps_]
            TS("dve", cv[:, 0:NPC], p[:, 0:NPC], wcaT[:, cc, 0:1], ALU.mult, reads=[Tpre[ps_], Twca], writes=[Tcv])
            for i in range(1, 4):
                STT(cv[:, 0:NPC], p[:, i:i + NPC], wcaT[:, cc, i:i + 1], cv[:, 0:NPC], ALU.mult, ALU.add, [Tpre[ps_], Twca], [Tcv])
            TS("dve", cv[:, DEC0:NT], p[:, 3 + DEC0:3 + NT], wcaT[:, cc, 3:4], ALU.mult, reads=[Tpre[ps_], Twca], writes=[Tcv])
            for i in range(3):
                STT(cv[:, DEC0:NT], prevT[:, cc, i, :], wcaT[:, cc, i:i + 1], cv[:, DEC0:NT], ALU.mult, ALU.add, [TprevT, Twca], [Tcv])
            COPY("pool", lastpre[:, cc, :], p[:, 3 + NPC - 3:3 + NPC], [Tpre[ps_]], [Tlast])
            COPY("pool", decpre[:, cc, :], p[:, 3 + DEC0:3 + NT], [Tpre[ps_]], [Tdecpre])

        def conv_S(cc):
            g = cc // 4
            if g == 2:
                ACT(qkvT[:, cc, :], cv[:, :], AF.Silu, [Tcv], Tq[g])
                ACT(decq[:, cc, :], cv[:, DEC0:NT], AF.Silu, [Tcv], [Tdecq])
                return
            sv, Tsv, sq, Tsq = sv_l[cc % 2], Tsv_l[cc % 2], sq_l[cc % 2], Tsq_l[cc % 2]
            ACT(sv[:, :], cv[:, :], AF.Silu, [Tcv], [Tsv])
            ACT(sq[:, :].bitcast(F32R) if KR_A1 else sq[:, :], sv[:, :], AF.Square, [Tsv], [Tsq])

        def conv_N(cc):
            g = cc // 4
            if g == 2:
                return
            sv, Tsv, sq, Tsq = sv_l[cc % 2], Tsv_l[cc % 2], sq_l[cc % 2], Tsq_l[cc % 2]
            for (c0, n) in SLABS:
                bk = 4 + (rnc[0] % 2)
                r = rnc[0] % 2
                rnc[0] += 1
                if KR_A1:
                    MM(PB[bk][:, 0:n], ones_r[:, :], sq[:, c0:c0 + n].bitcast(F32R), True, True, [Tsq], [PT[bk]])
                else:
                    MM(PB[bk][:, 0:n], ones_f[:, :], sq[:, c0:c0 + n], True, True, [Tsq], [PT[bk]])
                ACT(rn[r][:, 0:n], PB[bk][:, 0:n], AF.Ln, [], [PT[bk], Trn[r]], bias=eps_col[:, 0:1])
                if g == 0:
                    ACT(rn[r][:, 0:n], rn[r][:, 0:n], AF.Exp, [], [Trn[r]], scale=-0.5, bias=eps_col[:, 1:2])
                else:
                    ACT(rn[r][:, 0:n], rn[r][:, 0:n], AF.Exp, [], [Trn[r]], scale=-0.5)
                TT("dve", qkvT[:, cc, c0:c0 + n], sv[:, c0:c0 + n], rn[r][:, 0:n], ALU.mult, [Tsv, Trn[r]],
                   [Tq[g][ti] for ti in tiles_in(c0, n)])
                if c0 <= DEC0 < c0 + n:
                    TT("dve", decq[:, cc, :], sv[:, DEC0:NT], rn[r][:, DEC0 - c0:NT - c0], ALU.mult, [Tsv, Trn[r]], [Tdecq])

        def job_qkv(g):
            def fn(slot):
                for j in range(4):
                    conv_chunk(g * 4 + j, slot, j)
            return (w_in_blk(g * 512, 512), 8, 512, fn)

        def job_bg(slot):
            for ti in range(18):
                c0, n = TILES[ti]
                for kc in range(8):
                    MM(PB[6][0:n, ti * 8:ti * 8 + 8], hT[:, kc, c0:c0 + n], wsl[slot][:, kc, 0:8], kc == 0, kc == 7,
                       [Tw[slot], ThT[ti]], [PT[6]])
            for ti in range(18):
                c0, n = TILES[ti]
                COPY("dve", bg_all[0:n, ti, :], PB[6][0:n, ti * 8:ti * 8 + 8], [], [PT[6], Tbg])

        qslots = [wload(w_in_blk(g * 512, 512), 8, 512) for g in range(3)]
        conv_proj(0, qslots[0], 0)
        conv_proj(1, qslots[0], 1)
        conv_C(0)
        conv_S(0)
        for cc in range(12):
            if cc + 2 < 12:
                conv_proj(cc + 2, qslots[(cc + 2) // 4], (cc + 2) % 4)
            if cc + 1 < 12:
                conv_C(cc + 1)
            conv_N(cc)
            if cc + 1 < 12:
                conv_S(cc + 1)
        job_bg(wload(w_in_blk(BA, 8), 8, 8))

        mk = sb.mark()
        cvo = sb("cvo", [16, 1536], F32)
        Tcvo = T("cvo")
        cvo2 = sb("cvo2", [3, 1536], F32)
        Tcvo2 = T("cvo2")
        for cc in range(12):
            TR(PB[7][0:16, (cc % 4) * 128:(cc % 4) * 128 + 128], decpre[:, cc, :], ident_f[:, :], [Tdecpre], [PT[7]])
            if cc % 4 == 3:
                COPY("dve", cvo[0:16, (cc - 3) * 128:(cc + 1) * 128], PB[7][0:16, 0:512], [], [PT[7], Tcvo])
        DMA(O["ndc_s"][:, 2, :], cvo[0:16, :], reads=[Tcvo], own=Tcvo)
        for cc in range(12):
            TR(PB[7][0:3, (cc % 4) * 128:(cc % 4) * 128 + 128], lastpre[:, cc, :], ident_f[:, :], [Tlast], [PT[7]])
            if cc % 4 == 3:
                COPY("dve", cvo2[0:3, (cc - 3) * 128:(cc + 1) * 128], PB[7][0:3, 0:512], [], [PT[7], Tcvo2])
        DMA(O["ndc_p"][:, :], cvo2[0:3, :], reads=[Tcvo2], own=Tcvo2)
        em.barrier()
        sb.release(mk)
        sb.release(mA1)
        if stage <= 1:
            em.finish()
            print("sbuf peak", sb.peak, "ops", em.nops, "dsems", len(em.dsems))
            return nc

        mA2 = sb.mark()
        hd = []
        for h in range(4):
            d_ = {}
            CH_BF = os.environ.get("KCHAIN", "bf16") == "bf16"
            for nm in ["gcol", "tm", "Ds", "u", "eGbc"]:
                d_[nm] = sb("%s%d" % (nm, h), [128, 128], F32)
            for nm in ["A0", "A1", "B0", "B1", "P0", "P1"]:
                d_[nm] = sb("%s%d" % (nm, h), [128, 128], BF16 if CH_BF else F32)
            for nm in ["Pbf", "kbg", "kd", "vb", "wT", "atm", "qg", "vn", "on"]:
                d_[nm] = sb("%s%d" % (nm, h), [128, 128], BF16)
            d_["S"] = sb("S%d" % h, [128, 128], F32)
            d_["Sbf"] = sb("Sbf%d" % h, [128, 128], BF16)
            d_["col"] = sb("col%d" % h, [128, 8], F32)
            d_["T"] = {nm: T("%s%d" % (nm, h)) for nm in list(d_.keys())}
            d_["hb"] = [dict(u=d_["u"], wT=d_["wT"], atm=d_["atm"], qg=d_["qg"], kd=d_["kd"], col=d_["col"]),
                        dict(u=sb("u%db" % h, [128, 128], F32), wT=sb("wT%db" % h, [128, 128], BF16), atm=sb("atm%db" % h, [128, 128], BF16),
                             qg=sb("qg%db" % h, [128, 128], BF16), kd=sb("kd%db" % h, [128, 128], BF16), col=sb("col%db" % h, [128, 8], F32))]
            d_["hT"] = [{k: d_["T"][k] for k in d_["hb"][0]}, {k: T("%s%db" % (k, h)) for k in d_["hb"][1]}]
            hd.append(d_)
            MEMSET("pool", d_["S"][:, :], 0.0, [d_["T"]["S"]])
            MEMSET("pool", d_["Sbf"][:, :], 0.0, [d_["T"]["Sbf"]])
        sc = sb("sc", [128, 64], F32)
        Tsc = T("sc")
        ojk = sb("ojk", [128, 128], BF16)
        Tojk = T("ojk")
        sc2 = sb("sc2", [128, 16], F32)
        Tsc2 = T("sc2")
        F32R = mybir.dt.float32r
        tribd_r = sb("tribd_r", [128, 128], F32R)
        Ttr = T("tribd_r")
        COPY("dve", tribd_r[:, :], cn["c_tribd"][:, :], [], [Ttr])
        Sd = [sb("Sd%d" % i, [128, 4, 128], F32) for i in range(2)]
        TSd = [T("Sd%d" % i) for i in range(2)]
        Sn = [sb("Sn%d" % i, [128, 4, 128], F32) for i in range(2)]
        TSn = [T("Sn%d" % i) for i in range(2)]
        dX = sb("dX", [16, 192], F32)
        TdX = T("dX")
        dbc = sb("dbc", [128, 192], F32)
        Tdbc = T("dbc")
        dvb = sb("dvb", [128, 4, 16], F32)
        Tdvb = T("dvb")
        ddiag = sb("ddiag", [128, 128], F32)
        Tddiag = T("ddiag")
        dtmp = sb("dtmp", [128, 128], F32)
        Tdtmp = T("dtmp")
        derr = sb("derr", [128, 4], F32)
        Tderr = T("derr")
        odec = sb("odec", [128, 64], F32)
        Todec = T("odec")

        sca = sb("sca", [128, 18, 40], F32)
        Tsca = T("sca")
        gall = sb("gall", [128, 72], F32)
        Tgall = T("gall")

        def all_scalars():
            V = lambda a_, b_: sca[:, :, a_:b_]
            ACT(V(0, 4), bg_all[:, :, 0:4], AF.Sigmoid, [Tbg], [Tsca])
            TS("dve", V(4, 8), V(0, 4), -1.0, ALU.mult, reads=[], writes=[Tsca])
            for ti in range(18):
                TT("dve", sca[:, ti, 32:36], bg_all[:, ti, 4:8], vec[:, 0:4], ALU.add, [Tbg], [Tsca])
            ACT(V(32, 36), V(32, 36), AF.Exp, [], [Tsca])
            ACT(V(32, 36), V(32, 36), AF.Ln, [], [Tsca], bias=eps_col[:, 2:3])
            for ti in range(18):
                TT("dve", sca[:, ti, 8:12], sca[:, ti, 32:36], vec[:, 4:8], ALU.mult, [], [Tsca])
            COPY("dve", gall[:, :].rearrange("p (a b) -> p a b", a=18), V(8, 12), [Tsca], [Tgall])
            MM(PB[1][:, 0:72], cn["c_tribd"][:, :], gall[:, :], True, True, [Tgall], [PT[1]])
            MM(PB[1][:, 128:200], cn["c_blk"][:, :], gall[:, :], True, True, [Tgall], [PT[1]])
            COPY("dve", V(12, 16), PB[1][:, 0:72].rearrange("p (a b) -> p a b", a=18), [], [PT[1], Tsca])
            COPY("dve", V(16, 20), PB[1][:, 128:200].rearrange("p (a b) -> p a b", a=18), [], [PT[1], Tsca])
            MM(PB[1][0:16, 256:260], cn["c_blk"][0:16, 0:16], gall[0:16, 0:4], True, True, [Tgall], [PT[1]])
            COPY("dve", sca[0:16, 0, 16:20], PB[1][0:16, 256:260], [], [PT[1], Tsca])
            ACT(V(20, 24), V(12, 16), AF.Exp, [], [Tsca])
            TT("dve", V(32, 36), V(16, 20), V(12, 16), ALU.subtract, [], [Tsca])
            ACT(V(24, 28), V(32, 36), AF.Exp, [], [Tsca])
            TT("dve", V(28, 32), V(0, 4), V(20, 24), ALU.mult, [], [Tsca])

        def delta_prep(ti, heads):
            c0, n = TILES[ti]
            pb_ = ti % 2
            chunks = [(0, min(n, 64))] + ([(64, 64)] if n == 128 else [])
            nk = 5 if n == 128 else 3
            KF = os.environ.get("KF32R", "2")
            CH_BF = os.environ.get("KCHAIN", "bf16") == "bf16"
            USE_R = KF in ("1", "2") and not CH_BF
            USE_RG = KF in ("1", "3")
            R = (lambda ap: ap.bitcast(F32R)) if (n == 128 and USE_R) else (lambda ap: ap)
            RO = (lambda ap: ap.bitcast(F32R)) if USE_R else (lambda ap: ap)
            sc = sca[:, ti, :]
            Tsc = Tsca
            cs = slice(c0, c0 + n)
            qT = lambda h: qkvT[:, h, cs]
            kT = lambda h: qkvT[:, 4 + h, cs]
            vT = lambda h: qkvT[:, 8 + h, cs]
            rq = [Tq[0][ti]]
            rk = [Tq[1][ti]]
            rv = [Tq[2][ti]]
            H = heads
            D_ = lambda h: hd[h]
            TD_ = lambda h: hd[h]["T"]
            HB = lambda h: hd[h]["hb"][pb_]
            HT = lambda h: hd[h]["hT"][pb_]
            Wb = lambda h: 2 * h
            Gb = lambda h: 2 * h + 1
            X = lambda h, r, w: PB[Wb(h)][r, 0:w]
            Y = lambda h, r, w: PB[Wb(h)][r, 128:128 + w]
            for h in H:
                TS("pool", D_(h)["gcol"][0:n, :].bitcast(F32R) if KR_GBC else RO(D_(h)["gcol"][0:n, :]), ones_f[0:n, :], sc[0:n, 8 + h:9 + h], ALU.mult,
                   1.0, ALU.mult, reads=[Tsc], writes=[TD_(h)["gcol"]])
            yield
            for h in H:
                if n == 128 and KR_GBC:
                    MM(PB[Gb(h)][:, 0:n], D_(h)["gcol"][0:n, :].bitcast(F32R), tribd_r0[0:n, 0:n], True, True, [TD_(h)["gcol"]], [PT[Gb(h)]])
                else:
                    MM(PB[Gb(h)][:, 0:n], D_(h)["gcol"][0:n, :], cn["c_tribd"][0:n, 0:n], True, True, [TD_(h)["gcol"]], [PT[Gb(h)]])
                MM(X(h, slice(0, n), n), kT(h), kT(h), True, True, rk, [PT[Wb(h)]])
            yield
            for h in H:
                STT(D_(h)["tm"][0:n, 0:n], PB[Gb(h)][0:n, 0:n], sc[0:n, 12 + h:13 + h], cn["c_mbig"][0:n, 0:n], ALU.subtract, ALU.max,
                    [Tsc], [PT[Gb(h)], TD_(h)["tm"]])
            yield
            for h in H:
                ACT(D_(h)["Ds"][0:n, 0:n], D_(h)["tm"][0:n, 0:n], AF.Exp, [TD_(h)["tm"]], [TD_(h)["Ds"]], scale=-1.0)
            yield
            for h in H:
                STT(RO(D_(h)["A0"][0:n, 0:n]), X(h, slice(0, n), n), sc[0:n, 4 + h:5 + h], D_(h)["Ds"][0:n, 0:n], ALU.mult, ALU.mult,
                    [Tsc, TD_(h)["Ds"]], [PT[Wb(h)], TD_(h)["A0"]])
            yield
            if CH_BF:
                YB = lambda h: pbf(Wb(h))[0:n, 256:256 + n]
            else:
                YB = lambda h: Y(h, slice(0, n), n)
            for h in H:
                TR(YB(h), D_(h)["A0"][0:n, 0:n], (ident_b if CH_BF else ident_f)[0:n, 0:n], [TD_(h)["A0"]], [PT[Wb(h)]])
            yield
            for h in H:
                STT(D_(h)["tm"][0:n, 0:n], PB[Gb(h)][0:n, 0:n], sc[0:n, 12 + h:13 + h], cn["c_mneg"][0:n, 0:n], ALU.subtract, ALU.min,
                    [Tsc], [PT[Gb(h)], TD_(h)["tm"]])
            yield
            for h in H:
                COPY("act", RO(D_(h)["B0"][0:n, 0:n]), YB(h), [], [PT[Wb(h)], TD_(h)["B0"]])
            yield
            for h in H:
                TT("dve", RO(D_(h)["P0"][0:n, 0:n]), YB(h), ident_f[0:n, 0:n], ALU.add, [], [PT[Wb(h)], TD_(h)["P0"]])
            yield
            for h in H:
                ACT(D_(h)["Ds"][0:n, 0:n], D_(h)["tm"][0:n, 0:n], AF.Exp, [TD_(h)["tm"]], [TD_(h)["Ds"]])
                ACT(D_(h)["eGbc"][:, 0:n], PB[Gb(h)][:, 0:n], AF.Exp, [], [PT[Gb(h)], TD_(h)["eGbc"]])
                for ci, (r0, m) in enumerate(chunks):
                    ACT(HB(h)["col"][:, ci:ci + 1], PB[Gb(h)][:, r0 + m - 1:r0 + m], AF.Exp, [], [PT[Gb(h)], HT(h)["col"]])
            yield
            for k in range(nk):
                a, b_ = "A%d" % (k % 2), "A%d" % ((k + 1) % 2)
                ba, bb = "B%d" % (k % 2), "B%d" % ((k + 1) % 2)
                pa, pb2 = "P%d" % (k % 2), "P%d" % ((k + 1) % 2)
                for h in H:
                    d_, TD = D_(h), TD_(h)
                    MM(X(h, slice(0, n), n), R(d_[ba][0:n, 0:n]), R(d_[a][0:n, 0:n]), True, True, [TD[ba], TD[a]], [PT[Wb(h)]])
                    if k < nk - 1:
                        MM(PB[Gb(h)][0:n, 256:256 + n], R(d_[a][0:n, 0:n]), R(d_[ba][0:n, 0:n]), True, True, [TD[ba], TD[a]], [PT[Gb(h)]])
                yield
                for h in H:
                    d_, TD = D_(h), TD_(h)
                    COPY("act", RO(d_[b_][0:n, 0:n]), X(h, slice(0, n), n), [], [PT[Wb(h)], TD[b_]])
                yield
                if k < nk - 1:
                    for h in H:
                        d_, TD = D_(h), TD_(h)
                        COPY("act" if h >= 2 else "dve", RO(d_[bb][0:n, 0:n]), PB[Gb(h)][0:n, 256:256 + n], [], [PT[Gb(h)], TD[bb]])
                    yield
                for h in H:
                    d_, TD = D_(h), TD_(h)
                    MM(Y(h, slice(0, n), n), R(d_[b_][0:n, 0:n]), R(d_[pa][0:n, 0:n]), True, True, [TD[b_], TD[pa]], [PT[Wb(h)]])
                yield
                for h in H:
                    d_, TD = D_(h), TD_(h)
                    TT("dve", RO(d_[pb2][0:n, 0:n]), Y(h, slice(0, n), n), d_[pa][0:n, 0:n], ALU.add, [TD[pa]], [PT[Wb(h)], TD[pb2]])
                yield
            pf = "P%d" % (nk % 2)
            if CH_BF:
                PBF = lambda h: D_(h)[pf]
                TPBF = lambda h: TD_(h)[pf]
            else:
                PBF = lambda h: D_(h)["Pbf"]
                TPBF = lambda h: TD_(h)["Pbf"]
                for h in H:
                    COPY("act", D_(h)["Pbf"][0:n, 0:n], D_(h)[pf][0:n, 0:n], [TD_(h)[pf]], [TD_(h)["Pbf"]])
                yield
            for h in H:
                TR(pbf(Wb(h))[0:n, 0:128], kT(h), ident_b[:, :], rk, [PT[Wb(h)]])
                TR(pbf(Wb(h))[0:n, 128:256], vT(h), ident_b[:, :], rv, [PT[Wb(h)]])
            yield
            for h in H:
                ACT(D_(h)["kbg"][0:n, :], pbf(Wb(h))[0:n, 0:128], AF.Copy, [Tsc], [PT[Wb(h)], TD_(h)["kbg"]], scale=sc[0:n, 28 + h:29 + h])
            yield
            for h in H:
                TS("dve", HB(h)["kd"][0:n, :], pbf(Wb(h))[0:n, 0:128], sc[0:n, 24 + h:25 + h], ALU.mult, reads=[Tsc], writes=[PT[Wb(h)], HT(h)["kd"]])
            yield
            for h in H:
                ACT(D_(h)["vb"][0:n, :], pbf(Wb(h))[0:n, 128:256], AF.Copy, [Tsc], [PT[Wb(h)], TD_(h)["vb"]], scale=sc[0:n, 0 + h:1 + h])
            yield
            for h in H:
                TT("dve", HB(h)["qg"][:, 0:n], qT(h), D_(h)["eGbc"][:, 0:n], ALU.mult, rq + [TD_(h)["eGbc"]], [HT(h)["qg"]])
            yield
            for h in H:
                d_, TD = D_(h), TD_(h)
                MM(Y(h, slice(0, n), 128), PBF(h)[0:n, 0:n], d_["vb"][0:n, :], True, True, [TPBF(h), TD["vb"]], [PT[Wb(h)]])
                MM(X(h, slice(0, 128), n), d_["kbg"][0:n, :], PBF(h)[0:n, 0:n], True, True, [TPBF(h), TD["kbg"]], [PT[Wb(h)]])
            yield
            for h in H:
                COPY("act", HB(h)["u"][0:n, :], Y(h, slice(0, n), 128), [], [PT[Wb(h)], HT(h)["u"]])
            yield
            for h in H:
                COPY("dve", HB(h)["wT"][:, 0:n], X(h, slice(0, 128), n), [], [PT[Wb(h)], HT(h)["wT"]])
            yield
            for h in H:
                MM(Y(h, slice(0, n), n), kT(h), qT(h), True, True, rk + rq, [PT[Wb(h)]])
            yield
            for h in H:
                TT("dve", HB(h)["atm"][0:n, 0:n], Y(h, slice(0, n), n), D_(h)["Ds"][0:n, 0:n], ALU.mult, [TD_(h)["Ds"]], [PT[Wb(h)], HT(h)["atm"]])
            yield

        def delta_rec(ti):
            c0, n = TILES[ti]
            pb_ = ti % 2
            cs = slice(c0, c0 + n)
            chunks = [(0, min(n, 64))] + ([(64, 64)] if n == 128 else [])
            H = range(4)
            D_ = lambda h: hd[h]
            TD_ = lambda h: hd[h]["T"]
            HB = lambda h: hd[h]["hb"][pb_]
            HT = lambda h: hd[h]["hT"][pb_]
            Wb = lambda h: 2 * h
            Gb = lambda h: 2 * h + 1
            for ci, (r0, m) in enumerate(chunks):
                rs = slice(r0, r0 + m)
                for h in H:
                    d_, TD = D_(h), TD_(h)
                    MM(PB[Wb(h)][rs, 256:384], HB(h)["wT"][:, rs], d_["Sbf"][:, :], True, True, [HT(h)["wT"], TD["Sbf"]], [PT[Wb(h)]])
                yield
                for h in H:
                    d_, TD = D_(h), TD_(h)
                    TT("dve", d_["vn"][rs, :], HB(h)["u"][rs, :], PB[Wb(h)][rs, 256:384], ALU.subtract, [HT(h)["u"]], [PT[Wb(h)], TD["vn"]])
                yield
                for h in H:
                    d_, TD = D_(h), TD_(h)
                    MM(PB[Wb(h)][:, 256:384], HB(h)["kd"][rs, :], d_["vn"][rs, :], True, True, [HT(h)["kd"], TD["vn"]], [PT[Wb(h)]])
                    MM(PB[Gb(h)][rs, 128:256], HB(h)["qg"][:, rs], d_["Sbf"][:, :], True, False, [HT(h)["qg"], TD["Sbf"]], [PT[Gb(h)]])
                    MM(PB[Gb(h)][rs, 128:256], HB(h)["atm"][rs, rs], d_["vn"][rs, :], False, True, [HT(h)["atm"], TD["vn"]], [PT[Gb(h)]])
                yield
                for h in H:
                    d_, TD = D_(h), TD_(h)
                    STT(d_["S"][:, :], d_["S"][:, :], HB(h)["col"][:, ci:ci + 1], PB[Wb(h)][:, 256:384], ALU.mult, ALU.add,
                        [HT(h)["col"]], [PT[Wb(h)], TD["S"]])
                yield
                for h in H:
                    d_, TD = D_(h), TD_(h)
                    COPY("pool", d_["Sbf"][:, :], d_["S"][:, :], [TD["S"]], [TD["Sbf"]])
                yield
            for h in H:
                ACT(ojk[0:n, :], PB[Gb(h)][0:n, 128:256], AF.Square, [], [PT[Gb(h)], Tojk, Tsc2], accum_out=sc2[0:n, h:h + 1])
            ACT(sc2[0:n, 12:13], eps_col[0:n, 0:1], AF.Copy, [], [Tsc2])
            yield
            ACT(sc2[0:n, 4:8], sc2[0:n, 0:4], AF.Ln, [], [Tsc2], scale=1.0 / 128, bias=eps_col[0:n, 0:1])
            ACT(sc2[0:n, 8:12], sc2[0:n, 4:8], AF.Exp, [], [Tsc2], scale=-0.5)
            yield
            for h in H:
                STT(D_(h)["on"][0:n, :], PB[Gb(h)][0:n, 128:256], sc2[0:n, 8 + h:9 + h], onA_bc[0:n, :], ALU.mult, ALU.mult,
                    [Tsc2], [PT[Gb(h)], TD_(h)["on"]])
            yield
            for h in H:
                TR(pbf(Wb(h))[:, 768:768 + n], D_(h)["on"][0:n, :], ident_b[0:n, 0:n], [TD_(h)["on"]], [PT[Wb(h)]])
            yield
            for h in H:
                COPY("act" if h % 2 else "dve", qkvT[:, h, cs], pbf(Wb(h))[:, 768:768 + n], [], [PT[Wb(h)], Tq[0][ti]])
            yield

        def interleave(gens, weights=None):
            if weights is None:
                weights = [1] * len(gens)
            gens = [(g, w) for g, w in zip(gens, weights) if g is not None]
            while gens:
                nxt = []
                for g, w in gens:
                    alive = True
                    for _ in range(w):
                        try:
                            next(g)
                        except StopIteration:
                            alive = False
                            break
                    if alive:
                        nxt.append((g, w))
                gens = nxt

        def dec_delta_prep():
            ti = 17
            COPY("dve", sc[0:16, 0:12], sca[0:16, 17, 0:12], [Tsca], [Tsc])
            ACT(sc[0:16, 20:24], sc[0:16, 8:12], AF.Exp, [], [Tsc])
            TT("dve", sc[0:16, 32:36], sc[0:16, 20:24], sc[0:16, 4:8], ALU.mult, [], [Tsc])
            for h in range(4):
                TS("dve", dX[0:16, h * 16:(h + 1) * 16], ident_f[0:16, 0:16], sc[0:16, 20 + h:21 + h], ALU.mult, reads=[Tsc, Tc], writes=[TdX])
                TS("dve", dX[0:16, 64 + h * 16:64 + (h + 1) * 16], ident_f[0:16, 0:16], sc[0:16, 32 + h:33 + h], ALU.mult, reads=[Tsc, Tc], writes=[TdX])
                TS("dve", dX[0:16, 128 + h * 16:128 + (h + 1) * 16], ident_f[0:16, 0:16], sc[0:16, 0 + h:1 + h], ALU.mult, reads=[Tsc, Tc], writes=[TdX])
            MM(PB[7][:, 0:192], ones_f[0:16, :], dX[0:16, :], True, True, [TdX, Tc], [PT[7]])
            COPY("dve", dbc[:, :], PB[7][:, 0:192], [], [PT[7], Tdbc])
            TT("dve", dvb[:, :, :], decq[:, 8:12, :], dbc[:, 128:192].rearrange("p (h b) -> p h b", h=4), ALU.mult, [Tdecq, Tdbc], [Tdvb])

        def dec_delta(b):
            s = b % 2
            DMA(Sd[s][:, :, :], I["sd"][b].rearrange("h k v -> k h v"), writes=[TSd[s]], own=TSd[s])
            yield
            for h in range(4):
                W, G_ = 2 * h, 2 * h + 1
                hb = h * 16 + b
                MM(PB[W][:, 448:449], Sd[s][:, h, :], decq[:, 4 + h, b:b + 1], True, True, [TSd[s], Tdecq], [PT[W]])
                STT(derr[:, h:h + 1], PB[W][:, 448:449], dbc[:, 64 + hb:65 + hb], dvb[:, h, b:b + 1], ALU.mult, ALU.add,
                    [Tdbc, Tdvb], [PT[W], Tderr])
                yield
                TS("pool", ddiag[:, :].bitcast(F32R), ident_f[:, :], derr[:, h:h + 1], ALU.mult, 1.0, ALU.mult, reads=[Tderr], writes=[Tddiag])
                yield
                MM(PB[G_][:, 384:512], ones_r[:, :], ddiag[:, :].bitcast(F32R), True, True, [Tddiag], [PT[G_]])
                ACT(dtmp[:, :], PB[G_][:, 384:512], AF.Copy, [Tdecq], [PT[G_], Tdtmp], scale=decq[:, 4 + h, b:b + 1])
                yield
                STT(Sn[s][:, h, :], Sd[s][:, h, :], dbc[:, hb:hb + 1], dtmp[:, :], ALU.mult, ALU.add, [TSd[s], Tdbc, Tdtmp], [TSn[s]])
                yield
                MM(PB[W][:, 449:450], Sn[s][:, h, :], decq[:, 0 + h, b:b + 1], True, True, [TSn[s], Tdecq], [PT[W]])
                COPY("act", odec[:, hb:hb + 1], PB[W][:, 449:450], [], [PT[W], Todec])
                yield
            DMA(O["nd_s"][b].rearrange("h k v -> k h v"), Sn[s][:, :, :], reads=[TSn[s]], own=TSn[s])
            yield

        def dec_delta_finish():
            TT("dve", dtmp[:, 0:64], odec[:, :], odec[:, :], ALU.mult, [Todec], [Tdtmp])
            MM(PB[7][:, 0:64], ones_f[:, :], dtmp[:, 0:64], True, True, [Tdtmp, Tc], [PT[7]])
            ACT(dtmp[:, 64:128], PB[7][:, 0:64], AF.Ln, [], [PT[7], Tdtmp], scale=1.0 / 128, bias=eps_col[:, 0:1])
            ACT(dtmp[:, 64:128], dtmp[:, 64:128], AF.Exp, [], [Tdtmp], scale=-0.5)
            TT("dve", dtmp[:, 0:64], odec[:, :], dtmp[:, 64:128], ALU.mult, [Todec], [Tdtmp])
            TS("dve", qkvT[:, 0:4, DEC0:NT], dtmp[:, 0:64].rearrange("p (h b) -> p h b", h=4), oncol[:, 0:1], ALU.mult,
               reads=[Tdtmp, Tvec], writes=[Tq[0][17]])

        with nc.allow_non_contiguous_dma(reason="state tiles 512B rows"):
            print("A2 sbuf top", sb.top, "free", SB_END - sb.top)
            all_scalars()
            dec_delta_prep()
            interleave([delta_prep(0, (0, 1)), delta_prep(0, (2, 3))])
            for ti in range(17):
                interleave([delta_rec(ti),
                            delta_prep(ti + 1, (0, 1)) if ti + 1 < 17 else None,
                            delta_prep(ti + 1, (2, 3)) if ti + 1 < 17 else None,
                            dec_delta(ti - 1) if ti >= 1 else None], [2, 2, 2, 1])
            dec_delta_finish()
            for h in range(4):
                DMA(O["nd_p"][h], hd[h]["S"][:, :], reads=[hd[h]["T"]["S"]], own=hd[h]["T"]["S"])
        em.barrier()
        sb.release(mA2)
        if stage <= 2:
            em.finish()
            print("sbuf peak", sb.peak, "ops", em.nops, "dsems", len(em.dsems))
            return nc
        sb.release(mA)

        obT = sb("obT", [128, 8, NT], BF16)
        Tqk = [T("qkb%d" % i) for i in range(18)]
        mB = sb.mark()
        vtok = sb("vtok", [128, 18, 1024], BF16)
        Tvt = [T("vtok%d" % i) for i in range(18)]
        lrbT = sb("lrbT", [17, NT], F32)
        Tlrb = T("lrbT")
        MEMSET("pool", lrbT[:, :], 1.0, [Tlrb])
        wgk = sb("wgk", [17, 512], F32)
        Twgk = T("wgk")
        DMA(wgk[0:16, :], I["w_gk2"][:, :], writes=[Twgk], own=Twgk)
        DMA(wgk[16:17, :], I["b_gk"][0:1, :], writes=[Twgk], own=Twgk)
        decqb = sb("decqb", [128, 8, 16], F32)
        Tdecqb = T("decqb")
        mB1 = sb.mark()
        alloc_wsl()

        def job_lrb(slot):
            def epi(ps, c0, n, bk):
                COPY("act", lrbT[0:16, c0:c0 + n], ps, [], [PT[bk], Tlrb])
            fm_proj(slot, 0, 16, hT, ThT, 8, SLABS, epi, banks=(0, 1, 2, 3, 6, 7))

        def job_qkb(g):
            def fn(slot):
                for j in range(4):
                    cc = g * 4 + j

                    def epi(ps, c0, n, bk, cc=cc):
                        if c0 <= DEC0 < c0 + n:
                            psd = ps[:, DEC0 - c0:NT - c0]
                            if g == 0:
                                ACT(decqb[:, cc, :], psd, AF.Copy, [], [PT[bk], Tdecqb], scale=float(128.0 ** -0.5))
                            else:
                                COPY("act", decqb[:, cc, :], psd, [], [PT[bk], Tdecqb])
                        COPY("act" if (cc + c0 // 416) % 2 else "dve", obT[:, cc, c0:c0 + n], ps, [], [PT[bk]] + [Tqk[ti] for ti in tiles_in(c0, n)])
                    fm_proj(slot, j, 128, hT, ThT, 8, SLABS, epi, banks=(0, 1, 2, 3, 6, 7))
            return (w_in_blk(QB + g * 512, 512), 8, 512, fn)

        def job_vb(blk):
            def fn(slot):
                for ti in range(18):
                    c0, n = TILES[ti]
                    bk = 4 + ti % 2
                    for kc in range(8):
                        MM(PB[bk][0:n, 0:512], hT[:, kc, c0:c0 + n], wsl[slot][:, kc, 0:512], kc == 0, kc == 7, [Tw[slot], ThT[ti]], [PT[bk]])
                    COPY("act" if ti % 2 else "dve", vtok[0:n, ti, blk * 512:(blk + 1) * 512], PB[bk][0:n, 0:512], [], [PT[bk], Tvt[ti]])
            return (w_in_blk(VB + blk * 512, 512), 8, 512, fn)

        run_jobs([(w_in_blk(LRB, 16), 8, 16, job_lrb), job_qkb(0), job_qkb(1), job_vb(0), job_vb(1)])
        em.barrier()
        sb.release(mB1)
        if stage <= 3:
            em.finish()
            print("sbuf peak", sb.peak, "ops", em.nops, "dsems", len(em.dsems))
            return nc

        mB2 = sb.mark()
        sel_b = sb("sel_b", [16, 2048], BF16)
        Tsel = T("sel")
        mk = sb.mark()
        sel_f = sb("sel_f", [16, 2048], F32)
        DMA(sel_f[:, :], I["c_sel"][:, :], writes=[Tsel], own=Tsel)
        COPY("dve", sel_b[:, :], sel_f[:, :], [], [Tsel])
        em.barrier()
        sb.release(mk)
        ltok = sb("ltok", [128, 512], F32)
        Tlt = T("ltok")
        eE = sb("eE", [128, 512], F32)
        TeE = T("eE")
        khat_l = [sb("khat%d" % i, [128, 512], BF16) for i in range(2)]
        Tkh_l = [T("khat%d" % i) for i in range(2)]
        gsc = sb("gsc", [128, 16], F32)
        Tgsc = T("gsc")
        ojk2 = sb("ojk2", [128, 256], BF16)
        Tojk2 = T("ojk2")
        gh = []
        for h in range(4):
            d_ = {}
            d_["eB"] = sb("eB%d" % h, [128, 128], F32)
            d_["eNB"] = sb("eNB%d" % h, [128, 128], F32)
            for nm in ["qt", "kt", "atm"]:
                d_[nm] = sb("g%s%d" % (nm, h), [128, 128], BF16)
            d_["S"] = sb("gS%d" % h, [128, 256], F32)
            d_["Sbf"] = sb("gSbf%d" % h, [128, 256], BF16)
            d_["on"] = sb("gon%d" % h, [128, 256], BF16)
            d_["col"] = sb("gcol%d" % h, [128, 4], F32)
            d_["T"] = {nm: T("g%s%d" % (nm, h)) for nm in list(d_.keys())}
            d_["hb"] = [dict(qt=d_["qt"], atm=d_["atm"], col=d_["col"]),
                        dict(qt=sb("gqt%db" % h, [128, 128], BF16), atm=sb("gatm%db" % h, [128, 128], BF16), col=sb("gcol%db" % h, [128, 4], F32))]
            d_["hT"] = [{k: d_["T"][k] for k in d_["hb"][0]}, {k: T("g%s%db" % (k, h)) for k in d_["hb"][1]}]
            gh.append(d_)
            MEMSET("pool", d_["S"][:, :], 0.0, [d_["T"]["S"]])
            MEMSET("pool", d_["Sbf"][:, :], 0.0, [d_["T"]["Sbf"]])
        gdec = sb("gdec", [128, 4, 16], F32)
        Tgdec = T("gdec")
        Sg = [sb("Sg%d" % i, [128, 256], F32) for i in range(4)]
        TSg = [T("Sg%d" % i) for i in range(4)]
        Sgn = [sb("Sgn%d" % i, [128, 256], F32) for i in range(2)]
        TSgn = [T("Sgn%d" % i) for i in range(2)]
        gtmp = sb("gtmp", [128, 256], F32)
        Tgtmp = T("gtmp")
        odecb = sb("odecb", [128, 8, 16], F32)
        Todecb = T("odecb")

        def gla_prep(ti):
            c0, n = TILES[ti]
            cs = slice(c0, c0 + n)
            pb_ = ti % 2
            khat, Tkh = khat_l[pb_], Tkh_l[pb_]
            H = range(4)
            MM(PB[0][0:n, 0:512], lrbT[0:17, cs], wgk[0:17, :], True, True, [Tlrb, Twgk], [PT[0]])
            yield
            LW = (lambda ap: ap.bitcast(F32R)) if KR_GLA else (lambda ap: ap)
            LR = (lambda ap: ap.bitcast(F32R)) if (KR_GLA and n == 128) else (lambda ap: ap)
            ACT(LW(ltok[0:n, :]), PB[0][0:n, 0:512], AF.Exp, [], [PT[0], Tlt], scale=-1.0)
            yield
            ACT(LW(ltok[0:n, :]), ltok[0:n, :], AF.Ln, [], [Tlt], bias=eps_col[0:n, 2:3])
            yield
            MM(PB[0][0:n, 0:512], (trisuf_r if (KR_GLA and n == 128) else cn["c_trisuf"])[0:n, 0:n], LR(ltok[0:n, :]), True, True, [Tlt], [PT[0]])
            for h in H:
                TR(pbf(1)[0:n, h * 128:(h + 1) * 128], obT[:, 4 + h, cs], ident_b[:, :], [Tqk[ti]], [PT[1]])
            yield
            ACT(eE[0:n, :], PB[0][0:n, 0:512], AF.Exp, [], [PT[0], TeE])
            yield
            TT("dve", khat[0:n, :], pbf(1)[0:n, 0:512], eE[0:n, :], ALU.mult, [TeE], [PT[1], Tkh])
            for h in H:
                MM(PB[2 + h][:, 0:n], LR(ltok[0:n, h * 128:(h + 1) * 128]), (tris_r if (KR_GLA and n == 128) else cn["c_tris"])[0:n, 0:n], True, True, [Tlt], [PT[2 + h]])
            yield
            for h in H:
                d_, TD, bk = gh[h], gh[h]["T"], 2 + h
                ACT(d_["eB"][:, 0:n], PB[bk][:, 0:n], AF.Exp, [], [PT[bk], TD["eB"]], bias=eps_col[:, 1:2])
                ACT(d_["eNB"][:, 0:n], PB[bk][:, 0:n], AF.Exp, [], [PT[bk], TD["eNB"]], scale=-1.0)
                ACT(d_["hb"][pb_]["col"][:, 0:1], PB[bk][:, n - 1:n], AF.Exp, [], [PT[bk], d_["hT"][pb_]["col"]])
                yield
            for h in H:
                d_, TD = gh[h], gh[h]["T"]
                TT("dve", d_["hb"][pb_]["qt"][:, 0:n], obT[:, h, cs], d_["eB"][:, 0:n], ALU.mult, [Tqk[ti], TD["eB"]], [d_["hT"][pb_]["qt"]])
                TT("dve", d_["kt"][:, 0:n], obT[:, 4 + h, cs], d_["eNB"][:, 0:n], ALU.mult, [Tqk[ti], TD["eNB"]], [TD["kt"]])
            yield
            for h in H:
                d_, TD, bk = gh[h], gh[h]["T"], 2 + h
                MM(PB[bk][0:n, 128:128 + n], d_["kt"][:, 0:n], d_["hb"][pb_]["qt"][:, 0:n], True, True, [TD["kt"], d_["hT"][pb_]["qt"]], [PT[bk]])
            yield
            for h in H:
                d_, TD, bk = gh[h], gh[h]["T"], 2 + h
                TT("dve", d_["hb"][pb_]["atm"][0:n, 0:n], PB[bk][0:n, 128:128 + n], cn["c_m01"][0:n, 0:n], ALU.mult, [], [PT[bk], d_["hT"][pb_]["atm"]])
            yield

        def gla_rec(ti):
            c0, n = TILES[ti]
            cs = slice(c0, c0 + n)
            pb_ = ti % 2
            khat, Tkh = khat_l[pb_], Tkh_l[pb_]
            H = range(4)
            for h in H:
                d_, TD, bk = gh[h], gh[h]["T"], 2 + h
                hb_, hT_ = d_["hb"][pb_], d_["hT"][pb_]
                sbk = 6 + h % 2
                vt = vtok[0:n, ti, h * 256:(h + 1) * 256]
                MM(PB[sbk][:, (h // 2) * 256:(h // 2) * 256 + 256], khat[0:n, h * 128:(h + 1) * 128], vt, True, True, [Tkh, Tvt[ti]], [PT[sbk]])
                MM(PB[bk][0:n, 256:512], hb_["qt"][:, 0:n], d_["Sbf"][:, :], True, False, [hT_["qt"], TD["Sbf"]], [PT[bk]])
                MM(PB[bk][0:n, 256:512], hb_["atm"][0:n, 0:n], vt, False, True, [hT_["atm"], Tvt[ti]], [PT[bk]])
            yield
            for h in H:
                d_, TD = gh[h], gh[h]["T"]
                sbk = 6 + h % 2
                STT(d_["S"][:, :], d_["S"][:, :], d_["hb"][pb_]["col"][:, 0:1], PB[sbk][:, (h // 2) * 256:(h // 2) * 256 + 256], ALU.mult, ALU.add,
                    [d_["hT"][pb_]["col"]], [PT[sbk], TD["S"]])
            yield
            for h in H:
                d_, TD = gh[h], gh[h]["T"]
                COPY("pool", d_["Sbf"][:, :], d_["S"][:, :], [TD["S"]], [TD["Sbf"]])
            yield
            for h in H:
                bk = 2 + h
                ACT(ojk2[0:n, :], PB[bk][0:n, 256:512], AF.Square, [], [PT[bk], Tojk2, Tgsc], accum_out=gsc[0:n, h:h + 1])
            ACT(gsc[0:n, 12:13], eps_col[0:n, 0:1], AF.Copy, [], [Tgsc])
            yield
            ACT(gsc[0:n, 4:8], gsc[0:n, 0:4], AF.Ln, [], [Tgsc], scale=1.0 / 256, bias=eps_col[0:n, 0:1])
            ACT(gsc[0:n, 8:12], gsc[0:n, 4:8], AF.Exp, [], [Tgsc], scale=-0.5)
            yield
            for h in H:
                d_, TD, bk = gh[h], gh[h]["T"], 2 + h
                STT(d_["on"][0:n, :], PB[bk][0:n, 256:512], gsc[0:n, 8 + h:9 + h], onB_bc[0:n, :], ALU.mult, ALU.mult,
                    [Tgsc], [PT[bk], TD["on"]])
            yield
            for h in H:
                d_, TD = gh[h], gh[h]["T"]
                sbk = 6 + h % 2
                o0 = (h // 2) * 512
                for half in range(2):
                    TR(pbf(sbk)[:, o0 + half * 128:o0 + half * 128 + n], d_["on"][0:n, half * 128:(half + 1) * 128], ident_b[0:n, 0:n],
                       [TD["on"]], [PT[sbk]])
            yield
            for h in H:
                sbk = 6 + h % 2
                o0 = (h // 2) * 512
                COPY("act" if h % 2 else "dve", obT[:, 2 * h:2 * h + 2, cs],
                     pbf(sbk)[:, o0:o0 + 256].rearrange("p (a c) -> p a c", a=2)[:, :, 0:n], [], [PT[sbk], Tqk[ti]])
            yield

        def dec_gla_prep():
            for h in range(4):
                MM(PB[0][:, h * 16:(h + 1) * 16], wgk[0:17, h * 128:(h + 1) * 128], lrbT[0:17, DEC0:NT], True, True, [Tlrb, Twgk], [PT[0]])
            ACT(gdec[:, :, :].rearrange("p a b -> p (a b)"), PB[0][:, 0:64], AF.Exp, [], [PT[0], Tgdec], scale=-1.0)
            ACT(gdec[:, :, :].rearrange("p a b -> p (a b)"), gdec[:, :, :].rearrange("p a b -> p (a b)"), AF.Ln, [], [Tgdec], bias=eps_col[:, 2:3])
            ACT(gdec[:, :, :].rearrange("p a b -> p (a b)"), gdec[:, :, :].rearrange("p a b -> p (a b)"), AF.Exp, [], [Tgdec], scale=-1.0 / 16.0)

        dgc = [0]

        def dec_gla(b):
            for h in range(4):
                DMA(Sg[h][:, :], I["sg"][b, h], writes=[TSg[h]], own=TSg[h])
            yield
            for h in range(4):
                s = dgc[0] % 2
                dgc[0] += 1
                bk = 1
                MM(PB[bk][:, 256:512], sel_b[0:16, b * 128:(b + 1) * 128], vtok[0:16, 17, h * 256:(h + 1) * 256], True, True, [Tsel, Tvt[17]], [PT[bk]])
                ACT(gtmp[:, :], PB[bk][:, 256:512], AF.Copy, [Tdecqb], [PT[bk], Tgtmp], scale=decqb[:, 4 + h, b:b + 1])
                yield
                STT(Sgn[s][:, :], Sg[h][:, :], gdec[:, h, b:b + 1], gtmp[:, :], ALU.mult, ALU.add, [TSg[h], Tgdec, Tgtmp], [TSgn[s]])
                yield
                for half in range(2):
                    MM(PB[bk][:, 256 + half:257 + half], Sgn[s][:, half * 128:(half + 1) * 128], decqb[:, h, b:b + 1], True, True, [TSgn[s], Tdecqb], [PT[bk]])
                COPY("act", odecb[:, 2 * h:2 * h + 2, b], PB[bk][:, 256:258], [], [PT[bk], Todecb])
                DMA(O["ng_s"][b, h], Sgn[s][:, :], reads=[TSgn[s]], own=TSgn[s])
                yield

        def dec_gla_finish():
            of = odecb[:, :, :].rearrange("p a b -> p (a b)")
            TT("dve", gtmp[:, 0:128], of, of, ALU.mult, [Todecb], [Tgtmp])
            for h in range(4):
                for half in range(2):
                    cidx = (2 * h + half) * 16
                    MM(PB[0][:, h * 16:(h + 1) * 16], ones_f[:, :], gtmp[:, cidx:cidx + 16], half == 0, half == 1, [Tgtmp, Tc], [PT[0]])
            ACT(gtmp[:, 128:192], PB[0][:, 0:64], AF.Ln, [], [PT[0], Tgtmp], scale=1.0 / 256, bias=eps_col[:, 0:1])
            ACT(gtmp[:, 128:192], gtmp[:, 128:192], AF.Exp, [], [Tgtmp], scale=-0.5)
            for h in range(4):
                for half in range(2):
                    STT(obT[:, 2 * h + half, DEC0:NT], odecb[:, 2 * h + half, :], oncol[:, 1 + half:2 + half], gtmp[:, 128 + h * 16:128 + (h + 1) * 16],
                        ALU.mult, ALU.mult, [Todecb, Tvec, Tgtmp], [Tqk[17]])

        print("B2 sbuf top", sb.top, "free", SB_END - sb.top)
        dec_gla_prep()
        interleave([gla_prep(0)])
        for ti in range(17):
            interleave([gla_rec(ti), gla_prep(ti + 1) if ti + 1 < 17 else None, dec_gla(ti - 1) if ti >= 1 else None])
        dec_gla_finish()
        for h in range(4):
            DMA(O["ng_p"][h], gh[h]["S"][:, :], reads=[gh[h]["T"]["S"]], own=gh[h]["T"]["S"])
        em.barrier()
        sb.release(mB)
        if stage <= 4:
            em.finish()
            print("sbuf peak", sb.peak, "ops", em.nops, "dsems", len(em.dsems))
            return nc

        mG = sb.mark()
        alloc_wsl()
        gs = [sb("gs%d" % i, [128, 512], BF16) for i in range(4)]
        Tgs = [T("gs%d" % i) for i in range(4)]
        gsc_ = [0]

        def job_gate(c0w, dst, dstT, base):
            def fn(slot):
                for j in range(4):
                    def epi(ps, c0, n, bk, j=j):
                        r = gsc_[0] % 4
                        gsc_[0] += 1
                        ACT(gs[r][:, 0:n], ps, AF.Silu, [], [PT[bk], Tgs[r]])
                        TT("dve", dst[:, base + j, c0:c0 + n], dst[:, base + j, c0:c0 + n], gs[r][:, 0:n], ALU.mult, [Tgs[r]],
                           [dstT[ti] for ti in tiles_in(c0, n)])
                    fm_proj(slot, j, 128, hT, ThT, 8, SLABS, epi, banks=(0, 1, 2, 3, 4, 5, 6, 7))
            return (w_in_blk(c0w, 512), 8, 512, fn)

        run_jobs([job_gate(ZA, oaT, Tq[0], 0), job_gate(RB, obT, Tqk, 0), job_gate(RB + 512, obT, Tqk, 4)])
        em.barrier()
        sb.release(mG)

        mixT_off = sb.top
        mixT = sb("mixT", [128, 8, NT], BF16)
        Tmix = [T("mix%d" % i) for i in range(18)]
        mixT_end = sb.top
        mC1 = sb.mark()
        wa = sb("wa", [128, 4, 1024], BF16)
        wb = sb("wb", [128, 8, 1024], BF16)
        wga = sb("wga", [128, 8, 1024], BF16)
        wgb = sb("wgb", [128, 8, 1024], BF16)
        Twc = [T("wc%d" % i) for i in range(4)]
        em.dma("pool", wa[:, :, :], I["w_a_out"].rearrange("(k p) n -> p k n", p=128), (), [Twc[0]], Twc[0])
        em.dma("pool", wga[:, :, :], w_in_blk(GA, 1024), (), [Twc[2]], Twc[2])
        em.dma("pool", wb[:, :, :], I["w_b_out"].rearrange("(k p) n -> p k n", p=128), (), [Twc[1]], Twc[1])
        em.dma("pool", wgb[:, :, :], w_in_blk(GB, 1024), (), [Twc[3]], Twc[3])
        sg_ = [[sb("sg%d_%d" % (i, k), [128, 512], F32) for k in range(3)] for i in range(2)]
        Tsg = [[T("sg%d_%d" % (i, k)) for k in range(3)] for i in range(2)]
        cct = 0
        for oc in range(8):
            for (c0, n) in SLABS:
                st_i = cct % 2
                cct += 1
                b0 = 4 * st_i
                tl = tiles_in(c0, n)
                ocs = slice(oc * 128, (oc + 1) * 128)
                for kc in range(4):
                    MM(PB[b0][:, 0:n], wa[:, kc, ocs], oaT[:, kc, c0:c0 + n], kc == 0, kc == 3, [Twc[0]] + [Tq[0][ti] for ti in tl], [PT[b0]])
                for kc in range(8):
                    MM(PB[b0 + 1][:, 0:n], wga[:, kc, ocs], hT[:, kc, c0:c0 + n], kc == 0, kc == 7, [Twc[2]] + [ThT[ti] for ti in tl], [PT[b0 + 1]])
                for kc in range(8):
                    MM(PB[b0 + 2][:, 0:n], wb[:, kc, ocs], obT[:, kc, c0:c0 + n], kc == 0, kc == 7, [Twc[1]] + [Tqk[ti] for ti in tl], [PT[b0 + 2]])
                for kc in range(8):
                    MM(PB[b0 + 3][:, 0:n], wgb[:, kc, ocs], hT[:, kc, c0:c0 + n], kc == 0, kc == 7, [Twc[3]] + [ThT[ti] for ti in tl], [PT[b0 + 3]])
                sA, sB, m1 = sg_[st_i]
                TA, TB, TM = Tsg[st_i]
                ACT(sA[:, 0:n], PB[b0 + 1][:, 0:n], AF.Sigmoid, [], [PT[b0 + 1], TA])
                ACT(sB[:, 0:n], PB[b0 + 3][:, 0:n], AF.Sigmoid, [], [PT[b0 + 3], TB])
                TT("dve", m1[:, 0:n], PB[b0][:, 0:n], sA[:, 0:n], ALU.mult, [TA], [PT[b0], TM])
                TT("dve", sB[:, 0:n], PB[b0 + 2][:, 0:n], sB[:, 0:n], ALU.mult, [], [PT[b0 + 2], TB])
                TT("pool", mixT[:, oc, c0:c0 + n], m1[:, 0:n], sB[:, 0:n], ALU.add, [TM, TB], [Tmix[ti] for ti in tl])
        em.barrier()
        sb.release(mC1)
        dump("mixT", mixT[:, :, :], [128, 8, NT], BF16)
        dump("oaT", oaT[:, :, :], [128, 4, NT], BF16)
        dump("obT", obT[:, :, :], [128, 8, NT], BF16)
        if stage <= 5:
            em.finish()
            print("sbuf peak", sb.peak, "ops", em.nops, "dsems", len(em.dsems))
            return nc

        sb.top = hT_off
        x1 = sb("x1", [128, 18, D], F32)
        Tx1 = [T("x1_%d" % i) for i in range(18)]
        x1_end = sb.top
        assert x1_end <= mixT_off
        sb.top = mixT_end
        mC2 = sb.mark()
        wo = sb("wo", [128, 8, 1024], BF16)
        Two = T("wo")
        em.dma("pool", wo[:, :, :], I["w_o"].rearrange("(k p) n -> p k n", p=128), (), [Two], Two)
        for ti in range(18):
            c0, n = TILES[ti]
            DMA(x1[0:n, ti, :], tile_src(ti), writes=[Tx1[ti]], own=Tx1[ti])
        cct = 0
        for ti in range(18):
            c0, n = TILES[ti]
            for half in range(2):
                bk = cct % 8
                cct += 1
                for kc in range(8):
                    MM(PB[bk][0:n, 0:512], mixT[:, kc, c0:c0 + n], wo[:, kc, half * 512:(half + 1) * 512], kc == 0, kc == 7, [Two, Tmix[ti]], [PT[bk]])
                TT("dve", x1[0:n, ti, half * 512:(half + 1) * 512], x1[0:n, ti, half * 512:(half + 1) * 512], PB[bk][0:n, 0:512], ALU.add, [], [PT[bk], Tx1[ti]])
        em.barrier()
        sb.top = x1_end
        dump("x1", x1[:, :, :], [128, 18, D], F32)
        if stage <= 6:
            em.finish()
            print("sbuf peak", sb.peak, "ops", em.nops, "dsems", len(em.dsems))
            return nc

        h2T = sb("h2T", [128, 8, NT], BF16)
        Th2 = [T("h2T%d" % i) for i in range(18)]
        mD0 = sb.mark()
        nf_bc = sb("nf_bc", [128, D], F32)
        Tnf = T("nf")
        DMA(nf_bc[:, :], I["norm_ffn"][0:1, :].partition_broadcast(128), writes=[Tnf], own=Tnf)

        def get_x1(ti):
            c0, n = TILES[ti]
            return x1[0:n, ti, :], [Tx1[ti]]
        norm_tiles(h2T, Th2, get_x1, nf_bc, Tnf, 0)
        em.barrier()
        dump("nf", nf_bc[:, :], [128, D], F32)
        sb.release(mD0)
        dump("h2T", h2T[:, :, :], [128, 8, NT], BF16)

        mD1 = sb.mark()
        NF = 22
        wcfT = sb("wcfT", [128, NF, 4], F32)
        Twcf = T("wcfT")
        prevF = sb("prevF", [128, NF, 2, 16], F32)
        TprevF = T("prevF")
        lastF = sb("lastF", [128, NF, 2], F32)
        TlastF = T("lastF")
        decF = sb("decF", [128, NF, 16], F32)
        TdecF = T("decF")
        mk = sb.mark()
        wcf_sb = sb("wcf_sb", [4, DFF], F32)
        Twcfs = T("wcfs")
        DMA(wcf_sb[0:3, :], I["w_conv_f"][:, :], writes=[Twcfs], own=Twcfs)
        DMA(wcf_sb[3:4, :], I["b_conv_f"][0:1, :], writes=[Twcfs], own=Twcfs)
        sfc_sb = sb("sfc_sb", [16, 2, DFF], F32)
        Tsfc = T("sfc")
        DMA(sfc_sb[:, :, :], I["sfc"][:, :, :], writes=[Tsfc], own=Tsfc)
        DMA(O["nfc_s"][:, 0, :], sfc_sb[:, 1, :], reads=[Tsfc], own=Tsfc)
        for fc in range(NF):
            TR(PB[0][:, fc * 4:fc * 4 + 4], wcf_sb[0:4, fc * 128:(fc + 1) * 128], ident_f[0:4, 0:4], [Twcfs, Tc], [PT[0]])
        COPY("dve", wcfT[:, :, :].rearrange("p a b -> p (a b)"), PB[0][:, 0:NF * 4], [], [PT[0], Twcf])
        for fc in range(NF):
            bk = 1 + fc % 2
            for r in range(2):
                TR(PB[bk][:, r * 16:r * 16 + 16], sfc_sb[0:16, r, fc * 128:(fc + 1) * 128], ident_f[0:16, 0:16], [Tsfc, Tc], [PT[bk]])
            COPY("act", prevF[:, fc, :, :].rearrange("p a b -> p (a b)"), PB[bk][:, 0:32], [], [PT[bk], TprevF])
        em.barrier()
        sb.release(mk)
        mD1b = sb.mark()
        alloc_wsl()
        preF = [sb("preF%d" % i, [128, 2 + NT], F32) for i in range(2)]
        TpreF = [T("preF%d" % i) for i in range(2)]
        for i in range(2):
            MEMSET("pool", preF[i][:, 0:2], 0.0, [TpreF[i]])
        cvf = sb("cvf", [128, NT], F32)
        Tcvf = T("cvf")
        gu_l = [sb("gu%d" % i, [128, NT], BF16) for i in range(2)]
        Tgu_l = [T("gu%d" % i) for i in range(2)]
        GRP = 6
        aT = sb("aT", [128, GRP, NT], BF16)
        TaT = [T("aT%d" % i) for i in range(18)]

        def job_pair(p, g0):
            def fn(slot):
                for jj in range(2):
                    fc = 2 * p + jj
                    ps_ = fc % 2

                    def epi_u(ps, c0, n, bk, ps_=ps_):
                        COPY("act", preF[ps_][:, 2 + c0:2 + c0 + n], ps, [], [PT[bk], TpreF[ps_]])
                    fm_proj(slot, jj, 128, h2T, Th2, 8, SLABS, epi_u, banks=(0, 1, 2, 3, 4, 5, 6, 7))
                for jj in range(2):
                    fc = 2 * p + jj
                    ps_ = fc % 2
                    gu, Tgu = gu_l[ps_], Tgu_l[ps_]
                    pr = preF[ps_]
                    rdp = [TpreF[ps_], Twcf]
                    TS("dve", cvf[:, 0:NPC], pr[:, 0:NPC], wcfT[:, fc, 0:1], ALU.mult, wcfT[:, fc, 3:4], ALU.add, reads=rdp, writes=[Tcvf])
                    for i in range(1, 3):
                        STT(cvf[:, 0:NPC], pr[:, i:i + NPC], wcfT[:, fc, i:i + 1], cvf[:, 0:NPC], ALU.mult, ALU.add, rdp, [Tcvf])
                    TS("dve", cvf[:, DEC0:NT], pr[:, 2 + DEC0:2 + NT], wcfT[:, fc, 2:3], ALU.mult, wcfT[:, fc, 3:4], ALU.add, reads=rdp, writes=[Tcvf])
                    for i in range(2):
                        STT(cvf[:, DEC0:NT], prevF[:, fc, i, :], wcfT[:, fc, i:i + 1], cvf[:, DEC0:NT], ALU.mult, ALU.add, [TprevF, Twcf], [Tcvf])
                    COPY("pool", lastF[:, fc, :], pr[:, 2 + NPC - 2:2 + NPC], [TpreF[ps_]], [TlastF])
                    COPY("pool", decF[:, fc, :], pr[:, 2 + DEC0:2 + NT], [TpreF[ps_]], [TdecF])
                    ACT(gu[:, :], cvf[:, :], AF.Gelu_apprx_tanh, [Tcvf], [Tgu])

                    def epi_g(ps, c0, n, bk, fc=fc, gu=gu, Tgu=Tgu):
                        TT("dve", aT[:, fc - g0, c0:c0 + n], gu[:, c0:c0 + n], ps, ALU.mult, [Tgu], [PT[bk]] + [TaT[ti] for ti in tiles_in(c0, n)])
                    fm_proj(slot, 2 + jj, 128, h2T, Th2, 8, SLABS, epi_g, banks=(0, 1, 2, 3, 4, 5, 6, 7))
            s_u = I["w_ffn_in"][:, 2 * p * 128:2 * p * 128 + 256].rearrange("(k p) n -> p k n", p=128)
            s_g = I["w_ffn_in"][:, DFF + 2 * p * 128:DFF + 2 * p * 128 + 256].rearrange("(k p) n -> p k n", p=128)
            return ((s_u, s_g), 8, 256, fn)

        dct = [0]

        fst = sb("fst", [128, 18, 4], F32)
        Tfst_l = [T("fst%d" % i) for i in range(18)]
        nfin = gu_l[0][:, 0:2 * D].bitcast(F32)
        Tnfin = Tgu_l[0]

        def final_norm(ti):
            c0, n = TILES[ti]
            if ti == 0:
                return
            s_ = ti % 2
            Tfst = Tfst_l[ti]
            xa = x1[0:n, ti, :]
            yb = preF[s_][0:n, 0:D]
            ACT(cvf[0:n, 0:D], xa, AF.Square, [Tx1[ti]], [Tcvf, Tfst], accum_out=fst[0:n, ti, 0:1])
            ACT(fst[0:n, ti, 3:4], eps_col[0:n, 0:1], AF.Copy, [], [Tfst])
            ACT(fst[0:n, ti, 1:2], fst[0:n, ti, 0:1], AF.Ln, [], [Tfst], scale=1.0 / D, bias=eps_col[0:n, 0:1])
            ACT(fst[0:n, ti, 2:3], fst[0:n, ti, 1:2], AF.Exp, [], [Tfst], scale=-0.5)
            STT(yb, xa, fst[0:n, ti, 2:3], nfin[0:n, :], ALU.mult, ALU.mult, [Tx1[ti], Tfst, Tnfin], [TpreF[s_]])
            if ti == 17:
                DMA(O["y_s"][:, :], yb, reads=[TpreF[s_]], own=TpreF[s_])
            else:
                DMA(O["y_p"][128 * (ti - 1):128 * ti, :], yb, reads=[TpreF[s_]], own=TpreF[s_])

        def job_down(g0, gn, half):
            last = (g0 + gn == NF) and half == 1

            def fn(slot):
                if last:
                    DMA(nfin, I["norm_final"][0:1, :].partition_broadcast(128), writes=[Tnfin], own=Tnfin)
                for ti in range(18):
                    c0, n = TILES[ti]
                    bk = dct[0] % 8
                    dct[0] += 1
                    for k in range(gn):
                        MM(PB[bk][0:n, 0:512], aT[:, k, c0:c0 + n], wsl[slot][:, k, 0:512], k == 0, k == gn - 1, [Tw[slot], TaT[ti]], [PT[bk]])
                    TT("dve", x1[0:n, ti, half * 512:(half + 1) * 512], x1[0:n, ti, half * 512:(half + 1) * 512], PB[bk][0:n, 0:512], ALU.add,
                       [], [PT[bk], Tx1[ti]])
                    if last:
                        final_norm(ti)
            src = I["w_ffn_out"][g0 * 128:(g0 + gn) * 128, half * 512:(half + 1) * 512].rearrange("(k p) n -> p k n", p=128)
            return (src, gn, 512, fn)

        print("D1 sbuf top", sb.top, "free", SB_END - sb.top)
        jobs = []
        for g0 in range(0, NF, GRP):
            gn = min(GRP, NF - g0)
            for p in range(g0 // 2, (g0 + gn) // 2):
                jobs.append(job_pair(p, g0))
            jobs.append(job_down(g0, gn, 0))
            jobs.append(job_down(g0, gn, 1))
        run_jobs(jobs)
        em.barrier()
        sb.release(mD1b)

        mk = sb.mark()
        fo = sb("fo", [16, DFF], F32)
        Tfo = T("fo")
        fo2 = sb("fo2", [2, DFF], F32)
        Tfo2 = T("fo2")
        for fc in range(NF):
            bk = 4 + fc % 2
            TR(PB[bk][0:16, 0:128], decF[:, fc, :], ident_f[:, :], [TdecF, Tc], [PT[bk]])
            COPY("dve", fo[0:16, fc * 128:(fc + 1) * 128], PB[bk][0:16, 0:128], [], [PT[bk], Tfo])
            TR(PB[bk][0:2, 128:256], lastF[:, fc, :], ident_f[:, :], [TlastF, Tc], [PT[bk]])
            COPY("act", fo2[0:2, fc * 128:(fc + 1) * 128], PB[bk][0:2, 128:256], [], [PT[bk], Tfo2])
        DMA(O["nfc_s"][:, 1, :], fo[0:16, :], reads=[Tfo], own=Tfo)
        DMA(O["nfc_p"][:, :], fo2[0:2, :], reads=[Tfo2], own=Tfo2)
        em.barrier()
        sb.release(mk)
        sb.release(mD1)

        em.finish()
        print("sbuf peak", sb.peak, "ops", em.nops, "dsems", len(em.dsems))
    return nc


_NC_CACHE = {}


def _core_inputs(inp, c):
    f = lambda a: np.ascontiguousarray(a, dtype=np.float32)
    m = {
        "xp": f(inp["x_prompt"][c]), "meta": f(inp["meta_tokens"]), "xs": f(inp["x_sample"][16 * c:16 * c + 16, 0]),
        "sd": f(inp["state_delta"][0, 16 * c:16 * c + 16]), "sdc": f(inp["state_delta_conv"][0, 16 * c:16 * c + 16]),
        "sg": f(inp["state_gla"][0, 16 * c:16 * c + 16]), "sfc": f(inp["state_ffn_conv"][0, 16 * c:16 * c + 16]),
        "w_in": f(inp["w_in"][0]), "w_conv_a": f(inp["w_conv_a"][0]), "a_log": f(inp["a_log"]), "dt_bias": f(inp["dt_bias"]),
        "w_gk2": f(inp["w_gk2"][0]), "b_gk": f(inp["b_gk"]), "onorm_a": f(inp["onorm_a"]), "onorm_b": f(inp["onorm_b"]),
        "w_a_out": f(inp["w_a_out"][0]), "w_b_out": f(inp["w_b_out"][0]), "w_o": f(inp["w_o"][0]),
        "norm_mix": f(inp["norm_mix"]), "norm_ffn": f(inp["norm_ffn"]), "w_ffn_in": f(inp["w_ffn_in"][0]),
        "w_conv_f": f(inp["w_conv_f"][0]), "b_conv_f": f(inp["b_conv_f"]), "w_ffn_out": f(inp["w_ffn_out"][0]),
        "norm_final": f(inp["norm_final"]).reshape(1, D),
    }
    return m


def kernel(**inp):
    stage = int(os.environ.get("KSTAGE", "99"))
    if stage not in _NC_CACHE:
        _NC_CACHE[stage] = build(stage)
    nc = _NC_CACHE[stage]
    cst = _consts()
    in_maps = []
    for c in range(8):
        m = _core_inputs(inp, c)
        m.update(cst)
        in_maps.append(m)
    res = run_bass_kernel_spmd(nc, in_maps, core_ids=list(range(8)))
    R = res.results
    if os.environ.get("KDBG"):
        for k in R[0]:
            if k.startswith("dbg_"):
                np.save("/tmp/%s.npy" % k, np.stack([np.asarray(R[c][k]).astype(np.float32) for c in range(8)]))
    cat = lambda k: np.stack([R[c][k] for c in range(8)], 0)
    y_p = cat("y_p")
    y_s = np.concatenate([R[c]["y_s"] for c in range(8)], 0)[:, None, :]
    nd_p = cat("nd_p")[None]
    ndc_p = cat("ndc_p")[None]
    ng_p = cat("ng_p")[None]
    nfc_p = cat("nfc_p")[None]
    nd_s = np.concatenate([R[c]["nd_s"] for c in range(8)], 0)[None]
    ndc_s = np.concatenate([R[c]["ndc_s"] for c in range(8)], 0)[None]
    ng_s = np.concatenate([R[c]["ng_s"] for c in range(8)], 0)[None]
    nfc_s = np.concatenate([R[c]["nfc_s"] for c in range(8)], 0)[None]
    return (y_p, y_s, nd_p, ndc_p, ng_p, nfc_p, nd_s, ndc_s, ng_s, nfc_s)
```
